# Optimizing a Trainium2 kernel written in Bass

```python
import jax
import jax.numpy as jnp
from jax import lax
import numpy as np

D_MODEL = 1024
BATCH = 1
SEQ = 16384
DEPTH = 4

N_MIXERS = 4
NORM_EPS = 1e-6
GLA_HEADS = 4
GLA_DK = D_MODEL // 2
GLA_DV = D_MODEL
GLA_RANK = 16
GLA_TAU = 16.0
GLA_CHUNK = 64
GLA_IN = 2 * GLA_DK + 2 * GLA_DV + GLA_RANK
RWKV_HEAD = 64
RWKV_HEADS = D_MODEL // RWKV_HEAD
RWKV_DECAY_RANK = 64
RWKV_A_RANK = 64
RWKV_GATE_RANK = 128
RWKV_GN_EPS = 64e-5
RWKV_SHIFT_MIXES = 6
SB_HEADS = 16
SB_HEAD_DIM = D_MODEL // SB_HEADS
SB_BLOCK = 128
ML_HEADS = 4
ML_DK = D_MODEL // 2
ML_DV = D_MODEL
ML_CHUNK = 64
ML_IN = 2 * ML_DK + 2 * ML_DV + 2 * ML_HEADS
FFN_DIM = 2816
CONV_WIDTH = 3

kernel_name = "hybrid_gla_rwkv7_stickbreak_mlstm_trunk"


def rms_norm(x, g, eps=NORM_EPS):
    xf = x.astype(jnp.float32)
    y = xf * lax.rsqrt(jnp.mean(xf * xf, axis=-1, keepdims=True) + eps)
    return (y * g.astype(jnp.float32)).astype(x.dtype)


def _to_chunks(t, chunk):
    b, s, h, d = t.shape
    return t.reshape(b, s // chunk, chunk, h, d).transpose(1, 0, 3, 2, 4)


def _from_chunks(t):
    n, b, h, c, d = t.shape
    return t.transpose(1, 0, 3, 2, 4).reshape(b, n * c, h, d)


def _gates_to_chunks(t, chunk):
    b, s, h = t.shape
    return t.reshape(b, s // chunk, chunk, h).transpose(1, 0, 3, 2)


def gla_mixer(x, w_in, w_alpha_up, b_alpha, out_norm, w_out):
    f32 = jnp.float32
    b, s, _ = x.shape
    h, dk, dv, c = GLA_HEADS, GLA_DK // GLA_HEADS, GLA_DV // GLA_HEADS, GLA_CHUNK
    proj = x @ w_in
    q, k, v, r, a_low = jnp.split(
        proj, [GLA_DK, 2 * GLA_DK, 2 * GLA_DK + GLA_DV, 2 * GLA_DK + 2 * GLA_DV], axis=-1)
    log_alpha = jax.nn.log_sigmoid((a_low @ w_alpha_up + b_alpha).astype(f32)) / GLA_TAU
    q = q.astype(f32).reshape(b, s, h, dk) * dk ** -0.5
    k = k.astype(f32).reshape(b, s, h, dk)
    v = v.astype(f32).reshape(b, s, h, dv)
    qc, kc, vc = _to_chunks(q, c), _to_chunks(k, c), _to_chunks(v, c)
    cum = jnp.cumsum(_to_chunks(log_alpha.reshape(b, s, h, dk), c), axis=3)
    causal = jnp.tril(jnp.ones((c, c), dtype=bool))

    def step(state, inp):
        qi, ki, vi, bi = inp
        q_dec = qi * jnp.exp(bi)
        k_inv = ki * jnp.exp(-bi)
        scores = jnp.where(causal, jnp.einsum('bhtd,bhsd->bhts', q_dec, k_inv), 0.0)
        o = (jnp.einsum('bhts,bhsv->bhtv', scores, vi)
             + jnp.einsum('bhtd,bhdv->bhtv', q_dec, state))
        b_last = bi[:, :, -1:, :]
        k_end = ki * jnp.exp(b_last - bi)
        state = (state * jnp.exp(b_last)[:, :, 0, :, None]
                 + jnp.einsum('bhsd,bhsv->bhdv', k_end, vi))
        return state, o

    state0 = jnp.zeros((b, h, dk, dv), f32)
    _, o = lax.scan(step, state0, (qc, kc, vc, cum))
    o = _from_chunks(o)
    o = rms_norm(o, out_norm.reshape(h, dv)).reshape(b, s, GLA_DV)
    return (o.astype(x.dtype) * jax.nn.silu(r)) @ w_out


def rwkv7_mixer(x, mu, w_rkv, w0, w1, w2, a0, a1, a2, g1, g2, k_k, k_a, r_k, gn_g, gn_b, w_out):
    f32 = jnp.float32
    b, s, d = x.shape
    h, n = RWKV_HEADS, RWKV_HEAD
    x_prev = jnp.pad(x, ((0, 0), (1, 0), (0, 0)))[:, :-1]
    xx = x_prev - x
    xr, xw, xk, xv, xa, xg = (x + xx * mu[j] for j in range(RWKV_SHIFT_MIXES))
    r = xr @ w_rkv[0]
    k = xk @ w_rkv[1]
    v = xv @ w_rkv[2]
    w_log = -jax.nn.softplus(-(w0 + jnp.tanh(xw @ w1) @ w2).astype(f32)) - 0.5
    decay = jnp.exp(-jnp.exp(w_log))
    a = jax.nn.sigmoid((a0 + (xa @ a1) @ a2).astype(f32))
    g = jax.nn.sigmoid(xg @ g1) @ g2
    kk = (k * k_k).astype(f32).reshape(b, s, h, n)
    kk = kk / jnp.maximum(jnp.sqrt(jnp.sum(kk * kk, axis=-1, keepdims=True)), 1e-12)
    k = k.astype(f32) * (1.0 + (a - 1.0) * k_a.astype(f32))
    r = r.astype(f32).reshape(b, s, h, n)
    k = k.reshape(b, s, h, n)
    v = v.astype(f32).reshape(b, s, h, n)
    a = a.reshape(b, s, h, n)
    decay = decay.reshape(b, s, h, n)

    def step(state, inp):
        r_t, w_t, k_t, v_t, kk_t, a_t = inp
        sa = jnp.einsum('bhvk,bhk->bhv', state, -kk_t)
        state = (state * w_t[:, :, None, :]
                 + sa[..., None] * (kk_t * a_t)[:, :, None, :]
                 + v_t[..., None] * k_t[:, :, None, :])
        return state, jnp.einsum('bhvk,bhk->bhv', state, r_t)

    tm = lambda t: jnp.moveaxis(t, 1, 0)
    state0 = jnp.zeros((b, h, n, n), f32)
    _, y = lax.scan(step, state0, (tm(r), tm(decay), tm(k), tm(v), tm(kk), tm(a)))
    y = jnp.moveaxis(y, 0, 1)
    mean = jnp.mean(y, axis=-1, keepdims=True)
    var = jnp.mean(jnp.square(y - mean), axis=-1, keepdims=True)
    yn = ((y - mean) * lax.rsqrt(var + RWKV_GN_EPS)).reshape(b, s, d) * gn_g.astype(f32) + gn_b.astype(f32)
    bonus = jnp.sum(r * k * r_k.astype(f32).reshape(h, n), axis=-1, keepdims=True) * v
    out = (yn + bonus.reshape(b, s, d)).astype(x.dtype) * g
    return out @ w_out


def stick_breaking_mixer(x, w_qkv, w_out):
    f32 = jnp.float32
    b, s, d = x.shape
    h, dh, qb_len = SB_HEADS, SB_HEAD_DIM, SB_BLOCK
    q, k, v = jnp.split(x @ w_qkv, 3, axis=-1)
    q = q.reshape(b, s, h, dh) * dh ** -0.5
    k = k.reshape(b, s, h, dh)
    v = v.reshape(b, s, h, dh)
    n_blocks = s // qb_len
    q_blocks = q.reshape(b, n_blocks, qb_len, h, dh).transpose(1, 0, 3, 2, 4)
    starts = jnp.arange(n_blocks, dtype=jnp.int32) * qb_len
    key_pos = jnp.arange(s, dtype=jnp.int32)

    def block(args):
        q_blk, start = args
        z = jnp.einsum('bhqd,bkhd->bhqk', q_blk, k).astype(f32)
        q_pos = start + jnp.arange(qb_len, dtype=jnp.int32)
        before = key_pos[None, :] < q_pos[:, None]
        log_keep = jnp.where(before, jax.nn.log_sigmoid(-z), 0.0)
        suffix = lax.cumsum(log_keep, axis=3, reverse=True)
        log_w = jax.nn.log_sigmoid(z) + suffix - log_keep
        w = jnp.where(before, jnp.exp(log_w), 0.0)
        return jnp.einsum('bhqk,bkhd->bhqd', w.astype(v.dtype), v)

    o = lax.map(block, (q_blocks, starts))
    o = o.transpose(1, 0, 3, 2, 4).reshape(b, s, d)
    return o @ w_out


def mlstm_mixer(x, w_in, b_if, out_norm, w_out):
    f32 = jnp.float32
    b, s, _ = x.shape
    h, dk, dv, c = ML_HEADS, ML_DK // ML_HEADS, ML_DV // ML_HEADS, ML_CHUNK
    proj = x @ w_in
    q, k, v, o_pre, if_pre = jnp.split(
        proj, [ML_DK, 2 * ML_DK, 2 * ML_DK + ML_DV, 2 * ML_DK + 2 * ML_DV], axis=-1)
    if_pre = (if_pre + b_if).astype(f32)
    i_pre = if_pre[..., :h]
    log_f = jax.nn.log_sigmoid(if_pre[..., h:])
    qc = _to_chunks(q.astype(f32).reshape(b, s, h, dk), c)
    kc = _to_chunks(k.astype(f32).reshape(b, s, h, dk) * dk ** -0.5, c)
    vc = _to_chunks(v.astype(f32).reshape(b, s, h, dv), c)
    ic = _gates_to_chunks(i_pre, c)
    cum_f = jnp.cumsum(_gates_to_chunks(log_f, c), axis=-1)
    causal = jnp.tril(jnp.ones((c, c), dtype=bool))

    def step(carry, inp):
        c_st, n_st, m_prev = carry
        qi, ki, vi, ii, bi = inp
        d_log = jnp.where(causal, bi[..., :, None] - bi[..., None, :] + ii[..., None, :], -jnp.inf)
        inter = bi + m_prev[..., None]
        m_t = jnp.maximum(inter, jnp.max(d_log, axis=-1))
        w_intra = jnp.exp(d_log - m_t[..., None])
        scale = jnp.exp(inter - m_t)
        qk = jnp.einsum('bhtd,bhsd->bhts', qi, ki) * w_intra
        num = (jnp.einsum('bhts,bhsv->bhtv', qk, vi)
               + scale[..., None] * jnp.einsum('bhtd,bhdv->bhtv', qi, c_st))
        den = jnp.sum(qk, axis=-1) + scale * jnp.einsum('bhtd,bhd->bht', qi, n_st)
        h_out = num / jnp.maximum(jnp.abs(den), jnp.exp(-m_t))[..., None]
        b_last = bi[..., -1]
        g_end = b_last[..., None] - bi + ii
        m_new = jnp.maximum(b_last + m_prev, jnp.max(g_end, axis=-1))
        w_end = jnp.exp(g_end - m_new[..., None])
        carry_scale = jnp.exp(b_last + m_prev - m_new)
        c_st = carry_scale[..., None, None] * c_st + jnp.einsum('bhs,bhsd,bhsv->bhdv', w_end, ki, vi)
        n_st = carry_scale[..., None] * n_st + jnp.einsum('bhs,bhsd->bhd', w_end, ki)
        return (c_st, n_st, m_new), h_out

    carry0 = (jnp.zeros((b, h, dk, dv), f32), jnp.zeros((b, h, dk), f32), jnp.zeros((b, h), f32))
    _, hs = lax.scan(step, carry0, (qc, kc, vc, ic, cum_f))
    hs = _from_chunks(hs)
    hs = rms_norm(hs, out_norm.reshape(h, dv)).reshape(b, s, ML_DV)
    return (hs.astype(x.dtype) * jax.nn.sigmoid(o_pre)) @ w_out


def conv_ffn(x, w_up, conv_w, conv_b, w_down):
    s = x.shape[1]
    u = x @ w_up
    u_pad = jnp.pad(u, ((0, 0), (CONV_WIDTH - 1, 0), (0, 0)))
    u = sum(u_pad[:, j:j + s] * conv_w[j] for j in range(CONV_WIDTH)) + conv_b
    gate, up = jnp.split(u, 2, axis=-1)
    return (jax.nn.silu(gate) * up) @ w_down


def setup_inputs(seed: int = 0) -> dict:
    key = jax.random.key(seed)
    ks = iter(jax.random.split(key, 96))
    f32 = jnp.float32
    D, F2 = D_MODEL, 2 * FFN_DIM

    def nrm(shape, scale):
        return scale * jax.random.normal(next(ks), shape, f32)

    def dense(fan_in, shape):
        return nrm(shape, fan_in ** -0.5)

    def gain(n):
        return 1.0 + nrm((n,), 0.02)

    def unif(shape, lo, hi):
        return jax.random.uniform(next(ks), shape, f32, lo, hi)

    p = {}

    def ffn_params(i):
        p[f"l{i}_norm2"] = gain(D)
        p[f"l{i}_ffn_w_up"] = dense(D, (D, F2))
        p[f"l{i}_ffn_conv_w"] = dense(CONV_WIDTH, (CONV_WIDTH, F2))
        p[f"l{i}_ffn_conv_b"] = nrm((F2,), 0.02)
        p[f"l{i}_ffn_w_down"] = dense(FFN_DIM, (FFN_DIM, D))

    p["x"] = nrm((BATCH, SEQ, D), 1.0)
    p["l0_norm1"] = gain(D)
    p["l0_gla_w_in"] = dense(D, (D, GLA_IN))
    p["l0_gla_w_alpha_up"] = dense(GLA_RANK, (GLA_RANK, GLA_DK))
    p["l0_gla_b_alpha"] = nrm((GLA_DK,), 0.1)
    p["l0_gla_out_norm"] = gain(GLA_DV)
    p["l0_gla_w_out"] = dense(GLA_DV, (GLA_DV, D))
    ffn_params(0)
    p["l1_norm1"] = gain(D)
    p["l1_rwkv_mu"] = unif((RWKV_SHIFT_MIXES, D), 0.0, 1.0)
    p["l1_rwkv_w_rkv"] = dense(D, (3, D, D))
    p["l1_rwkv_w0"] = unif((D,), -6.0, -1.0)
    p["l1_rwkv_w1"] = dense(D, (D, RWKV_DECAY_RANK))
    p["l1_rwkv_w2"] = nrm((RWKV_DECAY_RANK, D), 0.1 * RWKV_DECAY_RANK ** -0.5)
    p["l1_rwkv_a0"] = nrm((D,), 0.1)
    p["l1_rwkv_a1"] = dense(D, (D, RWKV_A_RANK))
    p["l1_rwkv_a2"] = dense(RWKV_A_RANK, (RWKV_A_RANK, D))
    p["l1_rwkv_g1"] = dense(D, (D, RWKV_GATE_RANK))
    p["l1_rwkv_g2"] = dense(RWKV_GATE_RANK, (RWKV_GATE_RANK, D))
    p["l1_rwkv_k_k"] = 0.85 + nrm((D,), 0.05)
    p["l1_rwkv_k_a"] = 1.0 + nrm((D,), 0.05)
    p["l1_rwkv_r_k"] = nrm((D,), 0.1)
    p["l1_rwkv_gn_g"] = gain(D)
    p["l1_rwkv_gn_b"] = nrm((D,), 0.02)
    p["l1_rwkv_w_out"] = dense(D, (D, D))
    ffn_params(1)
    p["l2_norm1"] = gain(D)
    p["l2_sb_w_qkv"] = dense(D, (D, 3 * D))
    p["l2_sb_w_out"] = dense(D, (D, D))
    ffn_params(2)
    p["l3_norm1"] = gain(D)
    p["l3_ml_w_in"] = dense(D, (D, ML_IN))
    p["l3_ml_b_if"] = jnp.concatenate([nrm((ML_HEADS,), 0.1), unif((ML_HEADS,), 3.0, 6.0)])
    p["l3_ml_out_norm"] = gain(ML_DV)
    p["l3_ml_w_out"] = dense(ML_DV, (ML_DV, D))
    ffn_params(3)
    p["final_norm"] = gain(D)
    return p


def reference(x,
              l0_norm1, l0_gla_w_in, l0_gla_w_alpha_up, l0_gla_b_alpha, l0_gla_out_norm, l0_gla_w_out,
              l0_norm2, l0_ffn_w_up, l0_ffn_conv_w, l0_ffn_conv_b, l0_ffn_w_down,
              l1_norm1, l1_rwkv_mu, l1_rwkv_w_rkv, l1_rwkv_w0, l1_rwkv_w1, l1_rwkv_w2,
              l1_rwkv_a0, l1_rwkv_a1, l1_rwkv_a2, l1_rwkv_g1, l1_rwkv_g2,
              l1_rwkv_k_k, l1_rwkv_k_a, l1_rwkv_r_k, l1_rwkv_gn_g, l1_rwkv_gn_b, l1_rwkv_w_out,
              l1_norm2, l1_ffn_w_up, l1_ffn_conv_w, l1_ffn_conv_b, l1_ffn_w_down,
              l2_norm1, l2_sb_w_qkv, l2_sb_w_out,
              l2_norm2, l2_ffn_w_up, l2_ffn_conv_w, l2_ffn_conv_b, l2_ffn_w_down,
              l3_norm1, l3_ml_w_in, l3_ml_b_if, l3_ml_out_norm, l3_ml_w_out,
              l3_norm2, l3_ffn_w_up, l3_ffn_conv_w, l3_ffn_conv_b, l3_ffn_w_down,
              final_norm):
    mixers = (
        lambda h: gla_mixer(h, l0_gla_w_in, l0_gla_w_alpha_up, l0_gla_b_alpha, l0_gla_out_norm, l0_gla_w_out),
        lambda h: rwkv7_mixer(h, l1_rwkv_mu, l1_rwkv_w_rkv, l1_rwkv_w0, l1_rwkv_w1, l1_rwkv_w2,
                              l1_rwkv_a0, l1_rwkv_a1, l1_rwkv_a2, l1_rwkv_g1, l1_rwkv_g2,
                              l1_rwkv_k_k, l1_rwkv_k_a, l1_rwkv_r_k, l1_rwkv_gn_g, l1_rwkv_gn_b,
                              l1_rwkv_w_out),
        lambda h: stick_breaking_mixer(h, l2_sb_w_qkv, l2_sb_w_out),
        lambda h: mlstm_mixer(h, l3_ml_w_in, l3_ml_b_if, l3_ml_out_norm, l3_ml_w_out),
    )
    norm1 = (l0_norm1, l1_norm1, l2_norm1, l3_norm1)
    norm2 = (l0_norm2, l1_norm2, l2_norm2, l3_norm2)
    ffns = (
        (l0_ffn_w_up, l0_ffn_conv_w, l0_ffn_conv_b, l0_ffn_w_down),
        (l1_ffn_w_up, l1_ffn_conv_w, l1_ffn_conv_b, l1_ffn_w_down),
        (l2_ffn_w_up, l2_ffn_conv_w, l2_ffn_conv_b, l2_ffn_w_down),
        (l3_ffn_w_up, l3_ffn_conv_w, l3_ffn_conv_b, l3_ffn_w_down),
    )
    for layer in range(DEPTH):
        x = x + mixers[layer % N_MIXERS](rms_norm(x, norm1[layer]))
        x = x + conv_ffn(rms_norm(x, norm2[layer]), *ffns[layer])
    return rms_norm(x, final_norm)
```

```python
import numpy as np
from contextlib import ExitStack
import concourse.bass as bass
import concourse.mybir as mybir
from concourse.bass_utils import run_bass_kernel_spmd

F32 = mybir.dt.float32
BF16 = mybir.dt.bfloat16
AF = mybir.ActivationFunctionType
ALU = mybir.AluOpType
AX = mybir.AxisListType

ENGS = ['pe', 'act', 'dve', 'pool', 'sp']
NDS = 12


class View:
    __slots__ = ('ap', 'key')

    def __init__(self, ap, key):
        self.ap = ap
        self.key = key

    def __getitem__(self, idx):
        return View(self.ap[idx], self.key)

    def re(self, pat, **kw):
        return View(self.ap.rearrange(pat, **kw), self.key)

    def k(self, key):
        return View(self.ap, key)

    def bc(self, axis, shape):
        return View(self.ap.unsqueeze(axis).to_broadcast(list(shape)), self.key)


class Tile:
    def __init__(self, handle, key, is_dram=False):
        self.h = handle
        self.key = key
        self.is_dram = is_dram

    def __getitem__(self, idx):
        return View(self.h[idx], self.key)

    def sub(self, k):
        return Tile(self.h, (self.key, k), self.is_dram)


class Prog:
    def __init__(self, nc, es):
        self.nc = nc
        self.es = es
        self.ops = {e: [] for e in ENGS}
        self.cnt = {e: 0 for e in ENGS}
        self.know = {e: {} for e in ENGS}
        self.esem = {e: es.enter_context(nc.semaphore(f"s_{e}")) for e in ENGS}
        self.dsem = {q: [es.enter_context(nc.semaphore(f"d_{q}{i}")) for i in range(NDS)]
                     for q in ('sp', 'pool', 'act')}
        self.dcnt = {q: 0 for q in ('sp', 'pool', 'act')}
        self.dtok = {q: [None] * NDS for q in ('sp', 'pool', 'act')}
        self.semobj = {}
        for e in ENGS:
            self.semobj[f"s_{e}"] = self.esem[e]
        for q in self.dsem:
            for i, s in enumerate(self.dsem[q]):
                self.semobj[f"d_{q}{i}"] = s
        self.last_w = {}
        self.readers = {}
        self.nt = 0
        self.n_wait = 0

    def sb(self, shape, dt=F32, name=None):
        self.nt += 1
        name = name or f"t{self.nt}"
        h = self.es.enter_context(self.nc.sbuf_tensor(name, list(shape), dt))
        return Tile(h, name)

    def ps(self, shape, dt=F32, name=None):
        self.nt += 1
        name = name or f"p{self.nt}"
        h = self.es.enter_context(self.nc.psum_tensor(name, list(shape), dt))
        return Tile(h, name)

    def dram(self, name, shape, dt, kind):
        h = self.nc.dram_tensor(name, list(shape), dt, kind=kind).ap()
        return Tile(h, name, True)

    def op(self, eng, emit, reads=(), writes=(), dma=False):
        deps = []
        rkeys = [v.key if isinstance(v, View) else v for v in reads]
        wkeys = [v.key if isinstance(v, View) else v for v in writes]
        for k in rkeys:
            for t in self.last_w.get(k, {}).values():
                deps.append((t, 'raw'))
        for k in wkeys:
            for t in self.last_w.get(k, {}).values():
                deps.append((t, 'waw'))
            for t in self.readers.get(k, {}).values():
                deps.append((t, 'war'))
        if dma:
            q = eng
            j = self.dcnt[q]
            slot = j % NDS
            prev = self.dtok[q][slot]
            if prev is not None:
                deps.append((prev, 'raw'))
            semname = f"d_{q}{slot}"
            val = 16 * (j // NDS + 1)
            self.dcnt[q] += 1
            inc = 16
        else:
            self.cnt[eng] += 1
            semname = f"s_{eng}"
            val = self.cnt[eng]
            inc = 1
        know = self.know[eng]
        waits = {}
        for (tok, kind) in deps:
            tsem, tval, tclk, teng, tdma = tok
            if kind != 'raw' and teng == eng and not tdma and not dma:
                continue
            if know.get(tsem, 0) >= tval:
                continue
            if waits.get(tsem, 0) < tval:
                waits[tsem] = tval
            for s, v in tclk.items():
                if know.get(s, 0) < v:
                    know[s] = v
            if know.get(tsem, 0) < tval:
                know[tsem] = tval
        clk = dict(know)
        tok = (semname, val, clk, eng, dma)
        if dma:
            self.dtok[eng][slot] = tok
        for k in wkeys:
            self.last_w.setdefault(k, {})[semname] = tok
            self.readers[k] = {}
        for k in rkeys:
            self.readers.setdefault(k, {})[semname] = tok
        self.n_wait += len(waits)
        self.ops[eng].append((list(waits.items()), emit, semname, inc))
        return tok

    def mm(self, out, lhsT, rhs, start=True, stop=True, extra_reads=()):
        w = [out]
        r = [lhsT, rhs] + list(extra_reads)
        return self.op('pe', lambda e: e.matmul(out.ap, lhsT.ap, rhs.ap, start=start, stop=stop),
                       reads=r, writes=w)

    def transpose(self, out, in_, ident):
        return self.op('pe', lambda e: e.transpose(out.ap, in_.ap, ident.ap),
                       reads=[in_, ident], writes=[out])

    def act(self, out, in_, func, bias=None, scale=None, accum=None, eng='act'):
        reads = [in_]
        kw = {}
        if bias is not None:
            if isinstance(bias, View):
                reads.append(bias)
                kw['bias'] = bias.ap
            else:
                kw['bias'] = bias
        if scale is not None:
            if isinstance(scale, View):
                reads.append(scale)
                kw['scale'] = scale.ap
            else:
                kw['scale'] = scale
        writes = [out]
        if accum is not None:
            kw['accum_out'] = accum.ap
            writes.append(accum)
        return self.op(eng, lambda e: e.activation(out.ap, in_.ap, func, **kw), reads=reads, writes=writes)

    def tt(self, out, in0, in1, op, eng='dve'):
        return self.op(eng, lambda e: e.tensor_tensor(out.ap, in0.ap, in1.ap, op),
                       reads=[in0, in1], writes=[out])

    def ts(self, out, in0, s1, op0, s2=None, op1=None, accum=None, eng='dve'):
        reads = [in0]
        a1 = s1
        a2 = s2
        if isinstance(s1, View):
            reads.append(s1)
            a1 = s1.ap
        if isinstance(s2, View):
            reads.append(s2)
            a2 = s2.ap
        writes = [out]
        kw = {}
        if op1 is not None:
            kw['op1'] = op1
        if accum is not None:
            kw['accum_out'] = accum.ap
            writes.append(accum)
        return self.op(eng, lambda e: e.tensor_scalar(out.ap, in0.ap, a1, a2, op0, **kw),
                       reads=reads, writes=writes)

    def stt(self, out, in0, scalar, in1, op0, op1, eng='dve'):
        reads = [in0, in1]
        sc = scalar
        if isinstance(scalar, View):
            reads.append(scalar)
            sc = scalar.ap
        return self.op(eng, lambda e: e.scalar_tensor_tensor(out.ap, in0.ap, sc, in1.ap, op0, op1),
                       reads=reads, writes=[out])

    def copy(self, out, in_, eng='dve'):
        if eng == 'act':
            return self.op(eng, lambda e: e.copy(out.ap, in_.ap), reads=[in_], writes=[out])
        return self.op(eng, lambda e: e.tensor_copy(out.ap, in_.ap), reads=[in_], writes=[out])

    def memset(self, out, val, eng='dve'):
        return self.op(eng, lambda e: e.memset(out.ap, val), reads=[], writes=[out])

    def reduce(self, out, in_, op, axis=None, eng='dve'):
        axis = axis or AX.X
        return self.op(eng, lambda e: e.tensor_reduce(out.ap, in_.ap, axis, op), reads=[in_], writes=[out])

    def recip(self, out, in_):
        return self.op('dve', lambda e: e.reciprocal(out.ap, in_.ap), reads=[in_], writes=[out])

    def dma(self, out, in_, q='sp', **kw):
        return self.op(q, lambda e: e.dma_start(out=out.ap, in_=in_.ap, **kw), reads=[in_], writes=[out], dma=True)

    def emit(self, final_keys):
        self.op('sp', None, reads=list(final_keys), writes=[])
        nc = self.nc
        with nc.Block() as block:
            def run(engname):
                def f(eng):
                    for waits, emit, semname, inc in self.ops[engname]:
                        for s, v in waits:
                            eng.wait_ge(self.semobj[s], v)
                        if emit is None:
                            continue
                        ins = emit(eng)
                        ins.then_inc(self.semobj[semname], inc)
                return f
            if self.ops['sp']:
                block.sync(run('sp'))
            if self.ops['pe']:
                block.tensor(run('pe'))
            if self.ops['act']:
                block.scalar(run('act'))
            if self.ops['dve']:
                block.vector(run('dve'))
            if self.ops['pool']:
                block.gpsimd(run('pool'))


EPS = 1e-6
NF = 2816
NCH = 44


class Ctx:
    pass


def load_w_bf16(P, C, wd, K, N, name, rowscale=None, n0=0):
    kc = K // 128
    wb = P.sb([128, kc, N], BF16, name)
    for c in range(kc):
        for a in range(0, N, 1024):
            b = min(N, a + 1024)
            stg = C.stg[C.stg_i % 2]
            q = 'sp' if C.stg_i % 2 == 0 else 'pool'
            C.stg_i += 1
            P.dma(stg[:, 0:b - a], wd[c * 128:(c + 1) * 128, n0 + a:n0 + b], q=q)
            if rowscale is not None:
                P.ts(wb[:, c, a:b], stg[:, 0:b - a], rowscale[:, c:c + 1], ALU.mult)
            else:
                P.copy(wb[:, c, a:b], stg[:, 0:b - a])
    return wb


def setup_common(P):
    C = Ctx()
    C.stg = [P.sb([128, 1024], F32, "stg0"), P.sb([128, 1024], F32, "stg1")]
    C.stg_i = 0
    identd = P.dram("ident", [128, 128], F32, "ExternalInput")
    idf = P.sb([128, 128], F32, "idf")
    C.identf = idf
    C.ident = P.sb([128, 128], BF16, "idb")
    P.dma(idf[:], identd[:])
    P.copy(C.ident[:], idf[:])
    C.sm = [P.sb([128, 64], F32, f"sm{i}") for i in range(4)]
    C.sm_i = 0
    return C


def rstd_from_ssq(P, out, ssq, n, tmp, eps=EPS):
    P.ts(tmp, ssq, 1.0 / n, ALU.mult, eps, ALU.add)
    P.act(tmp, tmp, AF.Sqrt)
    P.recip(out, tmp)


def norm_T(P, C, x, dstT, junk, xn):
    sm = C.sm[C.sm_i % 4]
    C.sm_i += 1
    P.act(junk, x, AF.Square, accum=sm[:, 0:1])
    rstd_from_ssq(P, sm[:, 2:3], sm[:, 0:1], 1024.0, sm[:, 1:2])
    P.ts(xn, x, sm[:, 2:3], ALU.mult)
    to_T(P, C, xn, dstT)


def to_T(P, C, xb, dstT, nchunk=8):
    pt = C.pT[C.pT_i % len(C.pT)]
    C.pT_i += 1
    for c in range(nchunk):
        P.transpose(pt[:, c * 128:(c + 1) * 128], xb[:, c * 128:(c + 1) * 128], C.ident[:])
    P.copy(dstT, pt[:, 0:nchunk * 128].re("p (c t) -> p c t", c=nchunk), eng='act')


def build_back(Tn, variant, last):
    nc = bass.Bass("TRN2", target_bir_lowering=False)
    TT = Tn + 128
    SW = 512
    with ExitStack() as es:
        P = Prog(nc, es)
        C = setup_common(P)
        xh = P.dram("xh", [TT, 1024], F32, "ExternalInput")
        mi = {}
        if variant == 'sb':
            names = ['o']
        elif variant == 'gla':
            names = ['o', 'r']
        elif variant == 'ml':
            names = ['o', 'r']
            mi['den'] = P.dram("den", [TT, 4], F32, "ExternalInput")
        elif variant == 'rwkv':
            names = ['o', 'bonus', 'g']
        for n in names:
            mi[n] = P.dram(n, [TT, 1024], F32, "ExternalInput")
        rows = {}
        rown = {'sb': [], 'gla': ['onorm'], 'ml': ['onorm'], 'rwkv': ['gng', 'gnb']}[variant]
        if last:
            rown = rown + ['fnorm']
        for n in rown:
            d = P.dram(n, [128, 1024], F32, "ExternalInput")
            rows[n] = P.sb([128, 1024], F32, "row_" + n)
            P.dma(rows[n][:], d[:])
        w_out = P.dram("w_out", [1024, 1024], F32, "ExternalInput")
        n2 = P.dram("n2", [128, 8], F32, "ExternalInput")
        w_up = P.dram("w_up", [1024, 5632], F32, "ExternalInput")
        cw = P.dram("cw", [128, NCH * 3], F32, "ExternalInput")
        cb = P.dram("cb", [128, NCH], F32, "ExternalInput")
        w_down = P.dram("w_down", [NF, 1024], F32, "ExternalInput")
        out = P.dram("out", [Tn, 1024], F32, "ExternalOutput")

        n2s = P.sb([128, 8], F32, "n2s")
        cws = P.sb([128, NCH * 3], F32, "cws")
        cbs = P.sb([128, NCH], F32, "cbs")
        P.dma(n2s[:], n2[:])
        P.dma(cws[:], cw[:])
        P.dma(cbs[:], cb[:])
        woutb = load_w_bf16(P, C, w_out, 1024, 1024, "woutb")
        wstg = [P.sb([128, 8, 128], F32, f"wstg{i}") for i in range(2)]
        wch = [P.sb([128, 8, 128], BF16, f"wch{i}") for i in range(2)]
        wcnt = [0]
        wdb = load_w_bf16(P, C, w_down, NF, 1024, "wdb")

        C.pT = [P.ps([128, 1024], BF16, "pT0")]
        C.pT_i = 0
        py = [P.ps([128, 512], F32, "py0"), P.ps([128, 512], F32, "py1")]
        pu = [P.ps([128, 512], F32, "pu0"), P.ps([128, 512], F32, "pu1")]
        pd = [P.ps([128, 512], F32, "pd0"), P.ps([128, 512], F32, "pd1")]

        xt = [P.sb([128, 1024], F32, f"xt{i}") for i in range(2)]
        mt = {n: [P.sb([128, 1024], F32, f"m_{n}{i}") for i in range(1)] * 2 for n in names}
        dent = [P.sb([128, 4], F32, f"dent{i}") for i in range(2)] if variant == 'ml' else None
        junk = P.sb([128, 1024], F32, "junk")
        tmpf = P.sb([128, 1024], F32, "tmpf")
        ogb = P.sb([128, 1024], BF16, "ogb")
        ogT = P.sb([128, 8, 128], BF16, "ogT")
        x1 = P.sb([128, SW // 128, 1024], F32, "x1")
        x1n = P.sb([128, 1024], BF16, "x1n")
        x1nT = P.sb([128, 8, SW], BF16, "x1nT")
        hT = P.sb([128, 22, SW], BF16, "hT")
        carry = P.sb([128, NCH, 2], F32, "carry")
        ucat = [P.sb([128, SW + 2], F32, f"ucat{i}") for i in range(4)]
        cc = [P.sb([128, SW], F32, f"cc{i}") for i in range(4)]
        x2 = [P.sb([128, 1024], F32, f"x2{i}") for i in range(2)]
        P.memset(carry[:], 0.0)
        cnt = [0]

        def post(i, row0):
            b = cnt[0] % 2
            cnt[0] += 1
            P.dma(xt[b][:], xh[row0:row0 + 128, :], q='sp')
            for j, n in enumerate(names):
                P.dma(mt[n][b][:], mi[n][row0:row0 + 128, :], q='pool' if j % 2 == 0 else 'sp')
            sm = C.sm[C.sm_i % 4]
            C.sm_i += 1
            if variant == 'sb':
                P.copy(ogb[:], mt['o'][b][:])
            elif variant in ('gla', 'ml'):
                o = mt['o'][b]
                if variant == 'ml':
                    P.dma(dent[b][:], mi['den'][row0:row0 + 128, :], q='sp')
                    P.act(sm[:, 16:20], dent[b][:], AF.Abs)
                    P.ts(sm[:, 16:20], sm[:, 16:20], 1.0, ALU.max)
                    P.recip(sm[:, 20:24], sm[:, 16:20])
                    for h in range(4):
                        P.ts(o[:, h * 256:(h + 1) * 256], o[:, h * 256:(h + 1) * 256], sm[:, 20 + h:21 + h], ALU.mult)
                for h in range(4):
                    P.act(junk[:, h * 256:(h + 1) * 256], o[:, h * 256:(h + 1) * 256], AF.Square,
                          accum=sm[:, h:h + 1])
                rstd_from_ssq(P, sm[:, 8:12], sm[:, 0:4], 256.0, sm[:, 4:8])
                for h in range(4):
                    P.stt(tmpf[:, h * 256:(h + 1) * 256], o[:, h * 256:(h + 1) * 256], sm[:, 8 + h:9 + h],
                          rows['onorm'][:, h * 256:(h + 1) * 256], ALU.mult, ALU.mult)
                P.act(junk[:], mt['r'][b][:], AF.Silu if variant == 'gla' else AF.Sigmoid)
                P.tt(ogb[:], tmpf[:], junk[:], ALU.mult)
            elif variant == 'rwkv':
                y = mt['o'][b]
                y3 = y[:].re("p (h n) -> p h n", h=16)
                P.reduce(sm[:, 0:16], y3, ALU.add)
                P.tt(junk[:], y[:], y[:], ALU.mult)
                P.reduce(sm[:, 16:32], junk[:].re("p (h n) -> p h n", h=16), ALU.add)
                P.ts(sm[:, 0:16], sm[:, 0:16], 1.0 / 64, ALU.mult)
                P.tt(sm[:, 32:48], sm[:, 0:16], sm[:, 0:16], ALU.mult)
                P.stt(sm[:, 16:32], sm[:, 16:32], 1.0 / 64, sm[:, 32:48], ALU.mult, ALU.subtract)
                P.ts(sm[:, 16:32], sm[:, 16:32], 64e-5, ALU.add)
                P.act(sm[:, 16:32], sm[:, 16:32], AF.Sqrt)
                P.recip(sm[:, 32:48], sm[:, 16:32])
                t3 = tmpf[:].re("p (h n) -> p h n", h=16)
                P.tt(t3, y3, sm[:, 0:16].bc(2, [128, 16, 64]), ALU.subtract)
                P.tt(t3, t3, sm[:, 32:48].bc(2, [128, 16, 64]), ALU.mult)
                P.tt(tmpf[:], tmpf[:], rows['gng'][:], ALU.mult)
                P.tt(tmpf[:], tmpf[:], rows['gnb'][:], ALU.add)
                P.tt(tmpf[:], tmpf[:], mt['bonus'][b][:], ALU.add)
                P.tt(ogb[:], tmpf[:], mt['g'][b][:], ALU.mult)
            return xt[b]

        def token_tile(i, row0):
            xtile = post(i, row0)
            to_T(P, C, ogb[:], ogT[:])
            for hf in range(2):
                for c in range(8):
                    P.mm(py[hf][:], ogT[:, c, :], woutb[:, c, hf * 512:(hf + 1) * 512], start=(c == 0), stop=(c == 7))
                P.tt(x1[:, i, hf * 512:(hf + 1) * 512], xtile[:, hf * 512:(hf + 1) * 512], py[hf][:], ALU.add)
            norm_T(P, C, x1[:, i, :], x1nT[:, :, i * 128:(i + 1) * 128], junk[:], x1n[:])

        def up_conv(W, halo):
            for j in range(22):
                cs = []
                for part, ch in enumerate((j, j + 22)):
                    k = (2 * j + part) % 4
                    pp = pu[part]
                    wi = wcnt[0] % 2
                    wcnt[0] += 1
                    P.dma(wstg[wi][:], w_up[:, ch * 128:(ch + 1) * 128].re("(c p) f -> p c f", p=128),
                          q='sp' if wi == 0 else 'pool')
                    P.tt(wch[wi][:], wstg[wi][:], n2s[:].bc(2, [128, 8, 128]), ALU.mult)
                    for c in range(8):
                        P.mm(pp[:, 0:W], wch[wi][:, c, :], x1nT[:, c, 0:W],
                             start=(c == 0), stop=(c == 7))
                    u = ucat[k]
                    P.copy(u[:, 0:2], carry[:, ch, :], eng='pool')
                    P.copy(u[:, 2:2 + W], pp[:, 0:W], eng='act')
                    P.copy(carry[:, ch, :], u[:, W:W + 2], eng='pool')
                    if halo:
                        continue
                    c_ = cc[k]
                    P.ts(c_[:, 0:W], u[:, 2:2 + W], cws[:, ch * 3 + 2:ch * 3 + 3], ALU.mult, cbs[:, ch:ch + 1], ALU.add)
                    P.stt(c_[:, 0:W], u[:, 1:1 + W], cws[:, ch * 3 + 1:ch * 3 + 2], c_[:, 0:W], ALU.mult, ALU.add)
                    P.stt(c_[:, 0:W], u[:, 0:W], cws[:, ch * 3:ch * 3 + 1], c_[:, 0:W], ALU.mult, ALU.add)
                    cs.append(c_)
                if halo:
                    continue
                P.act(cs[0][:, 0:W], cs[0][:, 0:W], AF.Silu)
                P.tt(hT[:, j, 0:W], cs[0][:, 0:W], cs[1][:, 0:W], ALU.mult)

        def down(nt, tok0):
            for i in range(nt):
                xo = x2[i % 2]
                for hf in range(2):
                    for j in range(22):
                        P.mm(pd[hf][:], hT[:, j, i * 128:(i + 1) * 128], wdb[:, j, hf * 512:(hf + 1) * 512],
                             start=(j == 0), stop=(j == 21))
                    P.tt(xo[:, hf * 512:(hf + 1) * 512], x1[:, i, hf * 512:(hf + 1) * 512], pd[hf][:], ALU.add)
                if last:
                    sm = C.sm[C.sm_i % 4]
                    C.sm_i += 1
                    P.act(junk[:], xo[:], AF.Square, accum=sm[:, 0:1])
                    rstd_from_ssq(P, sm[:, 2:3], sm[:, 0:1], 1024.0, sm[:, 1:2])
                    P.stt(xo[:], xo[:], sm[:, 2:3], rows['fnorm'][:], ALU.mult, ALU.mult)
                P.dma(out[tok0 + i * 128: tok0 + (i + 1) * 128, :], xo[:], q='sp')

        token_tile(0, 0)
        up_conv(128, True)
        for s in range(Tn // SW):
            for i in range(SW // 128):
                token_tile(i, 128 + s * SW + i * 128)
            up_conv(SW, False)
            down(SW // 128, s * SW)
        P.emit([out.key])
    return nc


def build_front(Tn, N):
    nc = bass.Bass("TRN2", target_bir_lowering=False)
    with ExitStack() as es:
        P = Prog(nc, es)
        C = setup_common(P)
        x = P.dram("x", [Tn, 1024], F32, "ExternalInput")
        W = P.dram("W", [1024, N], F32, "ExternalInput")
        n1 = P.dram("n1", [128, 8], F32, "ExternalInput")
        out = P.dram("out", [Tn, N], F32, "ExternalOutput")
        n1s = P.sb([128, 8], F32, "n1s")
        P.dma(n1s[:], n1[:])
        wb = load_w_bf16(P, C, W, 1024, N, "wb", rowscale=n1s)
        C.pT = [P.ps([128, 1024], BF16, "pT0")]
        C.pT_i = 0
        py = [P.ps([128, 512], F32, f"py{i}") for i in range(4)]
        xt = [P.sb([128, 1024], F32, f"xt{i}") for i in range(2)]
        junk = P.sb([128, 1024], F32, "junk")
        xn = P.sb([128, 1024], BF16, "xn")
        xT = [P.sb([128, 8, 128], BF16, f"xT{i}") for i in range(2)]
        ysb = [P.sb([128, N], F32, f"ysb{i}") for i in range(2)]
        k = 0
        for t in range(Tn // 128):
            b = t % 2
            P.dma(xt[b][:], x[t * 128:(t + 1) * 128, :], q='sp')
            norm_T(P, C, xt[b][:], xT[b][:], junk[:], xn[:])
            for n0 in range(0, N, 512):
                n1_ = min(N, n0 + 512)
                pp = py[k % 4]
                for c in range(8):
                    P.mm(pp[:, 0:n1_ - n0], xT[b][:, c, :], wb[:, c, n0:n1_], start=(c == 0), stop=(c == 7))
                P.copy(ysb[b][:, n0:n1_], pp[:, 0:n1_ - n0], eng='act' if k % 2 == 0 else 'dve')
                k += 1
            P.dma(out[t * 128:(t + 1) * 128, :], ysb[b][:], q='pool')
        P.emit([out.key])
    return nc


RW_N = 3328


def build_front_rwkv(Tn):
    nc = bass.Bass("TRN2", target_bir_lowering=False)
    N = RW_N
    with ExitStack() as es:
        P = Prog(nc, es)
        C = setup_common(P)
        x = P.dram("x", [Tn, 1024], F32, "ExternalInput")
        xp = P.dram("xp", [Tn, 1024], F32, "ExternalInput")
        W = P.dram("W", [1024, N], F32, "ExternalInput")
        n1 = P.dram("n1", [128, 8], F32, "ExternalInput")
        mu = P.dram("mu", [128, 48], F32, "ExternalInput")
        w2a2 = P.dram("w2a2", [128, 1024], F32, "ExternalInput")
        g2 = P.dram("g2", [128, 1024], F32, "ExternalInput")
        rown = ['w0', 'a0', 'k_k', 'k_a', 'r_k']
        rows = {}
        for n in rown:
            d = P.dram(n, [128, 1024], F32, "ExternalInput")
            rows[n] = P.sb([128, 1024], F32, "row_" + n)
            P.dma(rows[n][:], d[:])
        out = P.dram("out", [Tn, 8, 1024], F32, "ExternalOutput")
        n1s = P.sb([128, 8], F32, "n1s")
        mus = P.sb([128, 6, 8], F32, "mus")
        s1 = P.sb([128, 6, 8], F32, "s1")
        s2 = P.sb([128, 6, 8], F32, "s2")
        P.dma(n1s[:], n1[:])
        P.dma(mus[:], mu[:].re("p (j c) -> p j c", j=6))
        P.tt(s2[:], mus[:], n1s[:].bc(1, [128, 6, 8]), ALU.mult)
        P.tt(s1[:], n1s[:].bc(1, [128, 6, 8]), s2[:], ALU.subtract)
        wb = P.sb([128, 16, N], BF16, "wb")
        blocks = [(0, 1024, 0), (1024, 2048, 2), (2048, 3072, 3), (3072, 3136, 1), (3136, 3200, 4), (3200, 3328, 5)]
        for c in range(8):
            for (a, b_, j) in blocks:
                stg = C.stg[C.stg_i % 2]
                q = 'sp' if C.stg_i % 2 == 0 else 'pool'
                C.stg_i += 1
                P.dma(stg[:, 0:b_ - a], W[c * 128:(c + 1) * 128, a:b_], q=q)
                P.ts(wb[:, c, a:b_], stg[:, 0:b_ - a], s1[:, j, c:c + 1], ALU.mult)
                P.ts(wb[:, 8 + c, a:b_], stg[:, 0:b_ - a], s2[:, j, c:c + 1], ALU.mult)
        w2a2b = P.sb([128, 1024], BF16, "w2a2b")
        g2b = P.sb([128, 1024], BF16, "g2b")
        for (src, dst) in ((w2a2, w2a2b), (g2, g2b)):
            stg = C.stg[C.stg_i % 2]
            C.stg_i += 1
            P.dma(stg[:], src[:])
            P.copy(dst[:], stg[:])
        C.pT = [P.ps([128, 1024], BF16, "pT0")]
        C.pT_i = 0
        py = [P.ps([128, 512], F32, f"py{i}") for i in range(2)]
        pz = [P.ps([128, 1024], F32, f"pz{i}") for i in range(2)]
        xt = [P.sb([128, 1024], F32, f"xt{i}") for i in range(2)]
        junk = P.sb([128, 1024], F32, "junk")
        xn = P.sb([128, 1024], BF16, "xn")
        xT = P.sb([128, 16, 128], BF16, "xT")
        ysb = P.sb([128, N], F32, "ysb")
        hb = P.sb([128, 256], BF16, "hb")
        hbT = P.sb([128, 2, 128], BF16, "hbT")
        ob = [P.sb([128, 1024], F32, f"ob{i}") for i in range(6)]
        tA = P.sb([128, 1024], F32, "tA")
        tB = P.sb([128, 1024], F32, "tB")
        kq = 0
        oi = [0]

        def emit_out(t, slot, view):
            P.dma(out[t * 128:(t + 1) * 128, slot, :], view, q='pool' if slot % 2 == 0 else 'sp')

        def nxt():
            o = ob[oi[0] % 6]
            oi[0] += 1
            return o

        for t in range(Tn // 128):
            P.dma(xt[0][:], x[t * 128:(t + 1) * 128, :], q='sp')
            P.dma(xt[1][:], xp[t * 128:(t + 1) * 128, :], q='pool')
            norm_T(P, C, xt[0][:], xT[:, 0:8, :], junk[:], xn[:])
            norm_T(P, C, xt[1][:], xT[:, 8:16, :], junk[:], xn[:])
            for n0 in range(0, N, 512):
                n1_ = min(N, n0 + 512)
                pp = py[kq % 2]
                for c in range(16):
                    P.mm(pp[:, 0:n1_ - n0], xT[:, c, :], wb[:, c, n0:n1_], start=(c == 0), stop=(c == 15))
                P.copy(ysb[:, n0:n1_], pp[:, 0:n1_ - n0], eng='act' if kq % 2 == 0 else 'dve')
                kq += 1
            r = ysb[:, 0:1024]
            kx = ysb[:, 1024:2048]
            v = ysb[:, 2048:3072]
            emit_out(t, 0, r)
            emit_out(t, 3, v)
            P.act(hb[:, 0:64], ysb[:, 3072:3136], AF.Tanh)
            P.copy(hb[:, 64:128], ysb[:, 3136:3200])
            P.act(hb[:, 128:256], ysb[:, 3200:3328], AF.Sigmoid)
            to_T(P, C, hb[:], hbT[:], nchunk=2)
            sm = C.sm[C.sm_i % 4]
            C.sm_i += 1
            for hf in range(2):
                P.mm(pz[0][:, hf * 512:(hf + 1) * 512], hbT[0:64, 0, :], w2a2b[0:64, hf * 512:(hf + 1) * 512])
            P.tt(tA[:], pz[0][:], rows['w0'][:], ALU.add)
            P.act(tA[:], tA[:], AF.Sigmoid)
            o_w = nxt()
            P.act(o_w[:], tA[:], AF.Exp, scale=-float(np.exp(-0.5)))
            emit_out(t, 1, o_w[:])
            for hf in range(2):
                P.mm(pz[1][:, hf * 512:(hf + 1) * 512], hbT[64:128, 0, :], w2a2b[64:128, hf * 512:(hf + 1) * 512])
            P.tt(tA[:], pz[1][:], rows['a0'][:], ALU.add)
            P.act(tA[:], tA[:], AF.Sigmoid)
            for hf in range(2):
                P.mm(pz[0][:, hf * 512:(hf + 1) * 512], hbT[:, 1, :], g2b[:, hf * 512:(hf + 1) * 512])
            o_g = nxt()
            P.copy(o_g[:], pz[0][:], eng='act')
            emit_out(t, 6, o_g[:])
            o_kk = nxt()
            P.tt(o_kk[:], kx, rows['k_k'][:], ALU.mult)
            P.tt(junk[:], o_kk[:], o_kk[:], ALU.mult)
            P.reduce(sm[:, 0:16], junk[:].re("p (h n) -> p h n", h=16), ALU.add)
            P.act(sm[:, 0:16], sm[:, 0:16], AF.Sqrt)
            P.ts(sm[:, 0:16], sm[:, 0:16], 1e-12, ALU.max)
            P.recip(sm[:, 16:32], sm[:, 0:16])
            kk3 = o_kk[:].re("p (h n) -> p h n", h=16)
            P.tt(kk3, kk3, sm[:, 16:32].bc(2, [128, 16, 64]), ALU.mult)
            emit_out(t, 4, o_kk[:])
            o_b = nxt()
            P.tt(o_b[:], o_kk[:], tA[:], ALU.mult)
            emit_out(t, 5, o_b[:])
            P.stt(tB[:], tA[:], -1.0, rows['k_a'][:], ALU.add, ALU.mult)
            o_k = nxt()
            P.stt(o_k[:], tB[:], 1.0, kx, ALU.add, ALU.mult)
            emit_out(t, 2, o_k[:])
            P.tt(tB[:], r, o_k[:], ALU.mult)
            P.tt(tB[:], tB[:], rows['r_k'][:], ALU.mult)
            P.reduce(sm[:, 32:48], tB[:].re("p (h n) -> p h n", h=16), ALU.add)
            o_bo = nxt()
            P.tt(o_bo[:].re("p (h n) -> p h n", h=16), v.re("p (h n) -> p h n", h=16),
                 sm[:, 32:48].bc(2, [128, 16, 64]), ALU.mult)
            emit_out(t, 7, o_bo[:])
        P.emit([out.key])
    return nc


def build_rwkv_core(S):
    nc = bass.Bass("TRN2", target_bir_lowering=False)
    nblk = S // 256
    with ExitStack() as es:
        P = Prog(nc, es)
        X = P.dram("X", [nblk, 128, 5 * 256], F32, "ExternalInput")
        sel = P.dram("sel", [128, 64 * 128], F32, "ExternalInput")
        vT = P.dram("vT", [128, S], F32, "ExternalInput")
        yT = P.dram("yT", [128, S], F32, "ExternalOutput")
        sels = P.sb([128, 64, 128], F32, "sels")
        for i in range(4):
            P.dma(sels[:, i * 16:(i + 1) * 16, :], sel[:, i * 2048:(i + 1) * 2048].re("p (j m) -> p j m", j=16),
                  q='sp' if i % 2 == 0 else 'pool')
        CH = min(S, 2048)
        vch = [P.sb([128, CH], F32, f"vch{i}") for i in range(2)]
        ych = [P.sb([128, CH], F32, f"ych{i}") for i in range(2)]
        St = P.sb([128, 64], F32, "St")
        tmp = P.sb([128, 64], F32, "tmp")
        sa = P.sb([128, 1], F32, "sa")
        P.memset(St[:], 0.0)
        xb = [P.sb([128, 5, 256], F32, f"xb{i}") for i in range(2)]
        pA = [P.ps([128, 512], F32, f"pA{i}") for i in range(2)]
        pB = [P.ps([128, 512], F32, f"pB{i}") for i in range(2)]
        pC = [P.ps([128, 512], F32, f"pC{i}") for i in range(2)]
        for blk in range(nblk):
            ci = (blk * 256) // CH
            cb = ci % 2
            if (blk * 256) % CH == 0:
                P.dma(vch[cb][:], vT[:, ci * CH:(ci + 1) * CH], q='pool')
            xbb = xb[blk % 2]
            P.dma(xbb[:], X[blk].re("p (s n) -> p s n", s=5), q='sp')
            for j in range(64):
                pb = j % 2
                P.mm(pA[pb][:, 0:256], sels[:, j, :], xbb[:, 0, :])
                P.mm(pA[pb][:, 256:512], sels[:, j, :], xbb[:, 1, :])
                P.mm(pB[pb][:, 0:256], sels[:, j, :], xbb[:, 2, :])
                P.mm(pB[pb][:, 256:512], sels[:, j, :], xbb[:, 3, :])
                P.mm(pC[pb][:, 0:256], sels[:, j, :], xbb[:, 4, :])
                for tq in range(4):
                    tl = (blk * 256) % CH + j * 4 + tq
                    a, b_ = tq * 64, (tq + 1) * 64
                    P.op('dve', (lambda e, a=a, b_=b_, pb=pb: e.scalar_tensor_tensor(
                        tmp[:].ap, St[:].ap, -1.0, pA[pb][:, a:b_].ap, ALU.mult, ALU.mult, accum_out=sa[:].ap)),
                        reads=[St[:], pA[pb][:]], writes=[tmp[:], sa[:]])
                    P.tt(St[:], St[:], pA[pb][:, 256 + a:256 + b_], ALU.mult)
                    P.stt(St[:], pB[pb][:, a:b_], sa[:], St[:], ALU.mult, ALU.add)
                    P.stt(St[:], pB[pb][:, 256 + a:256 + b_], vch[cb][:, tl:tl + 1], St[:], ALU.mult, ALU.add)
                    P.op('dve', (lambda e, a=a, b_=b_, pb=pb, cb=cb, tl=tl: e.scalar_tensor_tensor(
                        tmp[:].ap, St[:].ap, 1.0, pC[pb][:, a:b_].ap, ALU.mult, ALU.mult,
                        accum_out=ych[cb][:, tl:tl + 1].ap)),
                        reads=[St[:], pC[pb][:]], writes=[tmp[:], ych[cb][:]])
            if (blk * 256 + 256) % CH == 0:
                P.dma(yT[:, ci * CH:(ci + 1) * CH], ych[cb][:], q='pool')
        P.emit([yT.key])
    return nc


def build_lin_core(S, NV, kind):
    nc = bass.Bass("TRN2", target_bir_lowering=False)
    gs = 1.0 / 16 if kind == 'gla' else 1.0
    qscale = 128.0 ** -0.5
    with ExitStack() as es:
        P = Prog(nc, es)
        C = setup_common(P)
        q = P.dram("q", [S, 128], F32, "ExternalInput")
        k = P.dram("k", [S, 128], F32, "ExternalInput")
        v = P.dram("v", [S, NV], F32, "ExternalInput")
        tri = P.dram("tri", [128, 128], F32, "ExternalInput")
        su = P.dram("su", [128, 128], F32, "ExternalInput")
        out = P.dram("out", [S, NV], F32, "ExternalOutput")
        tris = P.sb([128, 128], F32, "tris")
        sus = P.sb([128, 128], F32, "sus")
        P.dma(tris[:], tri[:])
        P.dma(sus[:], su[:])
        onec = P.sb([128, 1], F32, "onec")
        P.memset(onec[:], 1.0)
        if kind == 'gla':
            alowT = P.dram("alowT", [16, S], F32, "ExternalInput")
            wau = P.dram("wau", [16, 128], F32, "ExternalInput")
            bal = P.dram("bal", [1, 128], F32, "ExternalInput")
            waus = P.sb([16, 128], F32, "waus")
            bals = P.sb([1, 128], F32, "bals")
            oner = P.sb([1, 128], F32, "oner")
            P.dma(waus[:], wau[:])
            P.dma(bals[:], bal[:])
            P.memset(oner[:], 1.0)
            alT = [P.sb([16, 128], F32, f"alT{i}") for i in range(2)]
        else:
            gates = P.dram("gates", [S, 2], F32, "ExternalInput")
            bif = P.dram("bif", [128, 2], F32, "ExternalInput")
            bifs = P.sb([128, 2], F32, "bifs")
            P.dma(bifs[:], bif[:])
            gt = [P.sb([128, 2], F32, f"gt{i}") for i in range(2)]
            igc = [P.sb([128, 1], F32, f"igc{i}") for i in range(2)]
            lc = P.sb([128, 1], F32, "lc")
        C.pT = [P.ps([128, 1024], BF16, "pT0")]
        C.pT_i = 0
        pz = P.ps([128, 512], F32, "pz")
        psc = P.ps([128, 512], F32, "psc")
        po = [P.ps([128, 512], F32, f"po{i}") for i in range(2)]
        pS = P.ps([128, 512], F32, "pS")
        pe = P.ps([128, 512], F32, "pe")
        qt = [P.sb([128, 128], F32, f"qt{i}") for i in range(2)]
        kt = [P.sb([128, 128], F32, f"kt{i}") for i in range(2)]
        vt = [P.sb([128, NV], F32, f"vt{i}") for i in range(2)]
        vb = P.sb([128, NV], BF16, "vb")
        L = P.sb([128, 128], F32, "L")
        eq = P.sb([128, 128], F32, "eq")
        ek = P.sb([128, 128], F32, "ek")
        ee = P.sb([128, 128], F32, "ee")
        qk = P.sb([128, 256], BF16, "qk")
        qkT = P.sb([128, 2, 128], BF16, "qkT")
        ke = P.sb([128, 128], BF16, "ke")
        scT = P.sb([128, 128], BF16, "scT")
        Sf = P.sb([128, NV], F32, "Sf")
        Sb = P.sb([128, NV], BF16, "Sb")
        el = P.sb([128, 1], F32, "el")
        ot = [P.sb([128, NV], F32, f"ot{i}") for i in range(2)]
        P.memset(Sf[:], 0.0)
        P.memset(Sb[:], 0.0)
        for t in range(S // 128):
            b = t % 2
            r0 = t * 128
            P.dma(qt[b][:], q[r0:r0 + 128, :], q='sp')
            P.dma(kt[b][:], k[r0:r0 + 128, :], q='pool')
            P.dma(vt[b][:], v[r0:r0 + 128, :], q='sp')
            P.copy(vb[:], vt[b][:], eng='pool')
            ig = None
            if kind == 'gla':
                P.dma(alT[b][:], alowT[:, r0:r0 + 128], q='pool')
                P.mm(pz[:, 0:128], alT[b][:], waus[:], start=True, stop=False)
                P.mm(pz[:, 0:128], oner[:], bals[:], start=False, stop=True)
                P.act(L[:], pz[:, 0:128], AF.Exp, scale=-1.0)
                P.act(L[:], L[:], AF.Ln, bias=onec[:])
            else:
                P.dma(gt[b][:], gates[r0:r0 + 128, :], q='pool')
                P.tt(igc[b][:], gt[b][:, 0:1], bifs[:, 0:1], ALU.add)
                ig = igc[b]
                P.tt(lc[:], gt[b][:, 1:2], bifs[:, 1:2], ALU.add)
                P.act(lc[:], lc[:], AF.Exp, scale=-1.0)
                P.act(lc[:], lc[:], AF.Ln, bias=onec[:])
                P.copy(L[:], View(lc[:].ap.to_broadcast([128, 128]), lc.key))
            P.mm(pz[:, 128:256], tris[:], L[:])
            P.mm(pz[:, 256:384], sus[:], L[:])
            P.mm(pe[:, 0:1], L[:], onec[:])
            P.act(eq[:], pz[:, 128:256], AF.Exp, scale=-gs)
            if ig is not None:
                P.act(ek[:], pz[:, 128:256], AF.Exp, scale=gs, bias=ig[:])
                P.act(ee[:], pz[:, 256:384], AF.Exp, scale=-gs, bias=ig[:])
            else:
                P.act(ek[:], pz[:, 128:256], AF.Exp, scale=gs)
                P.act(ee[:], pz[:, 256:384], AF.Exp, scale=-gs)
            P.act(el[:], pe[:, 0:1], AF.Exp, scale=-gs)
            P.stt(qk[:, 0:128], qt[b][:], qscale, eq[:], ALU.mult, ALU.mult)
            P.tt(qk[:, 128:256], kt[b][:], ek[:], ALU.mult)
            P.tt(ke[:], kt[b][:], ee[:], ALU.mult)
            to_T(P, C, qk[:], qkT[:], nchunk=2)
            P.mm(psc[:, 0:128], qkT[:, 1, :], qkT[:, 0, :])
            P.tt(scT[:], psc[:, 0:128], tris[:], ALU.mult)
            pp = po[b]
            P.mm(pp[:, 0:NV], scT[:], vb[:], start=True, stop=False)
            P.mm(pp[:, 0:NV], qkT[:, 0, :], Sb[:], start=False, stop=True)
            P.copy(ot[b][:], pp[:, 0:NV], eng='act')
            P.dma(out[r0:r0 + 128, :], ot[b][:], q='sp')
            P.mm(pS[:, 0:NV], ke[:], vb[:])
            P.stt(Sf[:], Sf[:], el[:], pS[:, 0:NV], ALU.mult, ALU.add)
            P.copy(Sb[:], Sf[:])
        P.emit([out.key])
    return nc


def build_sb_core(S):
    nc = bass.Bass("TRN2", target_bir_lowering=False)
    nb = S // 128
    with ExitStack() as es:
        P = Prog(nc, es)
        qT = P.dram("qT", [2, 64, S], F32, "ExternalInput")
        kT = P.dram("kT", [2, 64, S], F32, "ExternalInput")
        v = P.dram("v", [2, S, 64], F32, "ExternalInput")
        mstr = P.dram("mstr", [128, 128], F32, "ExternalInput")
        umat = P.dram("umat", [128, 128], F32, "ExternalInput")
        out = P.dram("out", [2, S, 64], F32, "ExternalOutput")
        stg = [P.sb([128, 2048], F32, f"stg{i}") for i in range(2)]
        si = [0]

        def load_cast(dst, src, np_):
            s_ = stg[si[0] % 2]
            q_ = 'sp' if si[0] % 2 == 0 else 'pool'
            si[0] += 1
            w_ = dst.ap.shape[-1] if False else None
            return s_, q_

        qb = P.sb([64, 2, S], BF16, "qb")
        kb = P.sb([64, 2, S], BF16, "kb")
        vb = P.sb([128, 2, nb, 64], BF16, "vb")
        CH = min(S, 2048)
        for h in range(2):
            for c0 in range(0, S, CH):
                for (src, dst) in ((qT, qb), (kT, kb)):
                    s_ = stg[si[0] % 2]
                    q_ = 'sp' if si[0] % 2 == 0 else 'pool'
                    si[0] += 1
                    P.dma(s_[0:64, 0:CH], src[h, :, c0:c0 + CH], q=q_)
                    P.copy(dst[:, h, c0:c0 + CH], s_[0:64, 0:CH], eng='dve' if si[0] % 2 == 0 else 'act')
            for c0 in range(0, nb, 32):
                c1 = min(nb, c0 + 32)
                s_ = stg[si[0] % 2]
                q_ = 'sp' if si[0] % 2 == 0 else 'pool'
                si[0] += 1
                P.dma(s_[:, 0:(c1 - c0) * 64].re("p (b d) -> p b d", d=64),
                      v[h, c0 * 128:c1 * 128, :].re("(b p) d -> p b d", p=128), q=q_)
                P.copy(vb[:, h, c0:c1, :], s_[:, 0:(c1 - c0) * 64].re("p (b d) -> p b d", d=64))
        mf = P.sb([128, 128], F32, "mf")
        uf = P.sb([128, 128], F32, "uf")
        P.dma(mf[:], mstr[:])
        P.dma(uf[:], umat[:])
        mb = P.sb([128, 128], BF16, "mb")
        ub = P.sb([128, 128], BF16, "ub")
        onesb = P.sb([128, 128], BF16, "onesb")
        onec = P.sb([128, 1], F32, "onec")
        P.copy(mb[:], mf[:])
        P.copy(ub[:], uf[:])
        P.memset(onesb[:], 1.0)
        P.memset(onec[:], 1.0)
        pz = [P.ps([128, 512], F32, f"pz{i}") for i in range(2)]
        psuf = [P.ps([128, 512], F32, f"psuf{i}") for i in range(2)]
        pcb = P.ps([128, 512], F32, "pcb")
        po = [P.ps([128, 512], F32, f"po{i}") for i in range(2)]
        e_ = [P.sb([128, 512], F32, f"e{i}") for i in range(2)]
        sp_ = [P.sb([128, 512], BF16, f"sp{i}") for i in range(2)]
        sfx = [P.sb([128, 512], F32, f"sfx{i}") for i in range(2)]
        w_ = [P.sb([128, 512], BF16, f"w{i}") for i in range(2)]
        CB = P.sb([128, 128], F32, "CB")
        ot = [P.sb([128, 64], F32, f"ot{i}") for i in range(2)]
        gi = 0
        qi = 0
        for h in range(2):
            for i in range(nb):
                blocks = list(range(i, -1, -1))
                groups = [blocks[a:a + 4] for a in range(0, len(blocks), 4)]
                pp = po[qi % 2]
                qs = qb[:, h, i * 128:(i + 1) * 128]
                nmm = 0
                for g_i, grp in enumerate(groups):
                    n = len(grp)
                    W = n * 128
                    b2 = gi % 2
                    gi += 1
                    for b, jb in enumerate(grp):
                        P.mm(pz[b2][:, b * 128:(b + 1) * 128], kb[:, h, jb * 128:(jb + 1) * 128], qs)
                    P.act(e_[b2][:, 0:W], pz[b2][:, 0:W], AF.Exp, scale=0.125)
                    P.act(sp_[b2][:, 0:W], e_[b2][:, 0:W], AF.Ln, bias=onec[:])
                    if g_i == 0:
                        P.tt(sp_[b2][:, 0:128], sp_[b2][:, 0:128], mb[:], ALU.mult)
                    for b in range(n):
                        P.mm(psuf[b2][:, b * 128:(b + 1) * 128], ub[:], sp_[b2][:, b * 128:(b + 1) * 128],
                             start=True, stop=(b == 0))
                        for b_ in range(b):
                            P.mm(psuf[b2][:, b * 128:(b + 1) * 128], onesb[:], sp_[b2][:, b_ * 128:(b_ + 1) * 128],
                                 start=False, stop=(b_ == b - 1))
                    last_g = (g_i == len(groups) - 1)
                    if not last_g:
                        for b in range(n):
                            P.mm(pcb[:, 0:128], onesb[:], sp_[b2][:, b * 128:(b + 1) * 128],
                                 start=(b == 0), stop=(b == n - 1))
                    if g_i == 0:
                        P.copy(sfx[b2][:, 0:W], psuf[b2][:, 0:W])
                    else:
                        P.tt(sfx[b2][:, 0:W].re("p (b t) -> p b t", t=128), psuf[b2][:, 0:W].re("p (b t) -> p b t", t=128),
                             CB[:].bc(1, [128, n, 128]), ALU.add)
                    if not last_g:
                        if g_i == 0:
                            P.copy(CB[:], pcb[:, 0:128])
                        else:
                            P.tt(CB[:], CB[:], pcb[:, 0:128], ALU.add)
                    P.act(sfx[b2][:, 0:W], sfx[b2][:, 0:W], AF.Exp, scale=-1.0)
                    P.tt(w_[b2][:, 0:W], e_[b2][:, 0:W], sfx[b2][:, 0:W], ALU.mult)
                    if g_i == 0:
                        P.tt(w_[b2][:, 0:128], w_[b2][:, 0:128], mb[:], ALU.mult)
                    for b, jb in enumerate(grp):
                        P.mm(pp[:, 0:64], w_[b2][:, b * 128:(b + 1) * 128], vb[:, h, jb, :],
                             start=(nmm == 0), stop=(nmm == len(blocks) - 1))
                        nmm += 1
                P.copy(ot[qi % 2][:], pp[:, 0:64], eng='act')
                P.dma(out[h, i * 128:(i + 1) * 128, :], ot[qi % 2][:], q='sp' if qi % 2 == 0 else 'pool')
                qi += 1
        P.emit([out.key])
    return nc


_S = 16384
_TN = 2048
_NC = 8
_PROG = {}


def _prog(key, fn):
    if key not in _PROG:
        _PROG[key] = fn()
    return _PROG[key]


def _run(nc, in_maps):
    res = run_bass_kernel_spmd(nc, in_maps, core_ids=list(range(_NC)))
    return res.results


def _col(g):
    return np.ascontiguousarray(g.reshape(8, 128).T)


def _rep(r, n=128):
    return np.ascontiguousarray(np.broadcast_to(r, (n, r.shape[-1])))


def _halo(arr, c):
    a = c * _TN
    if c == 0:
        return np.concatenate([np.zeros((128,) + arr.shape[1:], arr.dtype), arr[0:_TN]], axis=0)
    return np.ascontiguousarray(arr[a - 128:a + _TN])


def _front(x, W, g):
    N = W.shape[1]
    nc = _prog(('front', N), lambda: build_front(_TN, N))
    ident = np.eye(128, dtype=np.float32)
    n1 = _col(g)
    W = np.ascontiguousarray(W)
    maps = [dict(x=np.ascontiguousarray(x[c * _TN:(c + 1) * _TN]), W=W, n1=n1, ident=ident) for c in range(_NC)]
    res = _run(nc, maps)
    return np.concatenate([r["out"] for r in res], axis=0)


def _back(variant, last, x, mix, rows, p, li, final_norm=None):
    nc = _prog(('back', variant, last), lambda: build_back(_TN, variant, last))
    ident = np.eye(128, dtype=np.float32)
    cwv = p[f"l{li}_ffn_conv_w"]
    base = dict(ident=ident, w_out=np.ascontiguousarray(p['w_out']), n2=_col(p[f"l{li}_norm2"]),
                w_up=np.ascontiguousarray(p[f"l{li}_ffn_w_up"]),
                cw=np.ascontiguousarray(cwv.reshape(3, NCH, 128).transpose(2, 1, 0).reshape(128, NCH * 3)),
                cb=np.ascontiguousarray(p[f"l{li}_ffn_conv_b"].reshape(NCH, 128).T),
                w_down=np.ascontiguousarray(p[f"l{li}_ffn_w_down"]))
    for k_, v_ in rows.items():
        base[k_] = _rep(v_)
    if last:
        base['fnorm'] = _rep(final_norm)
    maps = []
    for c in range(_NC):
        m = dict(base)
        m['xh'] = _halo(x, c)
        for k_, v_ in mix.items():
            m[k_] = _halo(v_, c)
        maps.append(m)
    res = _run(nc, maps)
    return np.concatenate([r["out"] for r in res], axis=0)


def _pack_rwkv_streams(streams, S):
    nblk = S // 256
    X = np.zeros((nblk, 128, 5, 256), np.float32)
    for s, arr in enumerate(streams):
        a5 = arr.reshape(2, nblk, 64, 4, 64)
        X[:, :, s, :] = a5.transpose(1, 0, 2, 3, 4).reshape(nblk, 128, 256)
    return X.reshape(nblk, 128, 1280)


def _make_sel():
    sel = np.zeros((128, 64, 128), np.float32)
    for h in range(2):
        for j in range(64):
            sel[h * 64 + j, j, h * 64:(h + 1) * 64] = 1.0
    return sel.reshape(128, 64 * 128)


def kernel(**inp):
    p = {k_: np.asarray(v_, dtype=np.float32) for k_, v_ in inp.items()}
    S = _S
    x = np.ascontiguousarray(p['x'][0])
    tri = np.triu(np.ones((128, 128), np.float32))
    su = np.tril(np.ones((128, 128), np.float32), -1)
    ident = np.eye(128, dtype=np.float32)

    y = _front(x, p['l0_gla_w_in'], p['l0_norm1'])
    q, k, v, r, alow = y[:, 0:512], y[:, 512:1024], y[:, 1024:2048], y[:, 2048:3072], y[:, 3072:3088]
    nc = _prog(('lin', 'gla'), lambda: build_lin_core(S, 128, 'gla'))
    alowT = np.ascontiguousarray(alow.T)
    maps = []
    for c in range(_NC):
        h, hf = c // 2, c % 2
        maps.append(dict(q=np.ascontiguousarray(q[:, h * 128:(h + 1) * 128]),
                         k=np.ascontiguousarray(k[:, h * 128:(h + 1) * 128]),
                         v=np.ascontiguousarray(v[:, h * 256 + hf * 128:h * 256 + hf * 128 + 128]),
                         tri=tri, su=su, ident=ident, alowT=alowT,
                         wau=np.ascontiguousarray(p['l0_gla_w_alpha_up'][:, h * 128:(h + 1) * 128]),
                         bal=np.ascontiguousarray(p['l0_gla_b_alpha'][None, h * 128:(h + 1) * 128])))
    res = _run(nc, maps)
    o = np.zeros((S, 1024), np.float32)
    for c in range(_NC):
        h, hf = c // 2, c % 2
        o[:, h * 256 + hf * 128:h * 256 + hf * 128 + 128] = res[c]["out"]
    pp = dict(p)
    pp['w_out'] = p['l0_gla_w_out']
    x = _back('gla', False, x, dict(o=o, r=np.ascontiguousarray(r)), dict(onorm=p['l0_gla_out_norm']), pp, 0)
    del y, o

    nc = _prog(('front_rwkv',), lambda: build_front_rwkv(_TN))
    xp = np.concatenate([np.zeros((1, 1024), np.float32), x[:-1]], axis=0)
    Wc = np.ascontiguousarray(np.concatenate([p['l1_rwkv_w_rkv'][0], p['l1_rwkv_w_rkv'][1], p['l1_rwkv_w_rkv'][2],
                                               p['l1_rwkv_w1'], p['l1_rwkv_a1'], p['l1_rwkv_g1']], axis=1))
    base = dict(W=Wc, n1=_col(p['l1_norm1']),
                mu=np.ascontiguousarray(p['l1_rwkv_mu'].reshape(6, 8, 128).transpose(2, 0, 1).reshape(128, 48)),
                w2a2=np.ascontiguousarray(np.concatenate([p['l1_rwkv_w2'], p['l1_rwkv_a2']], axis=0)),
                g2=np.ascontiguousarray(p['l1_rwkv_g2']),
                w0=_rep(p['l1_rwkv_w0']), a0=_rep(p['l1_rwkv_a0']), k_k=_rep(p['l1_rwkv_k_k']),
                k_a=_rep(p['l1_rwkv_k_a']), r_k=_rep(p['l1_rwkv_r_k']), ident=ident)
    maps = []
    for c in range(_NC):
        m = dict(base)
        m['x'] = np.ascontiguousarray(x[c * _TN:(c + 1) * _TN])
        m['xp'] = np.ascontiguousarray(xp[c * _TN:(c + 1) * _TN])
        maps.append(m)
    res = _run(nc, maps)
    F = np.concatenate([r_["out"] for r_ in res], axis=0)
    del xp
    nc = _prog(('rwkv_core',), lambda: build_rwkv_core(S))
    sel = _make_sel()
    maps = []
    for c in range(_NC):
        cs = slice(c * 128, (c + 1) * 128)

        def hs(slot):
            return np.ascontiguousarray(F[:, slot, cs].reshape(S, 2, 64).transpose(1, 0, 2))
        X = _pack_rwkv_streams([hs(4), hs(1), hs(5), hs(2), hs(0)], S)
        maps.append(dict(X=X, sel=sel, vT=np.ascontiguousarray(F[:, 3, cs].T)))
    res = _run(nc, maps)
    del maps
    yv = np.zeros((S, 1024), np.float32)
    for c in range(_NC):
        yv[:, c * 128:(c + 1) * 128] = res[c]["yT"].T
    pp['w_out'] = p['l1_rwkv_w_out']
    x = _back('rwkv', False, x, dict(o=yv, bonus=np.ascontiguousarray(F[:, 7]), g=np.ascontiguousarray(F[:, 6])),
              dict(gng=p['l1_rwkv_gn_g'], gnb=p['l1_rwkv_gn_b']), pp, 1)
    del F, yv

    y = _front(x, p['l2_sb_w_qkv'], p['l2_norm1'])
    q, k, v = y[:, 0:1024], y[:, 1024:2048], y[:, 2048:3072]
    nc = _prog(('sb_core',), lambda: build_sb_core(S))
    mstr = np.triu(np.ones((128, 128), np.float32), 1)
    umat = np.tril(np.ones((128, 128), np.float32))
    maps = []
    for c in range(_NC):
        cs = slice(c * 128, (c + 1) * 128)
        maps.append(dict(qT=np.ascontiguousarray(q[:, cs].reshape(S, 2, 64).transpose(1, 2, 0)),
                         kT=np.ascontiguousarray(k[:, cs].reshape(S, 2, 64).transpose(1, 2, 0)),
                         v=np.ascontiguousarray(v[:, cs].reshape(S, 2, 64).transpose(1, 0, 2)),
                         mstr=mstr, umat=umat))
    res = _run(nc, maps)
    o = np.zeros((S, 1024), np.float32)
    for c in range(_NC):
        o[:, c * 128:(c + 1) * 128] = res[c]["out"].transpose(1, 0, 2).reshape(S, 128)
    pp['w_out'] = p['l2_sb_w_out']
    x = _back('sb', False, x, dict(o=o), {}, pp, 2)
    del y, o

    y = _front(x, p['l3_ml_w_in'], p['l3_norm1'])
    q, k, v, opre, ifp = y[:, 0:512], y[:, 512:1024], y[:, 1024:2048], y[:, 2048:3072], y[:, 3072:3080]
    nc = _prog(('lin', 'ml'), lambda: build_lin_core(S, 129, 'ml'))
    ones = np.ones((S, 1), np.float32)
    maps = []
    for c in range(_NC):
        h, hf = c // 2, c % 2
        maps.append(dict(q=np.ascontiguousarray(q[:, h * 128:(h + 1) * 128]),
                         k=np.ascontiguousarray(k[:, h * 128:(h + 1) * 128]),
                         v=np.ascontiguousarray(np.concatenate([v[:, h * 256 + hf * 128:h * 256 + hf * 128 + 128], ones], axis=1)),
                         tri=tri, su=su, ident=ident,
                         gates=np.ascontiguousarray(np.stack([ifp[:, h], ifp[:, 4 + h]], axis=1)),
                         bif=_rep(np.array([p['l3_ml_b_if'][h], p['l3_ml_b_if'][4 + h]], np.float32))))
    res = _run(nc, maps)
    num = np.zeros((S, 1024), np.float32)
    den = np.zeros((S, 4), np.float32)
    for c in range(_NC):
        h, hf = c // 2, c % 2
        num[:, h * 256 + hf * 128:h * 256 + hf * 128 + 128] = res[c]["out"][:, 0:128]
        if hf == 0:
            den[:, h] = res[c]["out"][:, 128]
    pp['w_out'] = p['l3_ml_w_out']
    x = _back('ml', True, x, dict(o=num, r=np.ascontiguousarray(opre), den=den), dict(onorm=p['l3_ml_out_norm']), pp, 3,
              final_norm=p['final_norm'])
    return x.reshape(1, S, 1024).astype(np.float32)
```

```python
import numpy as np
from contextlib import ExitStack
import concourse.bass as bass
import concourse.mybir as mybir
from concourse.bass_utils import run_bass_kernel_spmd

F32 = mybir.dt.float32
BF16 = mybir.dt.bfloat16
AF = mybir.ActivationFunctionType
ALU = mybir.AluOpType
AX = mybir.AxisListType

ENGS = ['pe', 'act', 'dve', 'pool', 'sp']
NDS = 12


class View:
    __slots__ = ('ap', 'key')

    def __init__(self, ap, key):
        self.ap = ap
        self.key = key

    def __getitem__(self, idx):
        return View(self.ap[idx], self.key)

    def re(self, pat, **kw):
        return View(self.ap.rearrange(pat, **kw), self.key)

    def k(self, key):
        return View(self.ap, key)

    def bc(self, axis, shape):
        return View(self.ap.unsqueeze(axis).to_broadcast(list(shape)), self.key)


class Tile:
    def __init__(self, handle, key, is_dram=False):
        self.h = handle
        self.key = key
        self.is_dram = is_dram

    def __getitem__(self, idx):
        return View(self.h[idx], self.key)

    def sub(self, k):
        return Tile(self.h, (self.key, k), self.is_dram)


class Prog:
    def __init__(self, nc, es):
        self.nc = nc
        self.es = es
        self.ops = {e: [] for e in ENGS}
        self.cnt = {e: 0 for e in ENGS}
        self.know = {e: {} for e in ENGS}
        self.esem = {e: es.enter_context(nc.semaphore(f"s_{e}")) for e in ENGS}
        self.dsem = {q: [es.enter_context(nc.semaphore(f"d_{q}{i}")) for i in range(NDS)]
                     for q in ('sp', 'pool', 'act')}
        self.dcnt = {q: 0 for q in ('sp', 'pool', 'act')}
        self.dtok = {q: [None] * NDS for q in ('sp', 'pool', 'act')}
        self.semobj = {}
        for e in ENGS:
            self.semobj[f"s_{e}"] = self.esem[e]
        for q in self.dsem:
            for i, s in enumerate(self.dsem[q]):
                self.semobj[f"d_{q}{i}"] = s
        self.last_w = {}
        self.readers = {}
        self.nt = 0
        self.n_wait = 0
        self.cval = {}
        self.dram_keys = set()
        self.phase = "p0"
        self.tes = es

    def sb(self, shape, dt=F32, name=None):
        self.nt += 1
        name = f"{self.phase}_{name or 't'}_{self.nt}"
        h = self.tes.enter_context(self.nc.sbuf_tensor(name, list(shape), dt))
        return Tile(h, name)

    def ps(self, shape, dt=F32, name=None):
        self.nt += 1
        name = f"{self.phase}_{name or 'p'}_{self.nt}"
        h = self.tes.enter_context(self.nc.psum_tensor(name, list(shape), dt))
        return Tile(h, name)

    def phase_begin(self, name):
        self.phase = name
        self.tes = ExitStack()
        self.tes.__enter__()

    def wait_only(self, eng, toks):
        know = self.know[eng]
        waits = {}
        for (tsem, tval, tclk) in toks:
            if know.get(tsem, 0) >= tval:
                continue
            waits[tsem] = max(waits.get(tsem, 0), tval)
            for s_, v_ in tclk.items():
                if know.get(s_, 0) < v_:
                    know[s_] = v_
            know[tsem] = tval
        if waits:
            self.ops[eng].append((list(waits.items()), None, None, 0))

    def barrier(self):
        toks = []
        for e in ENGS:
            if self.cnt[e] > 0:
                toks.append((f"s_{e}", self.cnt[e], self.know[e]))
        for q in self.dtok:
            for t in self.dtok[q]:
                if t is not None:
                    toks.append((t[0], t[1], t[2]))
        for cs, v in self.cval.items():
            toks.append((cs, v, {}))
        toks = [(a, b, dict(c)) for (a, b, c) in toks]
        for e in ENGS:
            self.wait_only(e, toks)

    def phase_end(self, final_keys=None):
        self.barrier()
        if final_keys:
            toks = []
            for k in final_keys:
                for t in self.last_w.get(k, {}).values():
                    toks.append((t[0], t[1], t[2]))
            self.wait_only('sp', toks)
        self.flush()
        self.tes.__exit__(None, None, None)
        self.last_w = {k: v for k, v in self.last_w.items() if k in self.dram_keys}
        self.readers = {k: v for k, v in self.readers.items() if k in self.dram_keys}

    def flush(self):
        nc = self.nc
        ops = self.ops
        self.ops = {e: [] for e in ENGS}
        with nc.Block() as block:
            def run(engname):
                def f(eng):
                    for waits, emit, semname, inc in ops[engname]:
                        for s, v in waits:
                            eng.wait_ge(self.semobj[s], v)
                        if emit is None:
                            continue
                        ins = emit(eng)
                        ins.then_inc(self.semobj[semname], inc)
                return f
            if ops['sp']:
                block.sync(run('sp'))
            if ops['pe']:
                block.tensor(run('pe'))
            if ops['act']:
                block.scalar(run('act'))
            if ops['dve']:
                block.vector(run('dve'))
            if ops['pool']:
                block.gpsimd(run('pool'))

    def dram(self, name, shape, dt, kind, **kw):
        h = self.nc.dram_tensor(name, list(shape), dt, kind=kind, **kw).ap()
        self.dram_keys.add(name)
        return Tile(h, name, True)

    def op(self, eng, emit, reads=(), writes=(), dma=False, csem=None):
        deps = []
        rkeys = [v.key if isinstance(v, View) else v for v in reads]
        wkeys = [v.key if isinstance(v, View) else v for v in writes]
        for k in rkeys:
            for t in self.last_w.get(k, {}).values():
                deps.append((t, 'raw'))
        for k in wkeys:
            for t in self.last_w.get(k, {}).values():
                deps.append((t, 'waw'))
            for t in self.readers.get(k, {}).values():
                deps.append((t, 'war'))
        if csem is not None:
            semname = csem
            if csem not in self.semobj:
                self.semobj[csem] = self.es.enter_context(self.nc.semaphore(csem))
                self.cval[csem] = 0
            self.cval[csem] += 1
            val = self.cval[csem]
            inc = 1
            dma = True
            slot = None
        elif dma:
            q = eng
            j = self.dcnt[q]
            slot = j % NDS
            prev = self.dtok[q][slot]
            if prev is not None:
                deps.append((prev, 'raw'))
            semname = f"d_{q}{slot}"
            val = 16 * (j // NDS + 1)
            self.dcnt[q] += 1
            inc = 16
        else:
            self.cnt[eng] += 1
            semname = f"s_{eng}"
            val = self.cnt[eng]
            inc = 1
        know = self.know[eng]
        waits = {}
        for (tok, kind) in deps:
            tsem, tval, tclk, teng, tdma = tok
            if kind != 'raw' and teng == eng and not tdma and not dma:
                continue
            if know.get(tsem, 0) >= tval:
                continue
            if waits.get(tsem, 0) < tval:
                waits[tsem] = tval
            for s, v in tclk.items():
                if know.get(s, 0) < v:
                    know[s] = v
            if know.get(tsem, 0) < tval:
                know[tsem] = tval
        clk = dict(know)
        tok = (semname, val, clk, eng, dma)
        if dma and slot is not None:
            self.dtok[eng][slot] = tok
        for k in wkeys:
            self.last_w.setdefault(k, {})[semname] = tok
            self.readers[k] = {}
        for k in rkeys:
            self.readers.setdefault(k, {})[semname] = tok
        self.n_wait += len(waits)
        self.ops[eng].append((list(waits.items()), emit, semname, inc))
        return tok

    def mm(self, out, lhsT, rhs, start=True, stop=True, extra_reads=()):
        w = [out]
        r = [lhsT, rhs] + list(extra_reads)
        return self.op('pe', lambda e: e.matmul(out.ap, lhsT.ap, rhs.ap, start=start, stop=stop),
                       reads=r, writes=w)

    def transpose(self, out, in_, ident):
        return self.op('pe', lambda e: e.transpose(out.ap, in_.ap, ident.ap),
                       reads=[in_, ident], writes=[out])

    def act(self, out, in_, func, bias=None, scale=None, accum=None, eng='act'):
        reads = [in_]
        kw = {}
        if bias is not None:
            if isinstance(bias, View):
                reads.append(bias)
                kw['bias'] = bias.ap
            else:
                kw['bias'] = bias
        if scale is not None:
            if isinstance(scale, View):
                reads.append(scale)
                kw['scale'] = scale.ap
            else:
                kw['scale'] = scale
        writes = [out]
        if accum is not None:
            kw['accum_out'] = accum.ap
            writes.append(accum)
        return self.op(eng, lambda e: e.activation(out.ap, in_.ap, func, **kw), reads=reads, writes=writes)

    def tt(self, out, in0, in1, op, eng='dve'):
        return self.op(eng, lambda e: e.tensor_tensor(out.ap, in0.ap, in1.ap, op),
                       reads=[in0, in1], writes=[out])

    def ts(self, out, in0, s1, op0, s2=None, op1=None, accum=None, eng='dve'):
        reads = [in0]
        a1 = s1
        a2 = s2
        if isinstance(s1, View):
            reads.append(s1)
            a1 = s1.ap
        if isinstance(s2, View):
            reads.append(s2)
            a2 = s2.ap
        writes = [out]
        kw = {}
        if op1 is not None:
            kw['op1'] = op1
        if accum is not None:
            kw['accum_out'] = accum.ap
            writes.append(accum)
        return self.op(eng, lambda e: e.tensor_scalar(out.ap, in0.ap, a1, a2, op0, **kw),
                       reads=reads, writes=writes)

    def stt(self, out, in0, scalar, in1, op0, op1, eng='dve'):
        reads = [in0, in1]
        sc = scalar
        if isinstance(scalar, View):
            reads.append(scalar)
            sc = scalar.ap
        return self.op(eng, lambda e: e.scalar_tensor_tensor(out.ap, in0.ap, sc, in1.ap, op0, op1),
                       reads=reads, writes=[out])

    def copy(self, out, in_, eng='dve'):
        if eng == 'act':
            return self.op(eng, lambda e: e.copy(out.ap, in_.ap), reads=[in_], writes=[out])
        return self.op(eng, lambda e: e.tensor_copy(out.ap, in_.ap), reads=[in_], writes=[out])

    def memset(self, out, val, eng='dve'):
        return self.op(eng, lambda e: e.memset(out.ap, val), reads=[], writes=[out])

    def reduce(self, out, in_, op, axis=None, eng='dve'):
        axis = axis or AX.X
        return self.op(eng, lambda e: e.tensor_reduce(out.ap, in_.ap, axis, op), reads=[in_], writes=[out])

    def recip(self, out, in_):
        return self.op('dve', lambda e: e.reciprocal(out.ap, in_.ap), reads=[in_], writes=[out])

    def dma(self, out, in_, q='sp', **kw):
        return self.op(q, lambda e: e.dma_start(out=out.ap, in_=in_.ap, **kw), reads=[in_], writes=[out], dma=True)

    def coll(self, kind, out, in_, name):
        groups = [list(range(8))]
        return self.op('pool', lambda e: e.collective_compute(kind, ALU.bypass, replica_groups=groups,
                                                              ins=[in_.ap.opt()], outs=[out.ap.opt()]),
                       reads=[in_], writes=[out], csem="cc_" + name)

    def emit(self, final_keys):
        toks = []
        for k in final_keys:
            for t in self.last_w.get(k, {}).values():
                toks.append((t[0], t[1], t[2]))
        self.wait_only('sp', toks)
        nc = self.nc
        with nc.Block() as block:
            def run(engname):
                def f(eng):
                    for waits, emit, semname, inc in self.ops[engname]:
                        for s, v in waits:
                            eng.wait_ge(self.semobj[s], v)
                        if emit is None:
                            continue
                        ins = emit(eng)
                        ins.then_inc(self.semobj[semname], inc)
                return f
            if self.ops['sp']:
                block.sync(run('sp'))
            if self.ops['pe']:
                block.tensor(run('pe'))
            if self.ops['act']:
                block.scalar(run('act'))
            if self.ops['dve']:
                block.vector(run('dve'))
            if self.ops['pool']:
                block.gpsimd(run('pool'))


EPS = 1e-6
NF = 2816
NCH = 44


class Ctx:
    pass


def load_w_bf16(P, C, wd, K, N, name, rowscale=None, n0=0):
    kc = K // 128
    wb = P.sb([128, kc, N], BF16, name)
    for c in range(kc):
        for a in range(0, N, 1024):
            b = min(N, a + 1024)
            stg = C.stg[C.stg_i % 2]
            q = 'sp' if C.stg_i % 2 == 0 else 'pool'
            C.stg_i += 1
            P.dma(stg[:, 0:b - a], wd[c * 128:(c + 1) * 128, n0 + a:n0 + b], q=q)
            if rowscale is not None:
                P.ts(wb[:, c, a:b], stg[:, 0:b - a], rowscale[:, c:c + 1], ALU.mult)
            else:
                P.copy(wb[:, c, a:b], stg[:, 0:b - a])
    return wb


def setup_common(P):
    C = Ctx()
    C.stg = [P.sb([128, 1024], F32, "stg0"), P.sb([128, 1024], F32, "stg1")]
    C.stg_i = 0
    identd = P.dram("ident", [128, 128], F32, "ExternalInput")
    idf = P.sb([128, 128], F32, "idf")
    C.identf = idf
    C.ident = P.sb([128, 128], BF16, "idb")
    P.dma(idf[:], identd[:])
    P.copy(C.ident[:], idf[:])
    C.sm = [P.sb([128, 64], F32, f"sm{i}") for i in range(4)]
    C.sm_i = 0
    return C


def rstd_from_ssq(P, out, ssq, n, tmp, eps=EPS):
    P.ts(tmp, ssq, 1.0 / n, ALU.mult, eps, ALU.add)
    P.act(tmp, tmp, AF.Sqrt)
    P.recip(out, tmp)


def norm_T(P, C, x, dstT, junk, xn):
    sm = C.sm[C.sm_i % 4]
    C.sm_i += 1
    P.act(junk, x, AF.Square, accum=sm[:, 0:1])
    rstd_from_ssq(P, sm[:, 2:3], sm[:, 0:1], 1024.0, sm[:, 1:2])
    P.ts(xn, x, sm[:, 2:3], ALU.mult)
    to_T(P, C, xn, dstT)


def to_T(P, C, xb, dstT, nchunk=8):
    pt = C.pT[C.pT_i % len(C.pT)]
    C.pT_i += 1
    for c in range(nchunk):
        P.transpose(pt[:, c * 128:(c + 1) * 128], xb[:, c * 128:(c + 1) * 128], C.ident[:])
    P.copy(dstT, pt[:, 0:nchunk * 128].re("p (c t) -> p c t", c=nchunk), eng='act')


EPS = 1e-6
NF = 2816
NCH = 44


class Ctx:
    pass


class Src:
    def __init__(self, key, fn):
        self.key = key
        self.fn = fn


_PV = {}


def pv(q, e, half=False):
    if ('pid', q) not in _PV:
        _PV[('pid', q)] = e.snap(e.partition_id())
    if half:
        if ('h', q) not in _PV:
            _PV[('h', q)] = e.snap(_PV[('pid', q)] // 2)
        return _PV[('h', q)]
    return _PV[('pid', q)]


def dma_in(P, dst, src, row0, n, q='sp', **kw):
    return P.op(q, lambda e: e.dma_start(out=dst.ap, in_=src.fn(e, row0, n), **kw), reads=[src.key], writes=[dst], dma=True)


def dma_out(P, dst, row0, n, srcv, q='sp'):
    return P.op(q, lambda e: e.dma_start(out=dst.fn(e, row0, n), in_=srcv.ap), reads=[srcv], writes=[dst.key], dma=True)


def rows_src(tile, base=0, c0=None, c1=None):
    if c0 is None:
        return Src(tile.key, lambda e, r, n: tile.h[base + r:base + r + n, :])
    return Src(tile.key, lambda e, r, n: tile.h[base + r:base + r + n, c0:c1])


def view2d(tile, nrows, ncols):
    return Tile(tile.h[0:nrows * ncols].rearrange("(r n) -> r n", n=ncols), tile.key, True)


def load_w_bf16(P, C, wd, K, N, name, rowscale=None, n0=0):
    kc = K // 128
    wb = P.sb([128, kc, N], BF16, name)
    for c in range(kc):
        for a in range(0, N, 1024):
            b = min(N, a + 1024)
            stg = C.stg[C.stg_i % 2]
            q = 'sp' if C.stg_i % 2 == 0 else 'pool'
            C.stg_i += 1
            P.dma(stg[:, 0:b - a], wd[c * 128:(c + 1) * 128, n0 + a:n0 + b], q=q)
            if rowscale is not None:
                P.ts(wb[:, c, a:b], stg[:, 0:b - a], rowscale[:, c:c + 1], ALU.mult)
            else:
                P.copy(wb[:, c, a:b], stg[:, 0:b - a])
    return wb


def phase_common(P, G):
    C = Ctx()
    C.stg = [P.sb([128, 1024], F32, "stg0"), P.sb([128, 1024], F32, "stg1")]
    C.stg_i = 0
    C.identf = P.sb([128, 128], F32, "idf")
    C.ident = P.sb([128, 128], BF16, "idb")
    P.dma(C.identf[:], G.ident[:])
    P.copy(C.ident[:], C.identf[:])
    C.sm = [P.sb([128, 64], F32, f"sm{i}") for i in range(4)]
    C.sm_i = 0
    C.pT = [P.ps([128, 1024], BF16, "pT0")]
    C.pT_i = 0
    return C


def rstd_from_ssq(P, out, ssq, n, tmp, eps=EPS):
    P.ts(tmp, ssq, 1.0 / n, ALU.mult, eps, ALU.add)
    P.act(tmp, tmp, AF.Sqrt)
    P.recip(out, tmp)


def norm_T(P, C, x, dstT, junk, xn):
    sm = C.sm[C.sm_i % 4]
    C.sm_i += 1
    P.act(junk, x, AF.Square, accum=sm[:, 0:1])
    rstd_from_ssq(P, sm[:, 2:3], sm[:, 0:1], 1024.0, sm[:, 1:2])
    P.ts(xn, x, sm[:, 2:3], ALU.mult)
    to_T(P, C, xn, dstT)


def to_T(P, C, xb, dstT, nchunk=8):
    pt = C.pT[C.pT_i % len(C.pT)]
    C.pT_i += 1
    for c in range(nchunk):
        P.transpose(pt[:, c * 128:(c + 1) * 128], xb[:, c * 128:(c + 1) * 128], C.ident[:])
    P.copy(dstT, pt[:, 0:nchunk * 128].re("p (c t) -> p c t", c=nchunk), eng='act')


def zero_rows(P, dst_tile_rows_view, ncols, zt):
    for a in range(0, ncols, 2048):
        b = min(ncols, a + 2048)
        P.dma(dst_tile_rows_view[:, a:b], zt[:, 0:b - a], q='sp')


def ph_xtail(P, G, cur, Tn):
    t = P.sb([128, 1024], F32, "xtl")
    P.dma(t[:], cur[Tn:Tn + 128, :])
    P.dma(G.xt_loc[:], t[:])
    P.coll("AllGather", G.G_xt[128:128 * 9, :], G.xt_loc[:], "xt")
    src = Src(G.G_xt.key, lambda e, r, n: G.G_xt.h[bass.ds(pv('sp', e) * 128, 128), :])
    P.op('sp', lambda e: e.dma_start(out=cur[0:128, :].ap, in_=src.fn(e, 0, 128)),
         reads=[G.G_xt.key], writes=[cur[0:128, :]], dma=True)


def blk3(tile, NB, T):
    return Tile(tile.h[0:NB * T * 128].rearrange("(j t d) -> j t d", j=NB, d=128), tile.key, True)


def gather_y(P, G, NB, Tn, parts=None):
    parts = parts or [(G.GY, 0, NB)]
    for (gy, j0, nb) in parts:
        n = nb * Tn
        P.coll("AllGather", Tile(gy.h[0:8 * n * 128].rearrange("(r d) -> r d", d=128), gy.key, True)[:, :],
               Tile(G.Yloc.h[j0 * Tn * 128:(j0 * Tn + n) * 128].rearrange("(r d) -> r d", d=128), G.Yloc.key, True)[:, :],
               "gy")
    P.coll("AllGather", G.G_T[128:128 * 9, :], G.T_loc[:, :], "gt")
    P.op('sp', lambda e: e.dma_start(out=G.Lt[:, :].ap, in_=G.G_T.h[bass.ds(pv('sp', e) * 128, 128), :]),
         reads=[G.G_T.key], writes=[G.Lt.key], dma=True)


def slab_copy(P, G, k, NB, Tn, blockfn, q, gy=None):
    gy = gy or G.GY
    x = Tn * 128
    bsz = min(x, 8192)
    src3 = gy.h[0:8 * NB * x].rearrange("(r j a b) -> r j a b", r=8, j=NB, b=bsz)
    dst3 = G.slab[k].h[0:8 * x].rearrange("(r o a b) -> r o a b", r=8, o=1, b=bsz)
    P.op(q, lambda e: e.dma_start(out=dst3, in_=src3[:, bass.ds(blockfn(q, e), 1), :, :]),
         reads=[gy.key], writes=[G.slab[k].key], dma=True)


def ph_front(P, G, cur, W, n1, N, Tn, S, tail_cols):
    C = phase_common(P, G)
    ph_xtail(P, G, cur, Tn)
    NBf = N // 128
    rem = N - NBf * 128
    NB = NBf + (1 if rem else 0)
    Yl3 = blk3(G.Yloc, NB, Tn)
    n1s = P.sb([128, 8], F32, "n1s")
    P.dma(n1s[:], n1[:])
    wb = load_w_bf16(P, C, W, 1024, N, "wb", rowscale=n1s)
    py = [P.ps([128, 512], F32, f"py{i}") for i in range(4)]
    xt = [P.sb([128, 1024], F32, f"xt{i}") for i in range(2)]
    junk = P.sb([128, 1024], F32, "junk")
    xn = P.sb([128, 1024], BF16, "xn")
    xT = [P.sb([128, 8, 128], BF16, f"xT{i}") for i in range(2)]
    ysb = [P.sb([128, NB * 128], F32, f"ysb{i}") for i in range(2)]
    if rem:
        for b in range(2):
            P.memset(ysb[b][:, NBf * 128:NB * 128], 0.0)
    k = 0
    nt = Tn // 128
    for t in range(nt):
        b = t % 2
        P.dma(xt[b][:], cur[128 + t * 128:128 + (t + 1) * 128, :], q='sp')
        norm_T(P, C, xt[b][:], xT[b][:], junk[:], xn[:])
        for n0 in range(0, N, 512):
            n1_ = min(N, n0 + 512)
            pp = py[k % 4]
            for c in range(8):
                P.mm(pp[:, 0:n1_ - n0], xT[b][:, c, :], wb[:, c, n0:n1_], start=(c == 0), stop=(c == 7))
            P.copy(ysb[b][:, n0:n1_], pp[:, 0:n1_ - n0], eng='act' if k % 2 == 0 else 'dve')
            k += 1
        P.dma(View(Yl3.h[:, t * 128:(t + 1) * 128, :].rearrange("j t d -> t j d"), Yl3.key),
              ysb[b][:].re("p (j d) -> p j d", d=128), q='pool')
        if t == nt - 1 and tail_cols is not None:
            P.dma(G.T_loc[:, 0:1024], ysb[b][:, tail_cols:tail_cols + 1024], q='sp')
    gather_y(P, G, NB, Tn)
    return NB


RW_N = 3328


def ph_front_rwkv(P, G, cur, Wd, Tn, S):
    C = phase_common(P, G)
    ph_xtail(P, G, cur, Tn)
    N = RW_N
    NB = 64
    Yl3 = blk3(G.Yloc, NB, Tn)
    nt = Tn // 128
    rown = ['w0', 'a0', 'k_k', 'k_a', 'r_k']
    rows = {}
    for n in rown:
        rows[n] = P.sb([128, 1024], F32, "row_" + n)
        P.dma(rows[n][:], Wd[n][:])
    n1s = P.sb([128, 8], F32, "n1s")
    mus = P.sb([128, 6, 8], F32, "mus")
    s1 = P.sb([128, 6, 8], F32, "s1")
    s2 = P.sb([128, 6, 8], F32, "s2")
    P.dma(n1s[:], Wd['n1'][:])
    P.dma(mus[:], Wd['mu'][:].re("p (j c) -> p j c", j=6))
    P.tt(s2[:], mus[:], n1s[:].bc(1, [128, 6, 8]), ALU.mult)
    P.tt(s1[:], n1s[:].bc(1, [128, 6, 8]), s2[:], ALU.subtract)
    wb = P.sb([128, 16, N], BF16, "wb")
    W = Wd['W']
    blocks = [(0, 1024, 0), (1024, 2048, 2), (2048, 3072, 3), (3072, 3136, 1), (3136, 3200, 4), (3200, 3328, 5)]
    for c in range(8):
        for (a, b_, j) in blocks:
            stg = C.stg[C.stg_i % 2]
            q = 'sp' if C.stg_i % 2 == 0 else 'pool'
            C.stg_i += 1
            P.dma(stg[:, 0:b_ - a], W[c * 128:(c + 1) * 128, a:b_], q=q)
            P.ts(wb[:, c, a:b_], stg[:, 0:b_ - a], s1[:, j, c:c + 1], ALU.mult)
            P.ts(wb[:, 8 + c, a:b_], stg[:, 0:b_ - a], s2[:, j, c:c + 1], ALU.mult)
    w2a2b = P.sb([128, 1024], BF16, "w2a2b")
    g2b = P.sb([128, 1024], BF16, "g2b")
    for (src, dst) in ((Wd['w2a2'], w2a2b), (Wd['g2'], g2b)):
        stg = C.stg[C.stg_i % 2]
        C.stg_i += 1
        P.dma(stg[:], src[:])
        P.copy(dst[:], stg[:])
    py = [P.ps([128, 512], F32, f"py{i}") for i in range(2)]
    pz = [P.ps([128, 1024], F32, f"pz{i}") for i in range(2)]
    xt = [P.sb([128, 1024], F32, f"xt{i}") for i in range(2)]
    junk = P.sb([128, 1024], F32, "junk")
    xn = P.sb([128, 1024], BF16, "xn")
    xT = P.sb([128, 16, 128], BF16, "xT")
    ysb = P.sb([128, N], F32, "ysb")
    hb = P.sb([128, 256], BF16, "hb")
    hbT = P.sb([128, 2, 128], BF16, "hbT")
    ob = [P.sb([128, 1024], F32, f"ob{i}") for i in range(6)]
    tA = P.sb([128, 1024], F32, "tA")
    tB = P.sb([128, 1024], F32, "tB")
    kq = 0
    oi = [0]

    def emit_out(t, slot, view):
        P.dma(View(Yl3.h[slot * 8:(slot + 1) * 8, t * 128:(t + 1) * 128, :].rearrange("j t d -> t j d"), Yl3.key),
              view.re("p (j d) -> p j d", d=128), q='pool' if slot % 2 == 0 else 'sp')
        if t == nt - 1 and slot in (6, 7):
            P.dma(G.T_loc[:, (slot - 5) * 1024:(slot - 4) * 1024], view, q='sp')

    def nxt():
        o = ob[oi[0] % 6]
        oi[0] += 1
        return o

    for t in range(Tn // 128):
        P.dma(xt[0][:], cur[128 + t * 128:128 + (t + 1) * 128, :], q='sp')
        P.dma(xt[1][:], cur[127 + t * 128:127 + (t + 1) * 128, :], q='pool')
        norm_T(P, C, xt[0][:], xT[:, 0:8, :], junk[:], xn[:])
        norm_T(P, C, xt[1][:], xT[:, 8:16, :], junk[:], xn[:])
        for n0 in range(0, N, 512):
            n1_ = min(N, n0 + 512)
            pp = py[kq % 2]
            for c in range(16):
                P.mm(pp[:, 0:n1_ - n0], xT[:, c, :], wb[:, c, n0:n1_], start=(c == 0), stop=(c == 15))
            P.copy(ysb[:, n0:n1_], pp[:, 0:n1_ - n0], eng='act' if kq % 2 == 0 else 'dve')
            kq += 1
        r = ysb[:, 0:1024]
        kx = ysb[:, 1024:2048]
        v = ysb[:, 2048:3072]
        emit_out(t, 0, r)
        emit_out(t, 3, v)
        P.act(hb[:, 0:64], ysb[:, 3072:3136], AF.Tanh)
        P.copy(hb[:, 64:128], ysb[:, 3136:3200])
        P.act(hb[:, 128:256], ysb[:, 3200:3328], AF.Sigmoid)
        to_T(P, C, hb[:], hbT[:], nchunk=2)
        sm = C.sm[C.sm_i % 4]
        C.sm_i += 1
        for hf in range(2):
            P.mm(pz[0][:, hf * 512:(hf + 1) * 512], hbT[0:64, 0, :], w2a2b[0:64, hf * 512:(hf + 1) * 512])
        P.tt(tA[:], pz[0][:], rows['w0'][:], ALU.add)
        P.act(tA[:], tA[:], AF.Sigmoid)
        o_w = nxt()
        P.act(o_w[:], tA[:], AF.Exp, scale=-float(np.exp(-0.5)))
        emit_out(t, 1, o_w[:])
        for hf in range(2):
            P.mm(pz[1][:, hf * 512:(hf + 1) * 512], hbT[64:128, 0, :], w2a2b[64:128, hf * 512:(hf + 1) * 512])
        P.tt(tA[:], pz[1][:], rows['a0'][:], ALU.add)
        P.act(tA[:], tA[:], AF.Sigmoid)
        for hf in range(2):
            P.mm(pz[0][:, hf * 512:(hf + 1) * 512], hbT[:, 1, :], g2b[:, hf * 512:(hf + 1) * 512])
        o_g = nxt()
        P.copy(o_g[:], pz[0][:], eng='act')
        emit_out(t, 6, o_g[:])
        o_kk = nxt()
        P.tt(o_kk[:], kx, rows['k_k'][:], ALU.mult)
        P.tt(junk[:], o_kk[:], o_kk[:], ALU.mult)
        P.reduce(sm[:, 0:16], junk[:].re("p (h n) -> p h n", h=16), ALU.add)
        P.act(sm[:, 0:16], sm[:, 0:16], AF.Sqrt)
        P.ts(sm[:, 0:16], sm[:, 0:16], 1e-12, ALU.max)
        P.recip(sm[:, 16:32], sm[:, 0:16])
        kk3 = o_kk[:].re("p (h n) -> p h n", h=16)
        P.tt(kk3, kk3, sm[:, 16:32].bc(2, [128, 16, 64]), ALU.mult)
        emit_out(t, 4, o_kk[:])
        o_b = nxt()
        P.tt(o_b[:], o_kk[:], tA[:], ALU.mult)
        emit_out(t, 5, o_b[:])
        P.stt(tB[:], tA[:], -1.0, rows['k_a'][:], ALU.add, ALU.mult)
        o_k = nxt()
        P.stt(o_k[:], tB[:], 1.0, kx, ALU.add, ALU.mult)
        emit_out(t, 2, o_k[:])
        P.tt(tB[:], r, o_k[:], ALU.mult)
        P.tt(tB[:], tB[:], rows['r_k'][:], ALU.mult)
        P.reduce(sm[:, 32:48], tB[:].re("p (h n) -> p h n", h=16), ALU.add)
        o_bo = nxt()
        P.tt(o_bo[:].re("p (h n) -> p h n", h=16), v.re("p (h n) -> p h n", h=16),
             sm[:, 32:48].bc(2, [128, 16, 64]), ALU.mult)
        emit_out(t, 7, o_bo[:])
    gather_y(P, G, NB, Tn, parts=[(G.GY, 0, 24), (G.GY2, 24, 24)])
    return NB


def gy_block_src(G, NB, Tn, j, c0, c1):
    def fn(e, r, n):
        rank, loc = r // Tn, r % Tn
        off = ((rank * NB + j) * Tn + loc) * 128
        return G.GY.h[off:off + n * 128].rearrange("(t d) -> t d", d=128)[:, c0:c1]
    return Src(G.GY.key, fn)


def slab_rows(G, k, S, c0=0, c1=128):
    v = G.slab[k].h[0:S * 128].rearrange("(t d) -> t d", d=128)
    return Src(G.slab[k].key, lambda e, r, n: v[r:r + n, c0:c1])


def ph_lin_core(P, G, NB, Tn, Wd, S, NV, kind):
    C = phase_common(P, G)
    gs = 1.0 / 16 if kind == 'gla' else 1.0
    qscale = 128.0 ** -0.5
    SP = S + 128
    ol = view2d(G.oloc, SP, NV)
    oa = view2d(G.oall, 8 * SP, NV)
    qq = 'act' if kind == 'gla' else 'pool'
    slab_copy(P, G, 0, NB, Tn, lambda q, e: pv(q, e, True), qq)
    slab_copy(P, G, 1, NB, Tn, lambda q, e: pv(q, e, True) + 4, qq)
    slab_copy(P, G, 2, NB, Tn, lambda q, e: pv(q, e) + 8, qq)
    qsrc = slab_rows(G, 0, S)
    ksrc = slab_rows(G, 1, S)
    vsrc = slab_rows(G, 2, S)
    tris = P.sb([128, 128], F32, "tris")
    sus = P.sb([128, 128], F32, "sus")
    P.dma(tris[:], G.tri[:])
    P.dma(sus[:], G.su[:])
    onec = P.sb([128, 1], F32, "onec")
    P.memset(onec[:], 1.0)
    zt = P.sb([128, 2048], F32, "zt")
    P.memset(zt[:], 0.0)
    zero_rows(P, ol[0:128, :], NV, zt)
    if kind == 'gla':
        asrc = gy_block_src(G, NB, Tn, 24, 0, 16)
        waus = P.sb([16, 128], F32, "waus")
        bals = P.sb([1, 128], F32, "bals")
        oner = P.sb([1, 128], F32, "oner")
        P.dma(waus[:], Wd['wau'][:])
        P.dma(bals[:], Wd['bal'][:])
        P.memset(oner[:], 1.0)
        alt = [P.sb([128, 16], F32, f"alt{i}") for i in range(2)]
        alT = P.sb([16, 128], F32, "alT")
        pal = P.ps([128, 512], F32, "pal")
    else:
        gsrc = gy_block_src(G, NB, Tn, 24, 0, 8)
        ohs = P.sb([128, 16], F32, "ohs")
        P.dma(ohs[:], Wd['oh'][:])
        g8 = [P.sb([128, 8], F32, f"g8{i}") for i in range(2)]
        t8 = P.sb([128, 8], F32, "t8")
        bifs = P.sb([128, 2], F32, "bifs")
        P.dma(bifs[:], Wd['bif'][:])
        gt = [P.sb([128, 2], F32, f"gt{i}") for i in range(2)]
        igc = [P.sb([128, 1], F32, f"igc{i}") for i in range(2)]
        lc = P.sb([128, 1], F32, "lc")
    pz = P.ps([128, 512], F32, "pz")
    psc = P.ps([128, 512], F32, "psc")
    po = [P.ps([128, 512], F32, f"po{i}") for i in range(2)]
    pS = P.ps([128, 512], F32, "pS")
    pe = P.ps([128, 512], F32, "pe")
    qt = [P.sb([128, 128], F32, f"qt{i}") for i in range(2)]
    kt = [P.sb([128, 128], F32, f"kt{i}") for i in range(2)]
    vt = [P.sb([128, 129], F32, f"vt{i}") for i in range(2)]
    vb = P.sb([128, NV], BF16, "vb")
    L = P.sb([128, 128], F32, "L")
    eq = P.sb([128, 128], F32, "eq")
    ek = P.sb([128, 128], F32, "ek")
    ee = P.sb([128, 128], F32, "ee")
    qk = P.sb([128, 256], BF16, "qk")
    qkT = P.sb([128, 2, 128], BF16, "qkT")
    ke = P.sb([128, 128], BF16, "ke")
    scT = P.sb([128, 128], BF16, "scT")
    Sf = P.sb([128, NV], F32, "Sf")
    Sb = P.sb([128, NV], BF16, "Sb")
    el = P.sb([128, 1], F32, "el")
    ot = [P.sb([128, NV], F32, f"ot{i}") for i in range(2)]
    P.memset(Sf[:], 0.0)
    P.memset(Sb[:], 0.0)
    if NV == 129:
        for b in range(2):
            P.memset(vt[b][:, 128:129], 1.0)
    for t in range(S // 128):
        b = t % 2
        r0 = t * 128
        dma_in(P, qt[b][:], qsrc, r0, 128, 'sp')
        dma_in(P, kt[b][:], ksrc, r0, 128, 'pool')
        dma_in(P, vt[b][:, 0:128], vsrc, r0, 128, 'sp')
        P.copy(vb[:], vt[b][:, 0:NV], eng='pool')
        ig = None
        if kind == 'gla':
            dma_in(P, alt[b][:], asrc, r0, 128, 'pool')
            P.transpose(pal[0:16, 0:128], alt[b][:], C.identf[:])
            P.copy(alT[:], pal[0:16, 0:128], eng='act')
            P.mm(pz[:, 0:128], alT[:], waus[:], start=True, stop=False)
            P.mm(pz[:, 0:128], oner[:], bals[:], start=False, stop=True)
            P.act(L[:], pz[:, 0:128], AF.Exp, scale=-1.0)
            P.act(L[:], L[:], AF.Ln, bias=onec[:])
        else:
            dma_in(P, g8[b][:], gsrc, r0, 128, 'pool')
            P.tt(t8[:], g8[b][:], ohs[:, 0:8], ALU.mult)
            P.reduce(gt[b][:, 0:1], t8[:], ALU.add)
            P.tt(t8[:], g8[b][:], ohs[:, 8:16], ALU.mult)
            P.reduce(gt[b][:, 1:2], t8[:], ALU.add)
            P.tt(igc[b][:], gt[b][:, 0:1], bifs[:, 0:1], ALU.add)
            ig = igc[b]
            P.tt(lc[:], gt[b][:, 1:2], bifs[:, 1:2], ALU.add)
            P.act(lc[:], lc[:], AF.Exp, scale=-1.0)
            P.act(lc[:], lc[:], AF.Ln, bias=onec[:])
            P.copy(L[:], View(lc[:].ap.to_broadcast([128, 128]), lc.key))
        P.mm(pz[:, 128:256], tris[:], L[:])
        P.mm(pz[:, 256:384], sus[:], L[:])
        P.mm(pe[:, 0:1], L[:], onec[:])
        P.act(eq[:], pz[:, 128:256], AF.Exp, scale=-gs)
        if ig is not None:
            P.act(ek[:], pz[:, 128:256], AF.Exp, scale=gs, bias=ig[:])
            P.act(ee[:], pz[:, 256:384], AF.Exp, scale=-gs, bias=ig[:])
        else:
            P.act(ek[:], pz[:, 128:256], AF.Exp, scale=gs)
            P.act(ee[:], pz[:, 256:384], AF.Exp, scale=-gs)
        P.act(el[:], pe[:, 0:1], AF.Exp, scale=-gs)
        P.stt(qk[:, 0:128], qt[b][:], qscale, eq[:], ALU.mult, ALU.mult)
        P.tt(qk[:, 128:256], kt[b][:], ek[:], ALU.mult)
        P.tt(ke[:], kt[b][:], ee[:], ALU.mult)
        to_T(P, C, qk[:], qkT[:], nchunk=2)
        P.mm(psc[:, 0:128], qkT[:, 1, :], qkT[:, 0, :])
        P.tt(scT[:], psc[:, 0:128], tris[:], ALU.mult)
        pp = po[b]
        P.mm(pp[:, 0:NV], scT[:], vb[:], start=True, stop=False)
        P.mm(pp[:, 0:NV], qkT[:, 0, :], Sb[:], start=False, stop=True)
        P.copy(ot[b][:], pp[:, 0:NV], eng='act')
        P.dma(ol[128 + r0:128 + r0 + 128, :], ot[b][:], q='sp')
        P.mm(pS[:, 0:NV], ke[:], vb[:])
        P.stt(Sf[:], Sf[:], el[:], pS[:, 0:NV], ALU.mult, ALU.add)
        P.copy(Sb[:], Sf[:])
    P.coll("AllGather", oa[:, :], ol[:, :], "oa")


def ph_rwkv_core(P, G, NB, Tn, S):
    C = phase_common(P, G)
    nblk = S // 256
    SP = S + 128
    ol = view2d(G.oloc, SP, 128)
    oa = view2d(G.oall, 8 * SP, 128)
    sels = P.sb([128, 64, 128], F32, "sels")
    for i in range(4):
        P.dma(sels[:, i * 16:(i + 1) * 16, :], G.sel[:, i * 2048:(i + 1) * 2048].re("p (j m) -> p j m", j=16),
              q='sp' if i % 2 == 0 else 'pool')
    zt = P.sb([128, 2048], F32, "zt")
    P.memset(zt[:], 0.0)
    zero_rows(P, ol[0:128, :], 128, zt)
    CH = min(S, 2048)
    vch = [P.sb([128, CH], F32, f"vch{i}") for i in range(2)]
    ych = [P.sb([128, CH], F32, f"ych{i}") for i in range(2)]
    vtl = [P.sb([128, 128], F32, f"vtl{i}") for i in range(2)]
    otl = [P.sb([128, 128], F32, f"otl{i}") for i in range(2)]
    St = P.sb([128, 64], F32, "St")
    tmp = P.sb([128, 64], F32, "tmp")
    sa = P.sb([128, 1], F32, "sa")
    P.memset(St[:], 0.0)
    xb = [P.sb([128, 5, 256], F32, f"xb{i}") for i in range(2)]
    pA = [P.ps([128, 512], F32, f"pA{i}") for i in range(2)]
    pB = [P.ps([128, 512], F32, f"pB{i}") for i in range(2)]
    pC = [P.ps([128, 512], F32, f"pC{i}") for i in range(2)]
    pV = P.ps([128, 512], F32, "pV")
    slots = [4, 1, 5, 2, 0]
    for sl in range(6):
        slab_copy(P, G, sl, 24, Tn, (lambda q, e, sl=sl: pv(q, e) + (sl % 3) * 8), 'pool',
                  gy=(G.GY if sl < 3 else G.GY2))
    vsrc = slab_rows(G, 3, S)
    dcount = 0
    for blk in range(nblk):
        ci = (blk * 256) // CH
        cb = ci % 2
        if (blk * 256) % CH == 0:
            for tt_ in range(CH // 128):
                vb_ = vtl[tt_ % 2]
                dma_in(P, vb_[:], vsrc, ci * CH + tt_ * 128, 128, 'pool')
                P.transpose(pV[:, (tt_ % 4) * 128:(tt_ % 4 + 1) * 128], vb_[:], C.identf[:])
                P.copy(vch[cb][:, tt_ * 128:(tt_ + 1) * 128], pV[:, (tt_ % 4) * 128:(tt_ % 4 + 1) * 128], eng='act')
        xbb = xb[blk % 2]
        for s_, slot in enumerate(slots):
            for h in range(2):
                sv = G.slab[slot].h[0:S * 128].rearrange("(t d) -> t d", d=128)
                src = Src(G.slab[slot].key, lambda e, r, n, sv=sv, h=h: sv[r:r + n, h * 64:(h + 1) * 64]
                          .rearrange("(j t) k -> j t k", t=4))
                dstv = View(xbb[h * 64:(h + 1) * 64, s_, :].ap.rearrange("j (t k) -> j t k", t=4), xbb.key)
                dma_in(P, dstv, src, blk * 256, 256, 'sp' if dcount % 2 == 0 else 'pool')
                dcount += 1
        for j in range(64):
            pb = j % 2
            P.mm(pA[pb][:, 0:256], sels[:, j, :], xbb[:, 0, :])
            P.mm(pA[pb][:, 256:512], sels[:, j, :], xbb[:, 1, :])
            P.mm(pB[pb][:, 0:256], sels[:, j, :], xbb[:, 2, :])
            P.mm(pB[pb][:, 256:512], sels[:, j, :], xbb[:, 3, :])
            P.mm(pC[pb][:, 0:256], sels[:, j, :], xbb[:, 4, :])
            for tq in range(4):
                tl = (blk * 256) % CH + j * 4 + tq
                a, b_ = tq * 64, (tq + 1) * 64
                P.op('dve', (lambda e, a=a, b_=b_, pb=pb: e.scalar_tensor_tensor(
                    tmp[:].ap, St[:].ap, -1.0, pA[pb][:, a:b_].ap, ALU.mult, ALU.mult, accum_out=sa[:].ap)),
                    reads=[St[:], pA[pb][:]], writes=[tmp[:], sa[:]])
                P.tt(St[:], St[:], pA[pb][:, 256 + a:256 + b_], ALU.mult)
                P.stt(St[:], pB[pb][:, a:b_], sa[:], St[:], ALU.mult, ALU.add)
                P.stt(St[:], pB[pb][:, 256 + a:256 + b_], vch[cb][:, tl:tl + 1], St[:], ALU.mult, ALU.add)
                P.op('dve', (lambda e, a=a, b_=b_, pb=pb, cb=cb, tl=tl: e.scalar_tensor_tensor(
                    tmp[:].ap, St[:].ap, 1.0, pC[pb][:, a:b_].ap, ALU.mult, ALU.mult,
                    accum_out=ych[cb][:, tl:tl + 1].ap)),
                    reads=[St[:], pC[pb][:]], writes=[tmp[:], ych[cb][:]])
        if (blk * 256 + 256) % CH == 0:
            for tt_ in range(CH // 128):
                P.transpose(pV[:, (tt_ % 4) * 128:(tt_ % 4 + 1) * 128], ych[cb][:, tt_ * 128:(tt_ + 1) * 128], C.identf[:])
                ob_ = otl[tt_ % 2]
                P.copy(ob_[:], pV[:, (tt_ % 4) * 128:(tt_ % 4 + 1) * 128], eng='act')
                r0 = ci * CH + tt_ * 128
                P.dma(ol[128 + r0:128 + r0 + 128, :], ob_[:], q='pool')
    P.coll("AllGather", oa[:, :], ol[:, :], "oa")


def ph_sb_core(P, G, NB, Tn, S):
    C = phase_common(P, G)
    nb = S // 128
    SP = S + 128
    ol = view2d(G.oloc, SP, 128)
    oa = view2d(G.oall, 8 * SP, 128)
    qb = P.sb([64, 2, S], BF16, "qb")
    kb = P.sb([64, 2, S], BF16, "kb")
    vb = P.sb([128, 2, nb, 64], BF16, "vb")
    zt = P.sb([128, 2048], F32, "zt")
    P.memset(zt[:], 0.0)
    zero_rows(P, ol[0:128, :], 128, zt)
    for j in range(3):
        slab_copy(P, G, j, NB, Tn, (lambda q, e, j=j: pv(q, e) + j * 8), 'sp')
    srcs = [slab_rows(G, j, S) for j in range(3)]
    ld = [P.sb([128, 3, 128], F32, f"ld{i}") for i in range(2)]
    ldb = P.sb([128, 256], BF16, "ldb")
    for t in range(nb):
        l = ld[t % 2]
        for j in range(3):
            dma_in(P, l[:, j, :], srcs[j], t * 128, 128, 'sp' if j % 2 == 0 else 'pool')
        P.copy(ldb[:], l[:, 0:2, :].re("p a d -> p (a d)"))
        pt = C.pT[0]
        for j in range(4):
            P.transpose(pt[0:64, j * 128:(j + 1) * 128], ldb[:, j * 64:(j + 1) * 64], C.ident[:])
        P.copy(qb[:, :, t * 128:(t + 1) * 128], pt[0:64, 0:256].re("p (h t) -> p h t", h=2), eng='act')
        P.copy(kb[:, :, t * 128:(t + 1) * 128], pt[0:64, 256:512].re("p (h t) -> p h t", h=2), eng='act')
        P.copy(vb[:, :, t, :], l[:, 2, :].re("p (h d) -> p h d", h=2), eng='pool')
    mf = P.sb([128, 128], F32, "mf")
    uf = P.sb([128, 128], F32, "uf")
    P.dma(mf[:], G.mstr[:])
    P.dma(uf[:], G.umat[:])
    mb = P.sb([128, 128], BF16, "mb")
    ub = P.sb([128, 128], BF16, "ub")
    onesb = P.sb([128, 128], BF16, "onesb")
    onec = P.sb([128, 1], F32, "onec")
    P.copy(mb[:], mf[:])
    P.copy(ub[:], uf[:])
    P.memset(onesb[:], 1.0)
    P.memset(onec[:], 1.0)
    pz = [P.ps([128, 512], F32, f"pz{i}") for i in range(2)]
    psuf = [P.ps([128, 512], F32, f"psuf{i}") for i in range(2)]
    pcb = P.ps([128, 512], F32, "pcb")
    po = [P.ps([128, 512], F32, f"po{i}") for i in range(2)]
    e_ = [P.sb([128, 512], F32, f"e{i}") for i in range(2)]
    sp_ = [P.sb([128, 512], BF16, f"sp{i}") for i in range(2)]
    sfx = [P.sb([128, 512], F32, f"sfx{i}") for i in range(2)]
    w_ = [P.sb([128, 512], BF16, f"w{i}") for i in range(2)]
    CB = P.sb([128, 128], F32, "CB")
    ot = [P.sb([128, 128], F32, f"ot{i}") for i in range(2)]
    gi = 0
    for i in range(nb):
        otile = ot[i % 2]
        for h in range(2):
            blocks = list(range(i, -1, -1))
            groups = [blocks[a:a + 4] for a in range(0, len(blocks), 4)]
            pp = po[h]
            qs = qb[:, h, i * 128:(i + 1) * 128]
            nmm = 0
            for g_i, grp in enumerate(groups):
                n = len(grp)
                W = n * 128
                b2 = gi % 2
                gi += 1
                for b, jb in enumerate(grp):
                    P.mm(pz[b2][:, b * 128:(b + 1) * 128], kb[:, h, jb * 128:(jb + 1) * 128], qs)
                P.act(e_[b2][:, 0:W], pz[b2][:, 0:W], AF.Exp, scale=0.125)
                P.act(sp_[b2][:, 0:W], e_[b2][:, 0:W], AF.Ln, bias=onec[:])
                if g_i == 0:
                    P.tt(sp_[b2][:, 0:128], sp_[b2][:, 0:128], mb[:], ALU.mult)
                for b in range(n):
                    P.mm(psuf[b2][:, b * 128:(b + 1) * 128], ub[:], sp_[b2][:, b * 128:(b + 1) * 128],
                         start=True, stop=(b == 0))
                    for b_ in range(b):
                        P.mm(psuf[b2][:, b * 128:(b + 1) * 128], onesb[:], sp_[b2][:, b_ * 128:(b_ + 1) * 128],
                             start=False, stop=(b_ == b - 1))
                last_g = (g_i == len(groups) - 1)
                if not last_g:
                    for b in range(n):
                        P.mm(pcb[:, 0:128], onesb[:], sp_[b2][:, b * 128:(b + 1) * 128],
                             start=(b == 0), stop=(b == n - 1))
                if g_i == 0:
                    P.copy(sfx[b2][:, 0:W], psuf[b2][:, 0:W])
                else:
                    P.tt(sfx[b2][:, 0:W].re("p (b t) -> p b t", t=128), psuf[b2][:, 0:W].re("p (b t) -> p b t", t=128),
                         CB[:].bc(1, [128, n, 128]), ALU.add)
                if not last_g:
                    if g_i == 0:
                        P.copy(CB[:], pcb[:, 0:128])
                    else:
                        P.tt(CB[:], CB[:], pcb[:, 0:128], ALU.add)
                P.act(sfx[b2][:, 0:W], sfx[b2][:, 0:W], AF.Exp, scale=-1.0)
                P.tt(w_[b2][:, 0:W], e_[b2][:, 0:W], sfx[b2][:, 0:W], ALU.mult)
                if g_i == 0:
                    P.tt(w_[b2][:, 0:128], w_[b2][:, 0:128], mb[:], ALU.mult)
                for b, jb in enumerate(grp):
                    P.mm(pp[:, 0:64], w_[b2][:, b * 128:(b + 1) * 128], vb[:, h, jb, :],
                         start=(nmm == 0), stop=(nmm == len(blocks) - 1))
                    nmm += 1
            P.copy(otile[:, h * 64:(h + 1) * 64], pp[:, 0:64], eng='act')
        P.dma(ol[128 + i * 128:128 + (i + 1) * 128, :], otile[:], q='sp' if i % 2 == 0 else 'pool')
    P.coll("AllGather", oa[:, :], ol[:, :], "oa")


def ph_back(P, G, NB, cur, dst, dst_base, Wd, variant, last, Tn, S):
    C = phase_common(P, G)
    TT = Tn + 128
    SW = 512
    SP = S + 128
    NV = 129 if variant == 'ml' else 128
    names = {'sb': ['o'], 'gla': ['o', 'r'], 'ml': ['o', 'r'], 'rwkv': ['o', 'bonus', 'g']}[variant]
    g16 = 16 * NV
    src_o = G.oall.h[0:8 * SP * NV].rearrange("(r t d) -> r t d", r=8, d=g16)
    dst_o = G.Lo.h[0:8 * TT * NV].rearrange("(r t d) -> r t d", r=8, d=g16)
    P.op('act', lambda e: e.dma_start(out=dst_o, in_=src_o[:, bass.ds(pv('act', e) * (Tn // 16), TT // 16), :]),
         reads=[G.oall.key], writes=[G.Lo.key], dma=True)
    Lo3 = G.Lo.h[0:8 * TT * NV].rearrange("(r t d) -> r t d", r=8, d=NV)
    Yl3 = blk3(G.Yloc, NB, Tn)
    srcs = {}
    srcs['o'] = Src(G.Lo.key, lambda e, r, n: Lo3[:, r:r + n, 0:128].rearrange("r t d -> t r d"))
    if variant == 'ml':
        srcs['den'] = Src(G.Lo.key, lambda e, r, n: Lo3[0:8:2, r:r + n, 128:129].rearrange("r t d -> t (r d)"))

    def own_blocks(j0):
        return Src(G.Yloc.key, lambda e, r, n: Yl3.h[j0:j0 + 8, r - 128:r - 128 + n, :].rearrange("j t d -> t j d"))
    halo_src = {}
    if variant in ('gla', 'ml'):
        srcs['r'] = own_blocks(16)
        halo_src['r'] = Src(G.Lt.key, lambda e, r, n: G.Lt.h[:, 0:1024])
    if variant == 'rwkv':
        srcs['bonus'] = own_blocks(56)
        srcs['g'] = own_blocks(48)
        halo_src['bonus'] = Src(G.Lt.key, lambda e, r, n: G.Lt.h[:, 2048:3072])
        halo_src['g'] = Src(G.Lt.key, lambda e, r, n: G.Lt.h[:, 1024:2048])
    rows = {}
    rown = {'sb': [], 'gla': ['onorm'], 'ml': ['onorm'], 'rwkv': ['gng', 'gnb']}[variant]
    if last:
        rown = rown + ['fnorm']
    for n in rown:
        rows[n] = P.sb([128, 1024], F32, "row_" + n)
        P.dma(rows[n][:], Wd[n][:])
    w_up = Wd['w_up']
    n2s = P.sb([128, 8], F32, "n2s")
    cws = P.sb([128, NCH * 3], F32, "cws")
    cbs = P.sb([128, NCH], F32, "cbs")
    P.dma(n2s[:], Wd['n2'][:])
    P.dma(cws[:], Wd['cw'][:])
    P.dma(cbs[:], Wd['cb'][:])
    woutb = load_w_bf16(P, C, Wd['w_out'], 1024, 1024, "woutb")
    wstg = [P.sb([128, 8, 128], F32, f"wstg{i}") for i in range(2)]
    wch = [P.sb([128, 8, 128], BF16, f"wch{i}") for i in range(2)]
    wcnt = [0]
    wdb = load_w_bf16(P, C, Wd['w_down'], NF, 1024, "wdb")
    py = [P.ps([128, 512], F32, "py0"), P.ps([128, 512], F32, "py1")]
    pu = [P.ps([128, 512], F32, "pu0"), P.ps([128, 512], F32, "pu1")]
    pd = [P.ps([128, 512], F32, "pd0"), P.ps([128, 512], F32, "pd1")]
    xt = [P.sb([128, 1024], F32, f"xt{i}") for i in range(2)]
    mt = {n: [P.sb([128, 1024], F32, f"m_{n}")] * 2 for n in names}
    dent = [P.sb([128, 4], F32, f"dent{i}") for i in range(2)] if variant == 'ml' else None
    junk = P.sb([128, 1024], F32, "junk")
    tmpf = P.sb([128, 1024], F32, "tmpf")
    ogb = P.sb([128, 1024], BF16, "ogb")
    ogT = P.sb([128, 8, 128], BF16, "ogT")
    x1 = P.sb([128, SW // 128, 1024], F32, "x1")
    x1n = P.sb([128, 1024], BF16, "x1n")
    x1nT = P.sb([128, 8, SW], BF16, "x1nT")
    hT = P.sb([128, 22, SW], BF16, "hT")
    carry = P.sb([128, NCH, 2], F32, "carry")
    ucat = [P.sb([128, SW + 2], F32, f"ucat{i}") for i in range(4)]
    cc = [P.sb([128, SW], F32, f"cc{i}") for i in range(4)]
    x2 = [P.sb([128, 1024], F32, f"x2{i}") for i in range(2)]
    P.memset(carry[:], 0.0)
    cnt = [0]

    def post(i, row0):
        b = cnt[0] % 2
        cnt[0] += 1
        P.dma(xt[b][:], cur[row0:row0 + 128, :], q='sp')
        for j, n in enumerate(names):
            dv = mt[n][b][:]
            sr = srcs[n]
            if n != 'o' and row0 == 0:
                sr = halo_src[n]
            else:
                dv = dv.re("p (r d) -> p r d", r=8)
            dma_in(P, dv, sr, row0, 128, 'pool' if j % 2 == 0 else 'sp')
        sm = C.sm[C.sm_i % 4]
        C.sm_i += 1
        if variant == 'sb':
            P.copy(ogb[:], mt['o'][b][:])
        elif variant in ('gla', 'ml'):
            o = mt['o'][b]
            if variant == 'ml':
                dma_in(P, dent[b][:], srcs['den'], row0, 128, 'sp', allow_slow_non_contiguous=True)
                P.act(sm[:, 16:20], dent[b][:], AF.Abs)
                P.ts(sm[:, 16:20], sm[:, 16:20], 1.0, ALU.max)
                P.recip(sm[:, 20:24], sm[:, 16:20])
                for h in range(4):
                    P.ts(o[:, h * 256:(h + 1) * 256], o[:, h * 256:(h + 1) * 256], sm[:, 20 + h:21 + h], ALU.mult)
            for h in range(4):
                P.act(junk[:, h * 256:(h + 1) * 256], o[:, h * 256:(h + 1) * 256], AF.Square,
                      accum=sm[:, h:h + 1])
            rstd_from_ssq(P, sm[:, 8:12], sm[:, 0:4], 256.0, sm[:, 4:8])
            for h in range(4):
                P.stt(tmpf[:, h * 256:(h + 1) * 256], o[:, h * 256:(h + 1) * 256], sm[:, 8 + h:9 + h],
                      rows['onorm'][:, h * 256:(h + 1) * 256], ALU.mult, ALU.mult)
            P.act(junk[:], mt['r'][b][:], AF.Silu if variant == 'gla' else AF.Sigmoid)
            P.tt(ogb[:], tmpf[:], junk[:], ALU.mult)
        elif variant == 'rwkv':
            y = mt['o'][b]
            y3 = y[:].re("p (h n) -> p h n", h=16)
            P.reduce(sm[:, 0:16], y3, ALU.add)
            P.tt(junk[:], y[:], y[:], ALU.mult)
            P.reduce(sm[:, 16:32], junk[:].re("p (h n) -> p h n", h=16), ALU.add)
            P.ts(sm[:, 0:16], sm[:, 0:16], 1.0 / 64, ALU.mult)
            P.tt(sm[:, 32:48], sm[:, 0:16], sm[:, 0:16], ALU.mult)
            P.stt(sm[:, 16:32], sm[:, 16:32], 1.0 / 64, sm[:, 32:48], ALU.mult, ALU.subtract)
            P.ts(sm[:, 16:32], sm[:, 16:32], 64e-5, ALU.add)
            P.act(sm[:, 16:32], sm[:, 16:32], AF.Sqrt)
            P.recip(sm[:, 32:48], sm[:, 16:32])
            t3 = tmpf[:].re("p (h n) -> p h n", h=16)
            P.tt(t3, y3, sm[:, 0:16].bc(2, [128, 16, 64]), ALU.subtract)
            P.tt(t3, t3, sm[:, 32:48].bc(2, [128, 16, 64]), ALU.mult)
            P.tt(tmpf[:], tmpf[:], rows['gng'][:], ALU.mult)
            P.tt(tmpf[:], tmpf[:], rows['gnb'][:], ALU.add)
            P.tt(tmpf[:], tmpf[:], mt['bonus'][b][:], ALU.add)
            P.tt(ogb[:], tmpf[:], mt['g'][b][:], ALU.mult)
        return xt[b]

    def token_tile(i, row0):
        xtile = post(i, row0)
        to_T(P, C, ogb[:], ogT[:])
        for hf in range(2):
            for c in range(8):
                P.mm(py[hf][:], ogT[:, c, :], woutb[:, c, hf * 512:(hf + 1) * 512], start=(c == 0), stop=(c == 7))
            P.tt(x1[:, i, hf * 512:(hf + 1) * 512], xtile[:, hf * 512:(hf + 1) * 512], py[hf][:], ALU.add)
        norm_T(P, C, x1[:, i, :], x1nT[:, :, i * 128:(i + 1) * 128], junk[:], x1n[:])

    def up_conv(W, halo):
        for j in range(22):
            cs = []
            for part, ch in enumerate((j, j + 22)):
                k = (2 * j + part) % 4
                pp = pu[part]
                wi = wcnt[0] % 2
                wcnt[0] += 1
                P.dma(wstg[wi][:], w_up[:, ch * 128:(ch + 1) * 128].re("(c p) f -> p c f", p=128),
                      q='sp' if wi == 0 else 'pool')
                P.tt(wch[wi][:], wstg[wi][:], n2s[:].bc(2, [128, 8, 128]), ALU.mult)
                for c in range(8):
                    P.mm(pp[:, 0:W], wch[wi][:, c, :], x1nT[:, c, 0:W], start=(c == 0), stop=(c == 7))
                u = ucat[k]
                P.copy(u[:, 0:2], carry[:, ch, :], eng='pool')
                P.copy(u[:, 2:2 + W], pp[:, 0:W], eng='act')
                P.copy(carry[:, ch, :], u[:, W:W + 2], eng='pool')
                if halo:
                    continue
                c_ = cc[k]
                P.ts(c_[:, 0:W], u[:, 2:2 + W], cws[:, ch * 3 + 2:ch * 3 + 3], ALU.mult, cbs[:, ch:ch + 1], ALU.add)
                P.stt(c_[:, 0:W], u[:, 1:1 + W], cws[:, ch * 3 + 1:ch * 3 + 2], c_[:, 0:W], ALU.mult, ALU.add)
                P.stt(c_[:, 0:W], u[:, 0:W], cws[:, ch * 3:ch * 3 + 1], c_[:, 0:W], ALU.mult, ALU.add)
                cs.append(c_)
            if halo:
                continue
            P.act(cs[0][:, 0:W], cs[0][:, 0:W], AF.Silu)
            P.tt(hT[:, j, 0:W], cs[0][:, 0:W], cs[1][:, 0:W], ALU.mult)

    def down(nt, tok0):
        for i in range(nt):
            xo = x2[i % 2]
            for hf in range(2):
                for j in range(22):
                    P.mm(pd[hf][:], hT[:, j, i * 128:(i + 1) * 128], wdb[:, j, hf * 512:(hf + 1) * 512],
                         start=(j == 0), stop=(j == 21))
                P.tt(xo[:, hf * 512:(hf + 1) * 512], x1[:, i, hf * 512:(hf + 1) * 512], pd[hf][:], ALU.add)
            if last:
                sm = C.sm[C.sm_i % 4]
                C.sm_i += 1
                P.act(junk[:], xo[:], AF.Square, accum=sm[:, 0:1])
                rstd_from_ssq(P, sm[:, 2:3], sm[:, 0:1], 1024.0, sm[:, 1:2])
                P.stt(xo[:], xo[:], sm[:, 2:3], rows['fnorm'][:], ALU.mult, ALU.mult)
            r_ = dst_base + tok0 + i * 128
            P.dma(dst[r_:r_ + 128, :], xo[:], q='sp')

    token_tile(0, 0)
    up_conv(128, True)
    for s in range(Tn // SW):
        for i in range(SW // 128):
            token_tile(i, 128 + s * SW + i * 128)
        up_conv(SW, False)
        down(SW // 128, s * SW)


LAYERS = ['gla', 'rwkv', 'sb', 'ml']
W_NAMES = {
    'gla': ['W', 'n1', 'wau', 'bal', 'onorm'],
    'rwkv': ['W', 'n1', 'mu', 'w2a2', 'g2', 'w0', 'a0', 'k_k', 'k_a', 'r_k', 'gng', 'gnb'],
    'sb': ['W', 'n1'],
    'ml': ['W', 'n1', 'bif', 'oh', 'onorm', 'fnorm'],
}
W_SHAPES = {
    'n1': [128, 8], 'wau': [16, 128], 'bal': [1, 128], 'onorm': [128, 1024], 'mu': [128, 48],
    'w2a2': [128, 1024], 'g2': [128, 1024], 'w0': [128, 1024], 'a0': [128, 1024], 'k_k': [128, 1024],
    'k_a': [128, 1024], 'r_k': [128, 1024], 'gng': [128, 1024], 'gnb': [128, 1024], 'bif': [128, 2],
    'fnorm': [128, 1024], 'oh': [128, 16], 'w_out': [1024, 1024], 'n2': [128, 8], 'w_up': [1024, 5632], 'cw': [128, NCH * 3],
    'cb': [128, NCH], 'w_down': [NF, 1024],
}
W_N = {'gla': 3088, 'rwkv': RW_N, 'sb': 3072, 'ml': 3080}


def build_fused(S, layers=(0, 1, 2, 3)):
    _PV.clear()
    Tn = S // 8
    TT = Tn + 128
    SP = S + 128
    nc = bass.Bass("TRN2", target_bir_lowering=False)
    with ExitStack() as es:
        P = Prog(nc, es)
        G = Ctx()
        xin = P.dram("x", [Tn, 1024], F32, "ExternalInput")
        for nm, shp in (('ident', [128, 128]), ('tri', [128, 128]), ('su', [128, 128]), ('mstr', [128, 128]),
                        ('umat', [128, 128]), ('sel', [128, 64 * 128])):
            setattr(G, nm, P.dram(nm, shp, F32, "ExternalInput"))
        Wd = {}
        for li in layers:
            kind = LAYERS[li]
            d = {}
            for nm in W_NAMES[kind] + ['w_out', 'n2', 'w_up', 'cw', 'cb', 'w_down']:
                shp = [1024, W_N[kind]] if nm == 'W' else W_SHAPES[nm]
                d[nm] = P.dram(f"l{li}_{nm}", shp, F32, "ExternalInput")
            Wd[li] = d
        out = P.dram("out", [Tn, 1024], F32, "ExternalOutput")
        xb = [P.dram("xbufA", [TT, 1024], F32, "Internal"), P.dram("xbufB", [TT, 1024], F32, "Internal")]
        G.xt_loc = P.dram("xt_loc", [128, 1024], F32, "Internal")
        G.G_xt = P.dram("G_xt", [128 * 9, 1024], F32, "Internal")
        G.Yloc = P.dram("Yloc", [Tn * 8192], F32, "Internal")
        G.GY = P.dram("GY", [8 * 25 * Tn * 128], F32, "Internal")
        G.GY2 = P.dram("GY2", [8 * 24 * Tn * 128], F32, "Internal")
        G.T_loc = P.dram("T_loc", [128, 3072], F32, "Internal")
        G.G_T = P.dram("G_T", [128 * 9, 3072], F32, "Internal")
        G.Lt = P.dram("Lt", [128, 3072], F32, "Internal")
        G.slab = [P.dram(f"slab{k}", [S * 128], F32, "Internal") for k in range(6)]
        G.Lo = P.dram("Lo", [8 * TT * 129], F32, "Internal")
        G.oloc = P.dram("oloc", [SP * 129], F32, "Internal")
        G.oall = P.dram("oall", [8 * SP * 129], F32, "Internal")

        P.phase_begin("init")
        zt = P.sb([128, 1024], F32, "zt")
        P.memset(zt[:], 0.0)
        P.dma(G.G_xt[0:128, :], zt[:])
        for a in range(3):
            P.dma(G.G_T[0:128, a * 1024:(a + 1) * 1024], zt[:])
            P.dma(G.T_loc[:, a * 1024:(a + 1) * 1024], zt[:])
        P.dma(xb[0][0:128, :], zt[:])
        P.dma(xb[1][0:128, :], zt[:])
        for a in range(0, Tn, 256):
            P.dma(xb[0][128 + a:128 + a + 256, :], xin[a:a + 256, :], q='pool')
        P.phase_end()

        for n_, li in enumerate(layers):
            kind = LAYERS[li]
            lastl = (n_ == len(layers) - 1)
            cur = xb[n_ % 2]
            nxt = xb[(n_ + 1) % 2]
            P.phase_begin(f"f{li}")
            if kind == 'rwkv':
                NB = ph_front_rwkv(P, G, cur, Wd[li], Tn, S)
            else:
                NB = ph_front(P, G, cur, Wd[li]['W'], Wd[li]['n1'], W_N[kind], Tn, S,
                              2048 if kind in ('gla', 'ml') else None)
            P.phase_end()
            P.phase_begin(f"c{li}")
            if kind == 'gla':
                ph_lin_core(P, G, NB, Tn, Wd[li], S, 128, 'gla')
            elif kind == 'ml':
                ph_lin_core(P, G, NB, Tn, Wd[li], S, 129, 'ml')
            elif kind == 'rwkv':
                ph_rwkv_core(P, G, NB, Tn, S)
            else:
                ph_sb_core(P, G, NB, Tn, S)
            P.phase_end()
            P.phase_begin(f"b{li}")
            if lastl:
                ph_back(P, G, NB, cur, out, 0, Wd[li], kind, kind == 'ml', Tn, S)
                P.phase_end(final_keys=[out.key])
            else:
                ph_back(P, G, NB, cur, nxt, 128, Wd[li], kind, False, Tn, S)
                P.phase_end()
    return nc


def _col(g):
    return np.ascontiguousarray(g.reshape(8, 128).T)


def _rep(r, n=128):
    return np.ascontiguousarray(np.broadcast_to(r, (n, r.shape[-1])))


def _make_sel():
    sel = np.zeros((128, 64, 128), np.float32)
    for h in range(2):
        for j in range(64):
            sel[h * 64 + j, j, h * 64:(h + 1) * 64] = 1.0
    return sel.reshape(128, 64 * 128)


def fused_in_maps(p, S, layers=(0, 1, 2, 3)):
    Tn = S // 8
    x = np.ascontiguousarray(p['x'].reshape(S, 1024))
    common = dict(ident=np.eye(128, dtype=np.float32),
                  tri=np.triu(np.ones((128, 128), np.float32)),
                  su=np.tril(np.ones((128, 128), np.float32), -1),
                  mstr=np.triu(np.ones((128, 128), np.float32), 1),
                  umat=np.tril(np.ones((128, 128), np.float32)),
                  sel=_make_sel())

    def ffn(li, d):
        cwv = p[f"l{li}_ffn_conv_w"]
        d[f"l{li}_n2"] = _col(p[f"l{li}_norm2"])
        d[f"l{li}_w_up"] = np.ascontiguousarray(p[f"l{li}_ffn_w_up"])
        d[f"l{li}_cw"] = np.ascontiguousarray(cwv.reshape(3, NCH, 128).transpose(2, 1, 0).reshape(128, NCH * 3))
        d[f"l{li}_cb"] = np.ascontiguousarray(p[f"l{li}_ffn_conv_b"].reshape(NCH, 128).T)
        d[f"l{li}_w_down"] = np.ascontiguousarray(p[f"l{li}_ffn_w_down"])

    if 0 in layers:
        common['l0_W'] = np.ascontiguousarray(p['l0_gla_w_in'])
        common['l0_n1'] = _col(p['l0_norm1'])
        common['l0_onorm'] = _rep(p['l0_gla_out_norm'])
        common['l0_w_out'] = np.ascontiguousarray(p['l0_gla_w_out'])
        ffn(0, common)
    if 1 in layers:
        common['l1_W'] = np.ascontiguousarray(np.concatenate(
            [p['l1_rwkv_w_rkv'][0], p['l1_rwkv_w_rkv'][1], p['l1_rwkv_w_rkv'][2],
             p['l1_rwkv_w1'], p['l1_rwkv_a1'], p['l1_rwkv_g1']], axis=1))
        common['l1_n1'] = _col(p['l1_norm1'])
        common['l1_mu'] = np.ascontiguousarray(p['l1_rwkv_mu'].reshape(6, 8, 128).transpose(2, 0, 1).reshape(128, 48))
        common['l1_w2a2'] = np.ascontiguousarray(np.concatenate([p['l1_rwkv_w2'], p['l1_rwkv_a2']], axis=0))
        common['l1_g2'] = np.ascontiguousarray(p['l1_rwkv_g2'])
        for nm in ('w0', 'a0', 'k_k', 'k_a', 'r_k'):
            common['l1_' + nm] = _rep(p['l1_rwkv_' + nm])
        common['l1_gng'] = _rep(p['l1_rwkv_gn_g'])
        common['l1_gnb'] = _rep(p['l1_rwkv_gn_b'])
        common['l1_w_out'] = np.ascontiguousarray(p['l1_rwkv_w_out'])
        ffn(1, common)
    if 2 in layers:
        common['l2_W'] = np.ascontiguousarray(p['l2_sb_w_qkv'])
        common['l2_n1'] = _col(p['l2_norm1'])
        common['l2_w_out'] = np.ascontiguousarray(p['l2_sb_w_out'])
        ffn(2, common)
    if 3 in layers:
        common['l3_W'] = np.ascontiguousarray(p['l3_ml_w_in'])
        common['l3_n1'] = _col(p['l3_norm1'])
        common['l3_onorm'] = _rep(p['l3_ml_out_norm'])
        common['l3_fnorm'] = _rep(p['final_norm'])
        common['l3_w_out'] = np.ascontiguousarray(p['l3_ml_w_out'])
        ffn(3, common)
    maps = []
    for c in range(8):
        m = dict(common)
        m['x'] = np.ascontiguousarray(x[c * Tn:(c + 1) * Tn])
        h = c // 2
        if 0 in layers:
            m['l0_wau'] = np.ascontiguousarray(p['l0_gla_w_alpha_up'][:, h * 128:(h + 1) * 128])
            m['l0_bal'] = np.ascontiguousarray(p['l0_gla_b_alpha'][None, h * 128:(h + 1) * 128])
        if 3 in layers:
            m['l3_bif'] = _rep(np.array([p['l3_ml_b_if'][h], p['l3_ml_b_if'][4 + h]], np.float32))
            oh = np.zeros((16,), np.float32)
            oh[h] = 1.0
            oh[8 + 4 + h] = 1.0
            m['l3_oh'] = _rep(oh)
        maps.append(m)
    return maps


_PROG = {}


def kernel(**inp):
    p = {k_: np.asarray(v_, dtype=np.float32) for k_, v_ in inp.items()}
    S = 16384
    if 'nc' not in _PROG:
        _PROG['nc'] = build_fused(S)
    maps = fused_in_maps(p, S)
    res = run_bass_kernel_spmd(_PROG['nc'], maps, core_ids=list(range(8)))
    out = np.concatenate([r["out"] for r in res.results], axis=0)
    return out.reshape(1, S, 1024).astype(np.float32)
```

```python
import numpy as np
from contextlib import ExitStack
import concourse.bass as bass
import concourse.mybir as mybir
from concourse.bass_utils import run_bass_kernel_spmd

F32 = mybir.dt.float32
BF16 = mybir.dt.bfloat16
AF = mybir.ActivationFunctionType
ALU = mybir.AluOpType
AX = mybir.AxisListType

ENGS = ['pe', 'act', 'dve', 'pool', 'sp']
NDS = 12


class View:
    __slots__ = ('ap', 'key')

    def __init__(self, ap, key):
        self.ap = ap
        self.key = key

    def __getitem__(self, idx):
        return View(self.ap[idx], self.key)

    def re(self, pat, **kw):
        return View(self.ap.rearrange(pat, **kw), self.key)

    def k(self, key):
        return View(self.ap, key)

    def bc(self, axis, shape):
        return View(self.ap.unsqueeze(axis).to_broadcast(list(shape)), self.key)


class Tile:
    def __init__(self, handle, key, is_dram=False):
        self.h = handle
        self.key = key
        self.is_dram = is_dram

    def __getitem__(self, idx):
        return View(self.h[idx], self.key)

    def sub(self, k):
        return Tile(self.h, (self.key, k), self.is_dram)


class Prog:
    def __init__(self, nc, es):
        self.nc = nc
        self.es = es
        self.ops = {e: [] for e in ENGS}
        self.cnt = {e: 0 for e in ENGS}
        self.know = {e: {} for e in ENGS}
        self.esem = {e: es.enter_context(nc.semaphore(f"s_{e}")) for e in ENGS}
        self.dsem = {q: [es.enter_context(nc.semaphore(f"d_{q}{i}")) for i in range(NDS)]
                     for q in ('sp', 'pool', 'act')}
        self.dcnt = {q: 0 for q in ('sp', 'pool', 'act')}
        self.dtok = {q: [None] * NDS for q in ('sp', 'pool', 'act')}
        self.semobj = {}
        for e in ENGS:
            self.semobj[f"s_{e}"] = self.esem[e]
        for q in self.dsem:
            for i, s in enumerate(self.dsem[q]):
                self.semobj[f"d_{q}{i}"] = s
        self.last_w = {}
        self.readers = {}
        self.nt = 0
        self.n_wait = 0
        self.cval = {}
        self.dram_keys = set()
        self.phase = "p0"
        self.tes = es

    def sb(self, shape, dt=F32, name=None):
        self.nt += 1
        name = f"{self.phase}_{name or 't'}_{self.nt}"
        h = self.tes.enter_context(self.nc.sbuf_tensor(name, list(shape), dt))
        return Tile(h, name)

    def ps(self, shape, dt=F32, name=None):
        self.nt += 1
        name = f"{self.phase}_{name or 'p'}_{self.nt}"
        h = self.tes.enter_context(self.nc.psum_tensor(name, list(shape), dt))
        return Tile(h, name)

    def phase_begin(self, name):
        self.phase = name
        self.tes = ExitStack()
        self.tes.__enter__()

    def wait_only(self, eng, toks):
        know = self.know[eng]
        waits = {}
        for (tsem, tval, tclk) in toks:
            if know.get(tsem, 0) >= tval:
                continue
            waits[tsem] = max(waits.get(tsem, 0), tval)
            for s_, v_ in tclk.items():
                if know.get(s_, 0) < v_:
                    know[s_] = v_
            know[tsem] = tval
        if waits:
            self.ops[eng].append((list(waits.items()), None, None, 0))

    def barrier(self):
        toks = []
        for e in ENGS:
            if self.cnt[e] > 0:
                toks.append((f"s_{e}", self.cnt[e], self.know[e]))
        for q in self.dtok:
            for t in self.dtok[q]:
                if t is not None:
                    toks.append((t[0], t[1], t[2]))
        for cs, v in self.cval.items():
            toks.append((cs, v, {}))
        toks = [(a, b, dict(c)) for (a, b, c) in toks]
        for e in ENGS:
            self.wait_only(e, toks)

    def phase_end(self, final_keys=None):
        self.barrier()
        if final_keys:
            toks = []
            for k in final_keys:
                for t in self.last_w.get(k, {}).values():
                    toks.append((t[0], t[1], t[2]))
            self.wait_only('sp', toks)
        self.flush()
        self.tes.__exit__(None, None, None)
        self.last_w = {k: v for k, v in self.last_w.items() if k in self.dram_keys}
        self.readers = {k: v for k, v in self.readers.items() if k in self.dram_keys}

    def flush(self):
        nc = self.nc
        ops = self.ops
        self.ops = {e: [] for e in ENGS}
        with nc.Block() as block:
            def run(engname):
                def f(eng):
                    for waits, emit, semname, inc in ops[engname]:
                        for s, v in waits:
                            eng.wait_ge(self.semobj[s], v)
                        if emit is None:
                            continue
                        ins = emit(eng)
                        ins.then_inc(self.semobj[semname], inc)
                return f
            if ops['sp']:
                block.sync(run('sp'))
            if ops['pe']:
                block.tensor(run('pe'))
            if ops['act']:
                block.scalar(run('act'))
            if ops['dve']:
                block.vector(run('dve'))
            if ops['pool']:
                block.gpsimd(run('pool'))

    def dram(self, name, shape, dt, kind, **kw):
        h = self.nc.dram_tensor(name, list(shape), dt, kind=kind, **kw).ap()
        self.dram_keys.add(name)
        return Tile(h, name, True)

    def op(self, eng, emit, reads=(), writes=(), dma=False, csem=None):
        deps = []
        rkeys = [v.key if isinstance(v, View) else v for v in reads]
        wkeys = [v.key if isinstance(v, View) else v for v in writes]
        for k in rkeys:
            for t in self.last_w.get(k, {}).values():
                deps.append((t, 'raw'))
        for k in wkeys:
            for t in self.last_w.get(k, {}).values():
                deps.append((t, 'waw'))
            for t in self.readers.get(k, {}).values():
                deps.append((t, 'war'))
        if csem is not None:
            semname = csem
            if csem not in self.semobj:
                self.semobj[csem] = self.es.enter_context(self.nc.semaphore(csem))
                self.cval[csem] = 0
            self.cval[csem] += 1
            val = self.cval[csem]
            inc = 1
            dma = True
            slot = None
        elif dma:
            q = eng
            j = self.dcnt[q]
            slot = j % NDS
            prev = self.dtok[q][slot]
            if prev is not None:
                deps.append((prev, 'raw'))
            semname = f"d_{q}{slot}"
            val = 16 * (j // NDS + 1)
            self.dcnt[q] += 1
            inc = 16
        else:
            self.cnt[eng] += 1
            semname = f"s_{eng}"
            val = self.cnt[eng]
            inc = 1
        know = self.know[eng]
        waits = {}
        for (tok, kind) in deps:
            tsem, tval, tclk, teng, tdma = tok
            if kind != 'raw' and teng == eng and not tdma and not dma:
                continue
            if know.get(tsem, 0) >= tval:
                continue
            if waits.get(tsem, 0) < tval:
                waits[tsem] = tval
            for s, v in tclk.items():
                if know.get(s, 0) < v:
                    know[s] = v
            if know.get(tsem, 0) < tval:
                know[tsem] = tval
        clk = dict(know)
        tok = (semname, val, clk, eng, dma)
        if dma and slot is not None:
            self.dtok[eng][slot] = tok
        for k in wkeys:
            self.last_w.setdefault(k, {})[semname] = tok
            self.readers[k] = {}
        for k in rkeys:
            self.readers.setdefault(k, {})[semname] = tok
        self.n_wait += len(waits)
        self.ops[eng].append((list(waits.items()), emit, semname, inc))
        return tok

    def mm(self, out, lhsT, rhs, start=True, stop=True, extra_reads=()):
        w = [out]
        r = [lhsT, rhs] + list(extra_reads)
        return self.op('pe', lambda e: e.matmul(out.ap, lhsT.ap, rhs.ap, start=start, stop=stop),
                       reads=r, writes=w)

    def transpose(self, out, in_, ident):
        return self.op('pe', lambda e: e.transpose(out.ap, in_.ap, ident.ap),
                       reads=[in_, ident], writes=[out])

    def act(self, out, in_, func, bias=None, scale=None, accum=None, eng='act'):
        reads = [in_]
        kw = {}
        if bias is not None:
            if isinstance(bias, View):
                reads.append(bias)
                kw['bias'] = bias.ap
            else:
                kw['bias'] = bias
        if scale is not None:
            if isinstance(scale, View):
                reads.append(scale)
                kw['scale'] = scale.ap
            else:
                kw['scale'] = scale
        writes = [out]
        if accum is not None:
            kw['accum_out'] = accum.ap
            writes.append(accum)
        return self.op(eng, lambda e: e.activation(out.ap, in_.ap, func, **kw), reads=reads, writes=writes)

    def tt(self, out, in0, in1, op, eng='dve'):
        return self.op(eng, lambda e: e.tensor_tensor(out.ap, in0.ap, in1.ap, op),
                       reads=[in0, in1], writes=[out])

    def ts(self, out, in0, s1, op0, s2=None, op1=None, accum=None, eng='dve'):
        reads = [in0]
        a1 = s1
        a2 = s2
        if isinstance(s1, View):
            reads.append(s1)
            a1 = s1.ap
        if isinstance(s2, View):
            reads.append(s2)
            a2 = s2.ap
        writes = [out]
        kw = {}
        if op1 is not None:
            kw['op1'] = op1
        if accum is not None:
            kw['accum_out'] = accum.ap
            writes.append(accum)
        return self.op(eng, lambda e: e.tensor_scalar(out.ap, in0.ap, a1, a2, op0, **kw),
                       reads=reads, writes=writes)

    def stt(self, out, in0, scalar, in1, op0, op1, eng='dve'):
        reads = [in0, in1]
        sc = scalar
        if isinstance(scalar, View):
            reads.append(scalar)
            sc = scalar.ap
        return self.op(eng, lambda e: e.scalar_tensor_tensor(out.ap, in0.ap, sc, in1.ap, op0, op1),
                       reads=reads, writes=[out])

    def copy(self, out, in_, eng='dve'):
        if eng == 'act':
            return self.op(eng, lambda e: e.copy(out.ap, in_.ap), reads=[in_], writes=[out])
        return self.op(eng, lambda e: e.tensor_copy(out.ap, in_.ap), reads=[in_], writes=[out])

    def memset(self, out, val, eng='dve'):
        return self.op(eng, lambda e: e.memset(out.ap, val), reads=[], writes=[out])

    def reduce(self, out, in_, op, axis=None, eng='dve'):
        axis = axis or AX.X
        return self.op(eng, lambda e: e.tensor_reduce(out.ap, in_.ap, axis, op), reads=[in_], writes=[out])

    def recip(self, out, in_):
        return self.op('dve', lambda e: e.reciprocal(out.ap, in_.ap), reads=[in_], writes=[out])

    def dma(self, out, in_, q='sp', **kw):
        return self.op(q, lambda e: e.dma_start(out=out.ap, in_=in_.ap, **kw), reads=[in_], writes=[out], dma=True)

    def coll(self, kind, out, in_, name):
        groups = [list(range(8))]
        return self.op('pool', lambda e: e.collective_compute(kind, ALU.bypass, replica_groups=groups,
                                                              ins=[in_.ap.opt()], outs=[out.ap.opt()]),
                       reads=[in_], writes=[out], csem="cc_" + name)

    def emit(self, final_keys):
        toks = []
        for k in final_keys:
            for t in self.last_w.get(k, {}).values():
                toks.append((t[0], t[1], t[2]))
        self.wait_only('sp', toks)
        nc = self.nc
        with nc.Block() as block:
            def run(engname):
                def f(eng):
                    for waits, emit, semname, inc in self.ops[engname]:
                        for s, v in waits:
                            eng.wait_ge(self.semobj[s], v)
                        if emit is None:
                            continue
                        ins = emit(eng)
                        ins.then_inc(self.semobj[semname], inc)
                return f
            if self.ops['sp']:
                block.sync(run('sp'))
            if self.ops['pe']:
                block.tensor(run('pe'))
            if self.ops['act']:
                block.scalar(run('act'))
            if self.ops['dve']:
                block.vector(run('dve'))
            if self.ops['pool']:
                block.gpsimd(run('pool'))


EPS = 1e-6
NF = 2816
NCH = 44


class Ctx:
    pass


def load_w_bf16(P, C, wd, K, N, name, rowscale=None, n0=0):
    kc = K // 128
    wb = P.sb([128, kc, N], BF16, name)
    for c in range(kc):
        for a in range(0, N, 1024):
            b = min(N, a + 1024)
            stg = C.stg[C.stg_i % 2]
            q = 'sp' if C.stg_i % 2 == 0 else 'pool'
            C.stg_i += 1
            P.dma(stg[:, 0:b - a], wd[c * 128:(c + 1) * 128, n0 + a:n0 + b], q=q)
            if rowscale is not None:
                P.ts(wb[:, c, a:b], stg[:, 0:b - a], rowscale[:, c:c + 1], ALU.mult)
            else:
                P.copy(wb[:, c, a:b], stg[:, 0:b - a])
    return wb


def setup_common(P):
    C = Ctx()
    C.stg = [P.sb([128, 1024], F32, "stg0"), P.sb([128, 1024], F32, "stg1")]
    C.stg_i = 0
    identd = P.dram("ident", [128, 128], F32, "ExternalInput")
    idf = P.sb([128, 128], F32, "idf")
    C.identf = idf
    C.ident = P.sb([128, 128], BF16, "idb")
    P.dma(idf[:], identd[:])
    P.copy(C.ident[:], idf[:])
    C.sm = [P.sb([128, 64], F32, f"sm{i}") for i in range(4)]
    C.sm_i = 0
    return C


def rstd_from_ssq(P, out, ssq, n, tmp, eps=EPS):
    P.ts(tmp, ssq, 1.0 / n, ALU.mult, eps, ALU.add)
    P.act(tmp, tmp, AF.Sqrt)
    P.recip(out, tmp)


def norm_T(P, C, x, dstT, junk, xn):
    sm = C.sm[C.sm_i % 4]
    C.sm_i += 1
    P.act(junk, x, AF.Square, accum=sm[:, 0:1])
    rstd_from_ssq(P, sm[:, 2:3], sm[:, 0:1], 1024.0, sm[:, 1:2])
    P.ts(xn, x, sm[:, 2:3], ALU.mult)
    to_T(P, C, xn, dstT)


def to_T(P, C, xb, dstT, nchunk=8):
    pt = C.pT[C.pT_i % len(C.pT)]
    C.pT_i += 1
    for c in range(nchunk):
        P.transpose(pt[:, c * 128:(c + 1) * 128], xb[:, c * 128:(c + 1) * 128], C.ident[:])
    P.copy(dstT, pt[:, 0:nchunk * 128].re("p (c t) -> p c t", c=nchunk), eng='act')


EPS = 1e-6
NF = 2816
NCH = 44


class Ctx:
    pass


class Src:
    def __init__(self, key, fn):
        self.key = key
        self.fn = fn


_PV = {}


def pv(q, e, half=False):
    if ('pid', q) not in _PV:
        _PV[('pid', q)] = e.snap(e.partition_id())
    if half:
        if ('h', q) not in _PV:
            _PV[('h', q)] = e.snap(_PV[('pid', q)] // 2)
        return _PV[('h', q)]
    return _PV[('pid', q)]


def dma_in(P, dst, src, row0, n, q='sp', **kw):
    return P.op(q, lambda e: e.dma_start(out=dst.ap, in_=src.fn(e, row0, n), **kw), reads=[src.key], writes=[dst], dma=True)


def dma_out(P, dst, row0, n, srcv, q='sp'):
    return P.op(q, lambda e: e.dma_start(out=dst.fn(e, row0, n), in_=srcv.ap), reads=[srcv], writes=[dst.key], dma=True)


def rows_src(tile, base=0, c0=None, c1=None):
    if c0 is None:
        return Src(tile.key, lambda e, r, n: tile.h[base + r:base + r + n, :])
    return Src(tile.key, lambda e, r, n: tile.h[base + r:base + r + n, c0:c1])


def view2d(tile, nrows, ncols):
    return Tile(tile.h[0:nrows * ncols].rearrange("(r n) -> r n", n=ncols), tile.key, True)


def load_w_bf16(P, C, wd, K, N, name, rowscale=None, n0=0):
    kc = K // 128
    wb = P.sb([128, kc, N], BF16, name)
    for c in range(kc):
        for a in range(0, N, 1024):
            b = min(N, a + 1024)
            stg = C.stg[C.stg_i % 2]
            q = 'sp' if C.stg_i % 2 == 0 else 'pool'
            C.stg_i += 1
            P.dma(stg[:, 0:b - a], wd[c * 128:(c + 1) * 128, n0 + a:n0 + b], q=q)
            if rowscale is not None:
                P.ts(wb[:, c, a:b], stg[:, 0:b - a], rowscale[:, c:c + 1], ALU.mult)
            else:
                P.copy(wb[:, c, a:b], stg[:, 0:b - a])
    return wb


def phase_common(P, G):
    C = Ctx()
    C.stg = [P.sb([128, 1024], F32, "stg0"), P.sb([128, 1024], F32, "stg1")]
    C.stg_i = 0
    C.identf = P.sb([128, 128], F32, "idf")
    C.ident = P.sb([128, 128], BF16, "idb")
    P.dma(C.identf[:], G.ident[:])
    P.copy(C.ident[:], C.identf[:])
    C.sm = [P.sb([128, 64], F32, f"sm{i}") for i in range(4)]
    C.sm_i = 0
    C.pT = [P.ps([128, 1024], BF16, "pT0")]
    C.pT_i = 0
    return C


def rstd_from_ssq(P, out, ssq, n, tmp, eps=EPS):
    P.ts(tmp, ssq, 1.0 / n, ALU.mult, eps, ALU.add)
    P.act(tmp, tmp, AF.Sqrt)
    P.recip(out, tmp)


def norm_T(P, C, x, dstT, junk, xn):
    sm = C.sm[C.sm_i % 4]
    C.sm_i += 1
    P.act(junk, x, AF.Square, accum=sm[:, 0:1])
    rstd_from_ssq(P, sm[:, 2:3], sm[:, 0:1], 1024.0, sm[:, 1:2])
    P.ts(xn, x, sm[:, 2:3], ALU.mult)
    to_T(P, C, xn, dstT)


def to_T(P, C, xb, dstT, nchunk=8):
    pt = C.pT[C.pT_i % len(C.pT)]
    C.pT_i += 1
    for c in range(nchunk):
        P.transpose(pt[:, c * 128:(c + 1) * 128], xb[:, c * 128:(c + 1) * 128], C.ident[:])
    P.copy(dstT, pt[:, 0:nchunk * 128].re("p (c t) -> p c t", c=nchunk), eng='act')


def zero_rows(P, dst_tile_rows_view, ncols, zt):
    for a in range(0, ncols, 2048):
        b = min(ncols, a + 2048)
        P.dma(dst_tile_rows_view[:, a:b], zt[:, 0:b - a], q='sp')


def ph_xtail(P, G, cur, Tn):
    t = P.sb([128, 1024], F32, "xtl")
    P.dma(t[:], cur[Tn:Tn + 128, :])
    P.dma(G.xt_loc[:], t[:])
    P.coll("AllGather", G.G_xt[128:128 * 9, :], G.xt_loc[:], "xt")
    src = Src(G.G_xt.key, lambda e, r, n: G.G_xt.h[bass.ds(pv('sp', e) * 128, 128), :])
    P.op('sp', lambda e: e.dma_start(out=cur[0:128, :].ap, in_=src.fn(e, 0, 128)),
         reads=[G.G_xt.key], writes=[cur[0:128, :]], dma=True)


def blk3(tile, NB, T):
    return Tile(tile.h[0:NB * T * 128].rearrange("(j t d) -> j t d", j=NB, d=128), tile.key, True)


def gather_y(P, G, NB, Tn, parts=None):
    parts = parts or [(G.GY, 0, NB)]
    for (gy, j0, nb) in parts:
        n = nb * Tn
        P.coll("AllGather", Tile(gy.h[0:8 * n * 128].rearrange("(r d) -> r d", d=128), gy.key, True)[:, :],
               Tile(G.Yloc.h[j0 * Tn * 128:(j0 * Tn + n) * 128].rearrange("(r d) -> r d", d=128), G.Yloc.key, True)[:, :],
               "gy")
    P.coll("AllGather", G.G_T[128:128 * 9, :], G.T_loc[:, :], "gt")
    P.op('sp', lambda e: e.dma_start(out=G.Lt[:, :].ap, in_=G.G_T.h[bass.ds(pv('sp', e) * 128, 128), :]),
         reads=[G.G_T.key], writes=[G.Lt.key], dma=True)


def slab_copy(P, G, k, NB, Tn, blockfn, q, gy=None):
    gy = gy or G.GY
    x = Tn * 128
    bsz = min(x, 8192)
    src3 = gy.h[0:8 * NB * x].rearrange("(r j a b) -> r j a b", r=8, j=NB, b=bsz)
    dst3 = G.slab[k].h[0:8 * x].rearrange("(r o a b) -> r o a b", r=8, o=1, b=bsz)
    P.op(q, lambda e: e.dma_start(out=dst3, in_=src3[:, bass.ds(blockfn(q, e), 1), :, :]),
         reads=[gy.key], writes=[G.slab[k].key], dma=True)


def ph_front(P, G, cur, W, n1, N, Tn, S, tail_cols):
    C = phase_common(P, G)
    ph_xtail(P, G, cur, Tn)
    NBf = N // 128
    rem = N - NBf * 128
    NB = NBf + (1 if rem else 0)
    Yl3 = blk3(G.Yloc, NB, Tn)
    n1s = P.sb([128, 8], F32, "n1s")
    P.dma(n1s[:], n1[:])
    wb = load_w_bf16(P, C, W, 1024, N, "wb", rowscale=n1s)
    py = [P.ps([128, 512], F32, f"py{i}") for i in range(4)]
    xt = [P.sb([128, 1024], F32, f"xt{i}") for i in range(2)]
    junk = P.sb([128, 1024], F32, "junk")
    xn = P.sb([128, 1024], BF16, "xn")
    xT = [P.sb([128, 8, 128], BF16, f"xT{i}") for i in range(2)]
    ysb = [P.sb([128, NB * 128], F32, f"ysb{i}") for i in range(2)]
    if rem:
        for b in range(2):
            P.memset(ysb[b][:, NBf * 128:NB * 128], 0.0)
    k = 0
    nt = Tn // 128
    for t in range(nt):
        b = t % 2
        P.dma(xt[b][:], cur[128 + t * 128:128 + (t + 1) * 128, :], q='sp')
        norm_T(P, C, xt[b][:], xT[b][:], junk[:], xn[:])
        for n0 in range(0, N, 512):
            n1_ = min(N, n0 + 512)
            pp = py[k % 4]
            for c in range(8):
                P.mm(pp[:, 0:n1_ - n0], xT[b][:, c, :], wb[:, c, n0:n1_], start=(c == 0), stop=(c == 7))
            P.copy(ysb[b][:, n0:n1_], pp[:, 0:n1_ - n0], eng='act' if k % 2 == 0 else 'dve')
            k += 1
        P.dma(View(Yl3.h[:, t * 128:(t + 1) * 128, :].rearrange("j t d -> t j d"), Yl3.key),
              ysb[b][:].re("p (j d) -> p j d", d=128), q='pool')
        if t == nt - 1 and tail_cols is not None:
            P.dma(G.T_loc[:, 0:1024], ysb[b][:, tail_cols:tail_cols + 1024], q='sp')
    gather_y(P, G, NB, Tn)
    return NB


RW_N = 3328


def ph_front_rwkv(P, G, cur, Wd, Tn, S):
    C = phase_common(P, G)
    ph_xtail(P, G, cur, Tn)
    N = RW_N
    NB = 64
    Yl3 = blk3(G.Yloc, NB, Tn)
    nt = Tn // 128
    rown = ['w0', 'a0', 'k_k', 'k_a', 'r_k']
    rows = {}
    for n in rown:
        rows[n] = P.sb([128, 1024], F32, "row_" + n)
        P.dma(rows[n][:], Wd[n][:])
    n1s = P.sb([128, 8], F32, "n1s")
    mus = P.sb([128, 6, 8], F32, "mus")
    s1 = P.sb([128, 6, 8], F32, "s1")
    s2 = P.sb([128, 6, 8], F32, "s2")
    P.dma(n1s[:], Wd['n1'][:])
    P.dma(mus[:], Wd['mu'][:].re("p (j c) -> p j c", j=6))
    P.tt(s2[:], mus[:], n1s[:].bc(1, [128, 6, 8]), ALU.mult)
    P.tt(s1[:], n1s[:].bc(1, [128, 6, 8]), s2[:], ALU.subtract)
    wb = P.sb([128, 16, N], BF16, "wb")
    W = Wd['W']
    blocks = [(0, 1024, 0), (1024, 2048, 2), (2048, 3072, 3), (3072, 3136, 1), (3136, 3200, 4), (3200, 3328, 5)]
    for c in range(8):
        for (a, b_, j) in blocks:
            stg = C.stg[C.stg_i % 2]
            q = 'sp' if C.stg_i % 2 == 0 else 'pool'
            C.stg_i += 1
            P.dma(stg[:, 0:b_ - a], W[c * 128:(c + 1) * 128, a:b_], q=q)
            P.ts(wb[:, c, a:b_], stg[:, 0:b_ - a], s1[:, j, c:c + 1], ALU.mult)
            P.ts(wb[:, 8 + c, a:b_], stg[:, 0:b_ - a], s2[:, j, c:c + 1], ALU.mult)
    w2a2b = P.sb([128, 1024], BF16, "w2a2b")
    g2b = P.sb([128, 1024], BF16, "g2b")
    for (src, dst) in ((Wd['w2a2'], w2a2b), (Wd['g2'], g2b)):
        stg = C.stg[C.stg_i % 2]
        C.stg_i += 1
        P.dma(stg[:], src[:])
        P.copy(dst[:], stg[:])
    py = [P.ps([128, 512], F32, f"py{i}") for i in range(2)]
    pz = [P.ps([128, 1024], F32, f"pz{i}") for i in range(2)]
    xt = [P.sb([128, 1024], F32, f"xt{i}") for i in range(2)]
    junk = P.sb([128, 1024], F32, "junk")
    xn = P.sb([128, 1024], BF16, "xn")
    xT = P.sb([128, 16, 128], BF16, "xT")
    ysb = P.sb([128, N], F32, "ysb")
    hb = P.sb([128, 256], BF16, "hb")
    hbT = P.sb([128, 2, 128], BF16, "hbT")
    ob = [P.sb([128, 1024], F32, f"ob{i}") for i in range(6)]
    tA = P.sb([128, 1024], F32, "tA")
    tB = P.sb([128, 1024], F32, "tB")
    kq = 0
    oi = [0]

    def emit_out(t, slot, view):
        P.dma(View(Yl3.h[slot * 8:(slot + 1) * 8, t * 128:(t + 1) * 128, :].rearrange("j t d -> t j d"), Yl3.key),
              view.re("p (j d) -> p j d", d=128), q='pool' if slot % 2 == 0 else 'sp')
        if t == nt - 1 and slot in (6, 7):
            P.dma(G.T_loc[:, (slot - 5) * 1024:(slot - 4) * 1024], view, q='sp')

    def nxt():
        o = ob[oi[0] % 6]
        oi[0] += 1
        return o

    for t in range(Tn // 128):
        P.dma(xt[0][:], cur[128 + t * 128:128 + (t + 1) * 128, :], q='sp')
        P.dma(xt[1][:], cur[127 + t * 128:127 + (t + 1) * 128, :], q='pool')
        norm_T(P, C, xt[0][:], xT[:, 0:8, :], junk[:], xn[:])
        norm_T(P, C, xt[1][:], xT[:, 8:16, :], junk[:], xn[:])
        for n0 in range(0, N, 512):
            n1_ = min(N, n0 + 512)
            pp = py[kq % 2]
            for c in range(16):
                P.mm(pp[:, 0:n1_ - n0], xT[:, c, :], wb[:, c, n0:n1_], start=(c == 0), stop=(c == 15))
            P.copy(ysb[:, n0:n1_], pp[:, 0:n1_ - n0], eng='act' if kq % 2 == 0 else 'dve')
            kq += 1
        r = ysb[:, 0:1024]
        kx = ysb[:, 1024:2048]
        v = ysb[:, 2048:3072]
        emit_out(t, 0, r)
        emit_out(t, 3, v)
        P.act(hb[:, 0:64], ysb[:, 3072:3136], AF.Tanh)
        P.copy(hb[:, 64:128], ysb[:, 3136:3200])
        P.act(hb[:, 128:256], ysb[:, 3200:3328], AF.Sigmoid)
        to_T(P, C, hb[:], hbT[:], nchunk=2)
        sm = C.sm[C.sm_i % 4]
        C.sm_i += 1
        for hf in range(2):
            P.mm(pz[0][:, hf * 512:(hf + 1) * 512], hbT[0:64, 0, :], w2a2b[0:64, hf * 512:(hf + 1) * 512])
        P.tt(tA[:], pz[0][:], rows['w0'][:], ALU.add)
        P.act(tA[:], tA[:], AF.Sigmoid)
        o_w = nxt()
        P.act(o_w[:], tA[:], AF.Exp, scale=-float(np.exp(-0.5)))
        emit_out(t, 1, o_w[:])
        for hf in range(2):
            P.mm(pz[1][:, hf * 512:(hf + 1) * 512], hbT[64:128, 0, :], w2a2b[64:128, hf * 512:(hf + 1) * 512])
        P.tt(tA[:], pz[1][:], rows['a0'][:], ALU.add)
        P.act(tA[:], tA[:], AF.Sigmoid)
        for hf in range(2):
            P.mm(pz[0][:, hf * 512:(hf + 1) * 512], hbT[:, 1, :], g2b[:, hf * 512:(hf + 1) * 512])
        o_g = nxt()
        P.copy(o_g[:], pz[0][:], eng='act')
        emit_out(t, 6, o_g[:])
        o_kk = nxt()
        P.tt(o_kk[:], kx, rows['k_k'][:], ALU.mult)
        P.tt(junk[:], o_kk[:], o_kk[:], ALU.mult)
        P.reduce(sm[:, 0:16], junk[:].re("p (h n) -> p h n", h=16), ALU.add)
        P.act(sm[:, 0:16], sm[:, 0:16], AF.Sqrt)
        P.ts(sm[:, 0:16], sm[:, 0:16], 1e-12, ALU.max)
        P.recip(sm[:, 16:32], sm[:, 0:16])
        kk3 = o_kk[:].re("p (h n) -> p h n", h=16)
        P.tt(kk3, kk3, sm[:, 16:32].bc(2, [128, 16, 64]), ALU.mult)
        emit_out(t, 4, o_kk[:])
        o_b = nxt()
        P.tt(o_b[:], o_kk[:], tA[:], ALU.mult)
        emit_out(t, 5, o_b[:])
        P.stt(tB[:], tA[:], -1.0, rows['k_a'][:], ALU.add, ALU.mult)
        o_k = nxt()
        P.stt(o_k[:], tB[:], 1.0, kx, ALU.add, ALU.mult)
        emit_out(t, 2, o_k[:])
        P.tt(tB[:], r, o_k[:], ALU.mult)
        P.tt(tB[:], tB[:], rows['r_k'][:], ALU.mult)
        P.reduce(sm[:, 32:48], tB[:].re("p (h n) -> p h n", h=16), ALU.add)
        o_bo = nxt()
        P.tt(o_bo[:].re("p (h n) -> p h n", h=16), v.re("p (h n) -> p h n", h=16),
             sm[:, 32:48].bc(2, [128, 16, 64]), ALU.mult)
        emit_out(t, 7, o_bo[:])
    gather_y(P, G, NB, Tn, parts=[(G.GY, 0, 24), (G.GY2, 24, 24)])
    return NB


def gy_block_src(G, NB, Tn, j, c0, c1):
    def fn(e, r, n):
        rank, loc = r // Tn, r % Tn
        off = ((rank * NB + j) * Tn + loc) * 128
        return G.GY.h[off:off + n * 128].rearrange("(t d) -> t d", d=128)[:, c0:c1]
    return Src(G.GY.key, fn)


def slab_rows(G, k, S, c0=0, c1=128):
    v = G.slab[k].h[0:S * 128].rearrange("(t d) -> t d", d=128)
    return Src(G.slab[k].key, lambda e, r, n: v[r:r + n, c0:c1])


def ph_lin_core(P, G, NB, Tn, Wd, S, NV, kind):
    C = phase_common(P, G)
    gs = 1.0 / 16 if kind == 'gla' else 1.0
    qscale = 128.0 ** -0.5
    SP = S + 128
    ol = view2d(G.oloc, SP, NV)
    oa = view2d(G.oall, 8 * SP, NV)
    qq = 'act' if kind == 'gla' else 'pool'
    slab_copy(P, G, 0, NB, Tn, lambda q, e: pv(q, e, True), qq)
    slab_copy(P, G, 1, NB, Tn, lambda q, e: pv(q, e, True) + 4, qq)
    slab_copy(P, G, 2, NB, Tn, lambda q, e: pv(q, e) + 8, qq)
    qsrc = slab_rows(G, 0, S)
    ksrc = slab_rows(G, 1, S)
    vsrc = slab_rows(G, 2, S)
    tris = P.sb([128, 128], F32, "tris")
    sus = P.sb([128, 128], F32, "sus")
    P.dma(tris[:], G.tri[:])
    P.dma(sus[:], G.su[:])
    onec = P.sb([128, 1], F32, "onec")
    P.memset(onec[:], 1.0)
    zt = P.sb([128, 2048], F32, "zt")
    P.memset(zt[:], 0.0)
    zero_rows(P, ol[0:128, :], NV, zt)
    if kind == 'gla':
        asrc = gy_block_src(G, NB, Tn, 24, 0, 16)
        waus = P.sb([16, 128], F32, "waus")
        bals = P.sb([1, 128], F32, "bals")
        oner = P.sb([1, 128], F32, "oner")
        P.dma(waus[:], Wd['wau'][:])
        P.dma(bals[:], Wd['bal'][:])
        P.memset(oner[:], 1.0)
        alt = [P.sb([128, 16], F32, f"alt{i}") for i in range(2)]
        alT = P.sb([16, 128], F32, "alT")
        pal = P.ps([128, 512], F32, "pal")
    else:
        gsrc = gy_block_src(G, NB, Tn, 24, 0, 8)
        ohs = P.sb([128, 16], F32, "ohs")
        P.dma(ohs[:], Wd['oh'][:])
        g8 = [P.sb([128, 8], F32, f"g8{i}") for i in range(2)]
        t8 = P.sb([128, 8], F32, "t8")
        bifs = P.sb([128, 2], F32, "bifs")
        P.dma(bifs[:], Wd['bif'][:])
        gt = [P.sb([128, 2], F32, f"gt{i}") for i in range(2)]
        igc = [P.sb([128, 1], F32, f"igc{i}") for i in range(2)]
        lc = P.sb([128, 1], F32, "lc")
    pz = P.ps([128, 512], F32, "pz")
    psc = P.ps([128, 512], F32, "psc")
    po = [P.ps([128, 512], F32, f"po{i}") for i in range(2)]
    pS = P.ps([128, 512], F32, "pS")
    pe = P.ps([128, 512], F32, "pe")
    qt = [P.sb([128, 128], F32, f"qt{i}") for i in range(2)]
    kt = [P.sb([128, 128], F32, f"kt{i}") for i in range(2)]
    vt = [P.sb([128, 129], F32, f"vt{i}") for i in range(2)]
    vb = P.sb([128, NV], BF16, "vb")
    L = P.sb([128, 128], F32, "L")
    eq = P.sb([128, 128], F32, "eq")
    ek = P.sb([128, 128], F32, "ek")
    ee = P.sb([128, 128], F32, "ee")
    qk = P.sb([128, 256], BF16, "qk")
    qkT = P.sb([128, 2, 128], BF16, "qkT")
    ke = P.sb([128, 128], BF16, "ke")
    scT = P.sb([128, 128], BF16, "scT")
    Sf = P.sb([128, NV], F32, "Sf")
    Sb = P.sb([128, NV], BF16, "Sb")
    el = P.sb([128, 1], F32, "el")
    ot = [P.sb([128, NV], F32, f"ot{i}") for i in range(2)]
    P.memset(Sf[:], 0.0)
    P.memset(Sb[:], 0.0)
    if NV == 129:
        for b in range(2):
            P.memset(vt[b][:, 128:129], 1.0)
    for t in range(S // 128):
        b = t % 2
        r0 = t * 128
        dma_in(P, qt[b][:], qsrc, r0, 128, 'sp')
        dma_in(P, kt[b][:], ksrc, r0, 128, 'pool')
        dma_in(P, vt[b][:, 0:128], vsrc, r0, 128, 'sp')
        P.copy(vb[:], vt[b][:, 0:NV], eng='pool')
        ig = None
        if kind == 'gla':
            dma_in(P, alt[b][:], asrc, r0, 128, 'pool')
            P.transpose(pal[0:16, 0:128], alt[b][:], C.identf[:])
            P.copy(alT[:], pal[0:16, 0:128], eng='act')
            P.mm(pz[:, 0:128], alT[:], waus[:], start=True, stop=False)
            P.mm(pz[:, 0:128], oner[:], bals[:], start=False, stop=True)
            P.act(L[:], pz[:, 0:128], AF.Exp, scale=-1.0)
            P.act(L[:], L[:], AF.Ln, bias=onec[:])
        else:
            dma_in(P, g8[b][:], gsrc, r0, 128, 'pool')
            P.tt(t8[:], g8[b][:], ohs[:, 0:8], ALU.mult)
            P.reduce(gt[b][:, 0:1], t8[:], ALU.add)
            P.tt(t8[:], g8[b][:], ohs[:, 8:16], ALU.mult)
            P.reduce(gt[b][:, 1:2], t8[:], ALU.add)
            P.tt(igc[b][:], gt[b][:, 0:1], bifs[:, 0:1], ALU.add)
            ig = igc[b]
            P.tt(lc[:], gt[b][:, 1:2], bifs[:, 1:2], ALU.add)
            P.act(lc[:], lc[:], AF.Exp, scale=-1.0)
            P.act(lc[:], lc[:], AF.Ln, bias=onec[:])
            P.copy(L[:], View(lc[:].ap.to_broadcast([128, 128]), lc.key))
        P.mm(pz[:, 128:256], tris[:], L[:])
        P.mm(pz[:, 256:384], sus[:], L[:])
        P.mm(pe[:, 0:1], L[:], onec[:])
        P.act(eq[:], pz[:, 128:256], AF.Exp, scale=-gs)
        if ig is not None:
            P.act(ek[:], pz[:, 128:256], AF.Exp, scale=gs, bias=ig[:])
            P.act(ee[:], pz[:, 256:384], AF.Exp, scale=-gs, bias=ig[:])
        else:
            P.act(ek[:], pz[:, 128:256], AF.Exp, scale=gs)
            P.act(ee[:], pz[:, 256:384], AF.Exp, scale=-gs)
        P.act(el[:], pe[:, 0:1], AF.Exp, scale=-gs)
        P.stt(qk[:, 0:128], qt[b][:], qscale, eq[:], ALU.mult, ALU.mult)
        P.tt(qk[:, 128:256], kt[b][:], ek[:], ALU.mult)
        P.tt(ke[:], kt[b][:], ee[:], ALU.mult)
        to_T(P, C, qk[:], qkT[:], nchunk=2)
        P.mm(psc[:, 0:128], qkT[:, 1, :], qkT[:, 0, :])
        P.tt(scT[:], psc[:, 0:128], tris[:], ALU.mult)
        pp = po[b]
        P.mm(pp[:, 0:NV], scT[:], vb[:], start=True, stop=False)
        P.mm(pp[:, 0:NV], qkT[:, 0, :], Sb[:], start=False, stop=True)
        P.copy(ot[b][:], pp[:, 0:NV], eng='act')
        P.dma(ol[128 + r0:128 + r0 + 128, :], ot[b][:], q='sp')
        P.mm(pS[:, 0:NV], ke[:], vb[:])
        P.stt(Sf[:], Sf[:], el[:], pS[:, 0:NV], ALU.mult, ALU.add)
        P.copy(Sb[:], Sf[:])
    P.coll("AllGather", oa[:, :], ol[:, :], "oa")


def ph_rwkv_core(P, G, NB, Tn, S):
    C = phase_common(P, G)
    nblk = S // 256
    SP = S + 128
    ol = view2d(G.oloc, SP, 128)
    oa = view2d(G.oall, 8 * SP, 128)
    sels = P.sb([128, 64, 128], F32, "sels")
    for i in range(4):
        P.dma(sels[:, i * 16:(i + 1) * 16, :], G.sel[:, i * 2048:(i + 1) * 2048].re("p (j m) -> p j m", j=16),
              q='sp' if i % 2 == 0 else 'pool')
    zt = P.sb([128, 2048], F32, "zt")
    P.memset(zt[:], 0.0)
    zero_rows(P, ol[0:128, :], 128, zt)
    CH = min(S, 2048)
    vch = [P.sb([128, CH], F32, f"vch{i}") for i in range(2)]
    ych = [P.sb([128, CH], F32, f"ych{i}") for i in range(2)]
    vtl = [P.sb([128, 128], F32, f"vtl{i}") for i in range(2)]
    otl = [P.sb([128, 128], F32, f"otl{i}") for i in range(2)]
    St = P.sb([128, 64], F32, "St")
    tmp = P.sb([128, 64], F32, "tmp")
    sa = P.sb([128, 1], F32, "sa")
    P.memset(St[:], 0.0)
    xb = [P.sb([128, 5, 256], F32, f"xb{i}") for i in range(2)]
    pA = [P.ps([128, 512], F32, f"pA{i}") for i in range(2)]
    pB = [P.ps([128, 512], F32, f"pB{i}") for i in range(2)]
    pC = [P.ps([128, 512], F32, f"pC{i}") for i in range(2)]
    pV = P.ps([128, 512], F32, "pV")
    slots = [4, 1, 5, 2, 0]
    for sl in range(6):
        slab_copy(P, G, sl, 24, Tn, (lambda q, e, sl=sl: pv(q, e) + (sl % 3) * 8), 'pool',
                  gy=(G.GY if sl < 3 else G.GY2))
    vsrc = slab_rows(G, 3, S)
    dcount = 0
    for blk in range(nblk):
        ci = (blk * 256) // CH
        cb = ci % 2
        if (blk * 256) % CH == 0:
            for tt_ in range(CH // 128):
                vb_ = vtl[tt_ % 2]
                dma_in(P, vb_[:], vsrc, ci * CH + tt_ * 128, 128, 'pool')
                P.transpose(pV[:, (tt_ % 4) * 128:(tt_ % 4 + 1) * 128], vb_[:], C.identf[:])
                P.copy(vch[cb][:, tt_ * 128:(tt_ + 1) * 128], pV[:, (tt_ % 4) * 128:(tt_ % 4 + 1) * 128], eng='act')
        xbb = xb[blk % 2]
        for s_, slot in enumerate(slots):
            for h in range(2):
                sv = G.slab[slot].h[0:S * 128].rearrange("(t d) -> t d", d=128)
                src = Src(G.slab[slot].key, lambda e, r, n, sv=sv, h=h: sv[r:r + n, h * 64:(h + 1) * 64]
                          .rearrange("(j t) k -> j t k", t=4))
                dstv = View(xbb[h * 64:(h + 1) * 64, s_, :].ap.rearrange("j (t k) -> j t k", t=4), xbb.key)
                dma_in(P, dstv, src, blk * 256, 256, 'sp' if dcount % 2 == 0 else 'pool')
                dcount += 1
        for j in range(64):
            pb = j % 2
            P.mm(pA[pb][:, 0:256], sels[:, j, :], xbb[:, 0, :])
            P.mm(pA[pb][:, 256:512], sels[:, j, :], xbb[:, 1, :])
            P.mm(pB[pb][:, 0:256], sels[:, j, :], xbb[:, 2, :])
            P.mm(pB[pb][:, 256:512], sels[:, j, :], xbb[:, 3, :])
            P.mm(pC[pb][:, 0:256], sels[:, j, :], xbb[:, 4, :])
            for tq in range(4):
                tl = (blk * 256) % CH + j * 4 + tq
                a, b_ = tq * 64, (tq + 1) * 64
                P.op('dve', (lambda e, a=a, b_=b_, pb=pb: e.scalar_tensor_tensor(
                    tmp[:].ap, St[:].ap, -1.0, pA[pb][:, a:b_].ap, ALU.mult, ALU.mult, accum_out=sa[:].ap)),
                    reads=[St[:], pA[pb][:]], writes=[tmp[:], sa[:]])
                P.tt(St[:], St[:], pA[pb][:, 256 + a:256 + b_], ALU.mult)
                P.stt(St[:], pB[pb][:, a:b_], sa[:], St[:], ALU.mult, ALU.add)
                P.stt(St[:], pB[pb][:, 256 + a:256 + b_], vch[cb][:, tl:tl + 1], St[:], ALU.mult, ALU.add)
                P.op('dve', (lambda e, a=a, b_=b_, pb=pb, cb=cb, tl=tl: e.scalar_tensor_tensor(
                    tmp[:].ap, St[:].ap, 1.0, pC[pb][:, a:b_].ap, ALU.mult, ALU.mult,
                    accum_out=ych[cb][:, tl:tl + 1].ap)),
                    reads=[St[:], pC[pb][:]], writes=[tmp[:], ych[cb][:]])
        if (blk * 256 + 256) % CH == 0:
            for tt_ in range(CH // 128):
                P.transpose(pV[:, (tt_ % 4) * 128:(tt_ % 4 + 1) * 128], ych[cb][:, tt_ * 128:(tt_ + 1) * 128], C.identf[:])
                ob_ = otl[tt_ % 2]
                P.copy(ob_[:], pV[:, (tt_ % 4) * 128:(tt_ % 4 + 1) * 128], eng='act')
                r0 = ci * CH + tt_ * 128
                P.dma(ol[128 + r0:128 + r0 + 128, :], ob_[:], q='pool')
    P.coll("AllGather", oa[:, :], ol[:, :], "oa")


SB_SKEW = (1, 2)


def ph_sb_core(P, G, NB, Tn, S):
    C = phase_common(P, G)
    nb = S // 128
    SP = S + 128
    ol = view2d(G.oloc, SP, 128)
    oa = view2d(G.oall, 8 * SP, 128)
    qb = P.sb([64, 2, S], BF16, "qb")
    kb = P.sb([64, 2, S], BF16, "kb")
    vb = P.sb([128, 2, nb, 64], BF16, "vb")
    zt = P.sb([128, 128], F32, "zt")
    P.memset(zt[:], 0.0)
    zero_rows(P, ol[0:128, :], 128, zt)
    for j in range(3):
        slab_copy(P, G, j, NB, Tn, (lambda q, e, j=j: pv(q, e) + j * 8), 'sp')
    srcs = [slab_rows(G, j, S) for j in range(3)]
    ld = [P.sb([128, 3, 128], F32, f"ld{i}") for i in range(2)]
    ldb = P.sb([128, 256], BF16, "ldb")
    for t in range(nb):
        l = ld[t % 2]
        for j in range(3):
            dma_in(P, l[:, j, :], srcs[j], t * 128, 128, 'sp' if j % 2 == 0 else 'pool')
        P.copy(ldb[:], l[:, 0:2, :].re("p a d -> p (a d)"))
        pt = C.pT[0]
        for j in range(4):
            P.transpose(pt[0:64, j * 128:(j + 1) * 128], ldb[:, j * 64:(j + 1) * 64], C.ident[:])
        P.copy(qb[:, :, t * 128:(t + 1) * 128], pt[0:64, 0:256].re("p (h t) -> p h t", h=2), eng='act')
        P.copy(kb[:, :, t * 128:(t + 1) * 128], pt[0:64, 256:512].re("p (h t) -> p h t", h=2), eng='act')
        P.copy(vb[:, :, t, :], l[:, 2, :].re("p (h d) -> p h d", h=2), eng='pool')
    mf = P.sb([128, 128], F32, "mf")
    uf = P.sb([128, 128], F32, "uf")
    P.dma(mf[:], G.mstr[:])
    P.dma(uf[:], G.umat[:])
    mb = P.sb([128, 128], BF16, "mb")
    ub = P.sb([128, 128], BF16, "ub")
    onesb = P.sb([128, 128], BF16, "onesb")
    onec = P.sb([128, 1], F32, "onec")
    P.copy(mb[:], mf[:])
    P.copy(ub[:], uf[:])
    P.memset(onesb[:], 1.0)
    P.memset(onec[:], 1.0)
    pz = [P.ps([128, 512], F32, f"pz{i}") for i in range(2)]
    psuf = [P.ps([128, 512], F32, f"psuf{i}") for i in range(2)]
    pcb = P.ps([128, 512], F32, "pcb")
    po = [P.ps([128, 512], F32, f"po{i}") for i in range(2)]
    ND = 3
    e_ = [P.sb([128, 512], F32, f"e{i}") for i in range(ND)]
    sp_ = [P.sb([128, 512], BF16, f"sp{i}") for i in range(ND)]
    sfx = [P.sb([128, 512], F32, f"sfx{i}") for i in range(ND)]
    w_ = [P.sb([128, 512], BF16, f"w{i}") for i in range(ND)]
    CB = P.sb([128, 128], F32, "CB")
    ot = [P.sb([128, 128], F32, f"ot{i}") for i in range(2)]
    glist = []
    for i in range(nb):
        for h in range(2):
            blocks = list(range(i, -1, -1))
            groups = [blocks[a_:a_ + 4] for a_ in range(0, len(blocks), 4)]
            off = 0
            for g_i, grp in enumerate(groups):
                glist.append((i, h, g_i, grp, len(groups), off, len(blocks)))
                off += len(grp)

    def stage_a(gidx):
        i, h, g_i, grp, ng, off, nblk = glist[gidx]
        n = len(grp)
        W = n * 128
        b2 = gidx % 2
        bd = gidx % ND
        qs = qb[:, h, i * 128:(i + 1) * 128]
        for b, jb in enumerate(grp):
            P.mm(pz[b2][:, b * 128:(b + 1) * 128], kb[:, h, jb * 128:(jb + 1) * 128], qs)
        P.act(e_[bd][:, 0:W], pz[b2][:, 0:W], AF.Exp, scale=0.125)
        P.act(sp_[bd][:, 0:W], e_[bd][:, 0:W], AF.Ln, bias=onec[:])
        if g_i == 0:
            P.tt(sp_[bd][:, 0:128], sp_[bd][:, 0:128], mb[:], ALU.mult)

    def stage_b(gidx):
        i, h, g_i, grp, ng, off, nblk = glist[gidx]
        n = len(grp)
        W = n * 128
        b2 = gidx % 2
        bd = gidx % ND
        for b in range(n):
            P.mm(psuf[b2][:, b * 128:(b + 1) * 128], ub[:], sp_[bd][:, b * 128:(b + 1) * 128],
                 start=True, stop=(b == 0))
            for b_ in range(b):
                P.mm(psuf[b2][:, b * 128:(b + 1) * 128], onesb[:], sp_[bd][:, b_ * 128:(b_ + 1) * 128],
                     start=False, stop=(b_ == b - 1))
        last_g = (g_i == ng - 1)
        if not last_g:
            for b in range(n):
                P.mm(pcb[:, 0:128], onesb[:], sp_[bd][:, b * 128:(b + 1) * 128],
                     start=(b == 0), stop=(b == n - 1))
        if g_i == 0:
            P.copy(sfx[bd][:, 0:W], psuf[b2][:, 0:W])
        else:
            P.tt(sfx[bd][:, 0:W].re("p (b t) -> p b t", t=128), psuf[b2][:, 0:W].re("p (b t) -> p b t", t=128),
                 CB[:].bc(1, [128, n, 128]), ALU.add)
        if not last_g:
            if g_i == 0:
                P.copy(CB[:], pcb[:, 0:128])
            else:
                P.tt(CB[:], CB[:], pcb[:, 0:128], ALU.add)
        P.act(sfx[bd][:, 0:W], sfx[bd][:, 0:W], AF.Exp, scale=-1.0)
        P.tt(w_[bd][:, 0:W], e_[bd][:, 0:W], sfx[bd][:, 0:W], ALU.mult)
        if g_i == 0:
            P.tt(w_[bd][:, 0:128], w_[bd][:, 0:128], mb[:], ALU.mult)

    def stage_c(gidx):
        i, h, g_i, grp, ng, off, nblk = glist[gidx]
        bd = gidx % ND
        pp = po[h]
        for b, jb in enumerate(grp):
            P.mm(pp[:, 0:64], w_[bd][:, b * 128:(b + 1) * 128], vb[:, h, jb, :],
                 start=(off + b == 0), stop=(off + b == nblk - 1))
        if g_i == ng - 1:
            otile = ot[i % 2]
            P.copy(otile[:, h * 64:(h + 1) * 64], pp[:, 0:64], eng='act')
            if h == 1:
                P.dma(ol[128 + i * 128:128 + (i + 1) * 128, :], otile[:], q='sp' if i % 2 == 0 else 'pool')

    NG = len(glist)
    SK1, SK2 = SB_SKEW
    for s_ in range(NG + SK2):
        if s_ < NG:
            stage_a(s_)
        if 0 <= s_ - SK1 < NG:
            stage_b(s_ - SK1)
        if 0 <= s_ - SK2 < NG:
            stage_c(s_ - SK2)
    P.coll("AllGather", oa[:, :], ol[:, :], "oa")


def ph_back(P, G, NB, cur, dst, dst_base, Wd, variant, last, Tn, S):
    C = phase_common(P, G)
    TT = Tn + 128
    SW = 512
    SP = S + 128
    NV = 129 if variant == 'ml' else 128
    names = {'sb': ['o'], 'gla': ['o', 'r'], 'ml': ['o', 'r'], 'rwkv': ['o', 'bonus', 'g']}[variant]
    g16 = 16 * NV
    src_o = G.oall.h[0:8 * SP * NV].rearrange("(r t d) -> r t d", r=8, d=g16)
    dst_o = G.Lo.h[0:8 * TT * NV].rearrange("(r t d) -> r t d", r=8, d=g16)
    P.op('act', lambda e: e.dma_start(out=dst_o, in_=src_o[:, bass.ds(pv('act', e) * (Tn // 16), TT // 16), :]),
         reads=[G.oall.key], writes=[G.Lo.key], dma=True)
    Lo3 = G.Lo.h[0:8 * TT * NV].rearrange("(r t d) -> r t d", r=8, d=NV)
    Yl3 = blk3(G.Yloc, NB, Tn)
    srcs = {}
    srcs['o'] = Src(G.Lo.key, lambda e, r, n: Lo3[:, r:r + n, 0:128].rearrange("r t d -> t r d"))
    if variant == 'ml':
        srcs['den'] = Src(G.Lo.key, lambda e, r, n: Lo3[0:8:2, r:r + n, 128:129].rearrange("r t d -> t (r d)"))

    def own_blocks(j0):
        return Src(G.Yloc.key, lambda e, r, n: Yl3.h[j0:j0 + 8, r - 128:r - 128 + n, :].rearrange("j t d -> t j d"))
    halo_src = {}
    if variant in ('gla', 'ml'):
        srcs['r'] = own_blocks(16)
        halo_src['r'] = Src(G.Lt.key, lambda e, r, n: G.Lt.h[:, 0:1024])
    if variant == 'rwkv':
        srcs['bonus'] = own_blocks(56)
        srcs['g'] = own_blocks(48)
        halo_src['bonus'] = Src(G.Lt.key, lambda e, r, n: G.Lt.h[:, 2048:3072])
        halo_src['g'] = Src(G.Lt.key, lambda e, r, n: G.Lt.h[:, 1024:2048])
    rows = {}
    rown = {'sb': [], 'gla': ['onorm'], 'ml': ['onorm'], 'rwkv': ['gng', 'gnb']}[variant]
    if last:
        rown = rown + ['fnorm']
    for n in rown:
        rows[n] = P.sb([128, 1024], F32, "row_" + n)
        P.dma(rows[n][:], Wd[n][:])
    w_up = Wd['w_up']
    n2s = P.sb([128, 8], F32, "n2s")
    cws = P.sb([128, NCH * 3], F32, "cws")
    cbs = P.sb([128, NCH], F32, "cbs")
    P.dma(n2s[:], Wd['n2'][:])
    P.dma(cws[:], Wd['cw'][:])
    P.dma(cbs[:], Wd['cb'][:])
    woutb = load_w_bf16(P, C, Wd['w_out'], 1024, 1024, "woutb")
    wstg = [P.sb([128, 8, 128], F32, f"wstg{i}") for i in range(2)]
    wch = [P.sb([128, 8, 128], BF16, f"wch{i}") for i in range(2)]
    wcnt = [0]
    wdb = load_w_bf16(P, C, Wd['w_down'], NF, 1024, "wdb")
    py = [P.ps([128, 512], F32, "py0"), P.ps([128, 512], F32, "py1")]
    pu = [P.ps([128, 512], F32, "pu0"), P.ps([128, 512], F32, "pu1")]
    pd = [P.ps([128, 512], F32, "pd0"), P.ps([128, 512], F32, "pd1")]
    xt = [P.sb([128, 1024], F32, f"xt{i}") for i in range(2)]
    mt = {n: [P.sb([128, 1024], F32, f"m_{n}")] * 2 for n in names}
    dent = [P.sb([128, 4], F32, f"dent{i}") for i in range(2)] if variant == 'ml' else None
    junk = P.sb([128, 1024], F32, "junk")
    tmpf = P.sb([128, 1024], F32, "tmpf")
    ogb = P.sb([128, 1024], BF16, "ogb")
    ogT = P.sb([128, 8, 128], BF16, "ogT")
    x1 = P.sb([128, SW // 128, 1024], F32, "x1")
    x1n = P.sb([128, 1024], BF16, "x1n")
    x1nT = P.sb([128, 8, SW], BF16, "x1nT")
    hT = P.sb([128, 22, SW], BF16, "hT")
    carry = P.sb([128, NCH, 2], F32, "carry")
    ucat = [P.sb([128, SW + 2], F32, f"ucat{i}") for i in range(4)]
    cc = [P.sb([128, SW], F32, f"cc{i}") for i in range(4)]
    x2 = [P.sb([128, 1024], F32, f"x2{i}") for i in range(2)]
    P.memset(carry[:], 0.0)
    cnt = [0]

    def post(i, row0):
        b = cnt[0] % 2
        cnt[0] += 1
        P.dma(xt[b][:], cur[row0:row0 + 128, :], q='sp')
        for j, n in enumerate(names):
            dv = mt[n][b][:]
            sr = srcs[n]
            if n != 'o' and row0 == 0:
                sr = halo_src[n]
            else:
                dv = dv.re("p (r d) -> p r d", r=8)
            dma_in(P, dv, sr, row0, 128, 'pool' if j % 2 == 0 else 'sp')
        sm = C.sm[C.sm_i % 4]
        C.sm_i += 1
        if variant == 'sb':
            P.copy(ogb[:], mt['o'][b][:])
        elif variant in ('gla', 'ml'):
            o = mt['o'][b]
            if variant == 'ml':
                dma_in(P, dent[b][:], srcs['den'], row0, 128, 'sp', allow_slow_non_contiguous=True)
                P.act(sm[:, 16:20], dent[b][:], AF.Abs)
                P.ts(sm[:, 16:20], sm[:, 16:20], 1.0, ALU.max)
                P.recip(sm[:, 20:24], sm[:, 16:20])
                for h in range(4):
                    P.ts(o[:, h * 256:(h + 1) * 256], o[:, h * 256:(h + 1) * 256], sm[:, 20 + h:21 + h], ALU.mult)
            for h in range(4):
                P.act(junk[:, h * 256:(h + 1) * 256], o[:, h * 256:(h + 1) * 256], AF.Square,
                      accum=sm[:, h:h + 1])
            rstd_from_ssq(P, sm[:, 8:12], sm[:, 0:4], 256.0, sm[:, 4:8])
            for h in range(4):
                P.stt(tmpf[:, h * 256:(h + 1) * 256], o[:, h * 256:(h + 1) * 256], sm[:, 8 + h:9 + h],
                      rows['onorm'][:, h * 256:(h + 1) * 256], ALU.mult, ALU.mult)
            P.act(junk[:], mt['r'][b][:], AF.Silu if variant == 'gla' else AF.Sigmoid)
            P.tt(ogb[:], tmpf[:], junk[:], ALU.mult)
        elif variant == 'rwkv':
            y = mt['o'][b]
            y3 = y[:].re("p (h n) -> p h n", h=16)
            P.reduce(sm[:, 0:16], y3, ALU.add)
            P.tt(junk[:], y[:], y[:], ALU.mult)
            P.reduce(sm[:, 16:32], junk[:].re("p (h n) -> p h n", h=16), ALU.add)
            P.ts(sm[:, 0:16], sm[:, 0:16], 1.0 / 64, ALU.mult)
            P.tt(sm[:, 32:48], sm[:, 0:16], sm[:, 0:16], ALU.mult)
            P.stt(sm[:, 16:32], sm[:, 16:32], 1.0 / 64, sm[:, 32:48], ALU.mult, ALU.subtract)
            P.ts(sm[:, 16:32], sm[:, 16:32], 64e-5, ALU.add)
            P.act(sm[:, 16:32], sm[:, 16:32], AF.Sqrt)
            P.recip(sm[:, 32:48], sm[:, 16:32])
            t3 = tmpf[:].re("p (h n) -> p h n", h=16)
            P.tt(t3, y3, sm[:, 0:16].bc(2, [128, 16, 64]), ALU.subtract)
            P.tt(t3, t3, sm[:, 32:48].bc(2, [128, 16, 64]), ALU.mult)
            P.tt(tmpf[:], tmpf[:], rows['gng'][:], ALU.mult)
            P.tt(tmpf[:], tmpf[:], rows['gnb'][:], ALU.add)
            P.tt(tmpf[:], tmpf[:], mt['bonus'][b][:], ALU.add)
            P.tt(ogb[:], tmpf[:], mt['g'][b][:], ALU.mult)
        return xt[b]

    def token_tile(i, row0):
        xtile = post(i, row0)
        to_T(P, C, ogb[:], ogT[:])
        for hf in range(2):
            for c in range(8):
                P.mm(py[hf][:], ogT[:, c, :], woutb[:, c, hf * 512:(hf + 1) * 512], start=(c == 0), stop=(c == 7))
            P.tt(x1[:, i, hf * 512:(hf + 1) * 512], xtile[:, hf * 512:(hf + 1) * 512], py[hf][:], ALU.add)
        norm_T(P, C, x1[:, i, :], x1nT[:, :, i * 128:(i + 1) * 128], junk[:], x1n[:])

    def up_conv(W, halo):
        for j in range(22):
            cs = []
            for part, ch in enumerate((j, j + 22)):
                k = (2 * j + part) % 4
                pp = pu[part]
                wi = wcnt[0] % 2
                wcnt[0] += 1
                P.dma(wstg[wi][:], w_up[:, ch * 128:(ch + 1) * 128].re("(c p) f -> p c f", p=128),
                      q='sp' if wi == 0 else 'pool')
                P.tt(wch[wi][:], wstg[wi][:], n2s[:].bc(2, [128, 8, 128]), ALU.mult)
                for c in range(8):
                    P.mm(pp[:, 0:W], wch[wi][:, c, :], x1nT[:, c, 0:W], start=(c == 0), stop=(c == 7))
                u = ucat[k]
                P.copy(u[:, 0:2], carry[:, ch, :], eng='pool')
                P.copy(u[:, 2:2 + W], pp[:, 0:W], eng='act')
                P.copy(carry[:, ch, :], u[:, W:W + 2], eng='pool')
                if halo:
                    continue
                c_ = cc[k]
                P.ts(c_[:, 0:W], u[:, 2:2 + W], cws[:, ch * 3 + 2:ch * 3 + 3], ALU.mult, cbs[:, ch:ch + 1], ALU.add)
                P.stt(c_[:, 0:W], u[:, 1:1 + W], cws[:, ch * 3 + 1:ch * 3 + 2], c_[:, 0:W], ALU.mult, ALU.add)
                P.stt(c_[:, 0:W], u[:, 0:W], cws[:, ch * 3:ch * 3 + 1], c_[:, 0:W], ALU.mult, ALU.add)
                cs.append(c_)
            if halo:
                continue
            P.act(cs[0][:, 0:W], cs[0][:, 0:W], AF.Silu)
            P.tt(hT[:, j, 0:W], cs[0][:, 0:W], cs[1][:, 0:W], ALU.mult)

    def down(nt, tok0):
        for i in range(nt):
            xo = x2[i % 2]
            for hf in range(2):
                for j in range(22):
                    P.mm(pd[hf][:], hT[:, j, i * 128:(i + 1) * 128], wdb[:, j, hf * 512:(hf + 1) * 512],
                         start=(j == 0), stop=(j == 21))
                P.tt(xo[:, hf * 512:(hf + 1) * 512], x1[:, i, hf * 512:(hf + 1) * 512], pd[hf][:], ALU.add)
            if last:
                sm = C.sm[C.sm_i % 4]
                C.sm_i += 1
                P.act(junk[:], xo[:], AF.Square, accum=sm[:, 0:1])
                rstd_from_ssq(P, sm[:, 2:3], sm[:, 0:1], 1024.0, sm[:, 1:2])
                P.stt(xo[:], xo[:], sm[:, 2:3], rows['fnorm'][:], ALU.mult, ALU.mult)
            r_ = dst_base + tok0 + i * 128
            P.dma(dst[r_:r_ + 128, :], xo[:], q='sp')

    token_tile(0, 0)
    up_conv(128, True)
    for s in range(Tn // SW):
        for i in range(SW // 128):
            token_tile(i, 128 + s * SW + i * 128)
        up_conv(SW, False)
        down(SW // 128, s * SW)


LAYERS = ['gla', 'rwkv', 'sb', 'ml']
W_NAMES = {
    'gla': ['W', 'n1', 'wau', 'bal', 'onorm'],
    'rwkv': ['W', 'n1', 'mu', 'w2a2', 'g2', 'w0', 'a0', 'k_k', 'k_a', 'r_k', 'gng', 'gnb'],
    'sb': ['W', 'n1'],
    'ml': ['W', 'n1', 'bif', 'oh', 'onorm', 'fnorm'],
}
W_SHAPES = {
    'n1': [128, 8], 'wau': [16, 128], 'bal': [1, 128], 'onorm': [128, 1024], 'mu': [128, 48],
    'w2a2': [128, 1024], 'g2': [128, 1024], 'w0': [128, 1024], 'a0': [128, 1024], 'k_k': [128, 1024],
    'k_a': [128, 1024], 'r_k': [128, 1024], 'gng': [128, 1024], 'gnb': [128, 1024], 'bif': [128, 2],
    'fnorm': [128, 1024], 'oh': [128, 16], 'w_out': [1024, 1024], 'n2': [128, 8], 'w_up': [1024, 5632], 'cw': [128, NCH * 3],
    'cb': [128, NCH], 'w_down': [NF, 1024],
}
W_N = {'gla': 3088, 'rwkv': RW_N, 'sb': 3072, 'ml': 3080}


def build_fused(S, layers=(0, 1, 2, 3)):
    _PV.clear()
    Tn = S // 8
    TT = Tn + 128
    SP = S + 128
    nc = bass.Bass("TRN2", target_bir_lowering=False)
    with ExitStack() as es:
        P = Prog(nc, es)
        G = Ctx()
        xin = P.dram("x", [Tn, 1024], F32, "ExternalInput")
        for nm, shp in (('ident', [128, 128]), ('tri', [128, 128]), ('su', [128, 128]), ('mstr', [128, 128]),
                        ('umat', [128, 128]), ('sel', [128, 64 * 128])):
            setattr(G, nm, P.dram(nm, shp, F32, "ExternalInput"))
        Wd = {}
        for li in layers:
            kind = LAYERS[li]
            d = {}
            for nm in W_NAMES[kind] + ['w_out', 'n2', 'w_up', 'cw', 'cb', 'w_down']:
                shp = [1024, W_N[kind]] if nm == 'W' else W_SHAPES[nm]
                d[nm] = P.dram(f"l{li}_{nm}", shp, F32, "ExternalInput")
            Wd[li] = d
        out = P.dram("out", [Tn, 1024], F32, "ExternalOutput")
        xb = [P.dram("xbufA", [TT, 1024], F32, "Internal"), P.dram("xbufB", [TT, 1024], F32, "Internal")]
        G.xt_loc = P.dram("xt_loc", [128, 1024], F32, "Internal")
        G.G_xt = P.dram("G_xt", [128 * 9, 1024], F32, "Internal")
        G.Yloc = P.dram("Yloc", [Tn * 8192], F32, "Internal")
        G.GY = P.dram("GY", [8 * 25 * Tn * 128], F32, "Internal")
        G.GY2 = P.dram("GY2", [8 * 24 * Tn * 128], F32, "Internal")
        G.T_loc = P.dram("T_loc", [128, 3072], F32, "Internal")
        G.G_T = P.dram("G_T", [128 * 9, 3072], F32, "Internal")
        G.Lt = P.dram("Lt", [128, 3072], F32, "Internal")
        G.slab = [P.dram(f"slab{k}", [S * 128], F32, "Internal") for k in range(6)]
        G.Lo = P.dram("Lo", [8 * TT * 129], F32, "Internal")
        G.oloc = P.dram("oloc", [SP * 129], F32, "Internal")
        G.oall = P.dram("oall", [8 * SP * 129], F32, "Internal")

        P.phase_begin("init")
        zt = P.sb([128, 1024], F32, "zt")
        P.memset(zt[:], 0.0)
        P.dma(G.G_xt[0:128, :], zt[:])
        for a in range(3):
            P.dma(G.G_T[0:128, a * 1024:(a + 1) * 1024], zt[:])
            P.dma(G.T_loc[:, a * 1024:(a + 1) * 1024], zt[:])
        P.dma(xb[0][0:128, :], zt[:])
        P.dma(xb[1][0:128, :], zt[:])
        for a in range(0, Tn, 256):
            P.dma(xb[0][128 + a:128 + a + 256, :], xin[a:a + 256, :], q='pool')
        P.phase_end()

        for n_, li in enumerate(layers):
            kind = LAYERS[li]
            lastl = (n_ == len(layers) - 1)
            cur = xb[n_ % 2]
            nxt = xb[(n_ + 1) % 2]
            P.phase_begin(f"f{li}")
            if kind == 'rwkv':
                NB = ph_front_rwkv(P, G, cur, Wd[li], Tn, S)
            else:
                NB = ph_front(P, G, cur, Wd[li]['W'], Wd[li]['n1'], W_N[kind], Tn, S,
                              2048 if kind in ('gla', 'ml') else None)
            P.phase_end()
            P.phase_begin(f"c{li}")
            if kind == 'gla':
                ph_lin_core(P, G, NB, Tn, Wd[li], S, 128, 'gla')
            elif kind == 'ml':
                ph_lin_core(P, G, NB, Tn, Wd[li], S, 129, 'ml')
            elif kind == 'rwkv':
                ph_rwkv_core(P, G, NB, Tn, S)
            else:
                ph_sb_core(P, G, NB, Tn, S)
            P.phase_end()
            P.phase_begin(f"b{li}")
            if lastl:
                ph_back(P, G, NB, cur, out, 0, Wd[li], kind, kind == 'ml', Tn, S)
                P.phase_end(final_keys=[out.key])
            else:
                ph_back(P, G, NB, cur, nxt, 128, Wd[li], kind, False, Tn, S)
                P.phase_end()
    return nc


def _col(g):
    return np.ascontiguousarray(g.reshape(8, 128).T)


def _rep(r, n=128):
    return np.ascontiguousarray(np.broadcast_to(r, (n, r.shape[-1])))


def _make_sel():
    sel = np.zeros((128, 64, 128), np.float32)
    for h in range(2):
        for j in range(64):
            sel[h * 64 + j, j, h * 64:(h + 1) * 64] = 1.0
    return sel.reshape(128, 64 * 128)


def fused_in_maps(p, S, layers=(0, 1, 2, 3)):
    Tn = S // 8
    x = np.ascontiguousarray(p['x'].reshape(S, 1024))
    common = dict(ident=np.eye(128, dtype=np.float32),
                  tri=np.triu(np.ones((128, 128), np.float32)),
                  su=np.tril(np.ones((128, 128), np.float32), -1),
                  mstr=np.triu(np.ones((128, 128), np.float32), 1),
                  umat=np.tril(np.ones((128, 128), np.float32)),
                  sel=_make_sel())

    def ffn(li, d):
        cwv = p[f"l{li}_ffn_conv_w"]
        d[f"l{li}_n2"] = _col(p[f"l{li}_norm2"])
        d[f"l{li}_w_up"] = np.ascontiguousarray(p[f"l{li}_ffn_w_up"])
        d[f"l{li}_cw"] = np.ascontiguousarray(cwv.reshape(3, NCH, 128).transpose(2, 1, 0).reshape(128, NCH * 3))
        d[f"l{li}_cb"] = np.ascontiguousarray(p[f"l{li}_ffn_conv_b"].reshape(NCH, 128).T)
        d[f"l{li}_w_down"] = np.ascontiguousarray(p[f"l{li}_ffn_w_down"])

    if 0 in layers:
        common['l0_W'] = np.ascontiguousarray(p['l0_gla_w_in'])
        common['l0_n1'] = _col(p['l0_norm1'])
        common['l0_onorm'] = _rep(p['l0_gla_out_norm'])
        common['l0_w_out'] = np.ascontiguousarray(p['l0_gla_w_out'])
        ffn(0, common)
    if 1 in layers:
        common['l1_W'] = np.ascontiguousarray(np.concatenate(
            [p['l1_rwkv_w_rkv'][0], p['l1_rwkv_w_rkv'][1], p['l1_rwkv_w_rkv'][2],
             p['l1_rwkv_w1'], p['l1_rwkv_a1'], p['l1_rwkv_g1']], axis=1))
        common['l1_n1'] = _col(p['l1_norm1'])
        common['l1_mu'] = np.ascontiguousarray(p['l1_rwkv_mu'].reshape(6, 8, 128).transpose(2, 0, 1).reshape(128, 48))
        common['l1_w2a2'] = np.ascontiguousarray(np.concatenate([p['l1_rwkv_w2'], p['l1_rwkv_a2']], axis=0))
        common['l1_g2'] = np.ascontiguousarray(p['l1_rwkv_g2'])
        for nm in ('w0', 'a0', 'k_k', 'k_a', 'r_k'):
            common['l1_' + nm] = _rep(p['l1_rwkv_' + nm])
        common['l1_gng'] = _rep(p['l1_rwkv_gn_g'])
        common['l1_gnb'] = _rep(p['l1_rwkv_gn_b'])
        common['l1_w_out'] = np.ascontiguousarray(p['l1_rwkv_w_out'])
        ffn(1, common)
    if 2 in layers:
        common['l2_W'] = np.ascontiguousarray(p['l2_sb_w_qkv'])
        common['l2_n1'] = _col(p['l2_norm1'])
        common['l2_w_out'] = np.ascontiguousarray(p['l2_sb_w_out'])
        ffn(2, common)
    if 3 in layers:
        common['l3_W'] = np.ascontiguousarray(p['l3_ml_w_in'])
        common['l3_n1'] = _col(p['l3_norm1'])
        common['l3_onorm'] = _rep(p['l3_ml_out_norm'])
        common['l3_fnorm'] = _rep(p['final_norm'])
        common['l3_w_out'] = np.ascontiguousarray(p['l3_ml_w_out'])
        ffn(3, common)
    maps = []
    for c in range(8):
        m = dict(common)
        m['x'] = np.ascontiguousarray(x[c * Tn:(c + 1) * Tn])
        h = c // 2
        if 0 in layers:
            m['l0_wau'] = np.ascontiguousarray(p['l0_gla_w_alpha_up'][:, h * 128:(h + 1) * 128])
            m['l0_bal'] = np.ascontiguousarray(p['l0_gla_b_alpha'][None, h * 128:(h + 1) * 128])
        if 3 in layers:
            m['l3_bif'] = _rep(np.array([p['l3_ml_b_if'][h], p['l3_ml_b_if'][4 + h]], np.float32))
            oh = np.zeros((16,), np.float32)
            oh[h] = 1.0
            oh[8 + 4 + h] = 1.0
            m['l3_oh'] = _rep(oh)
        maps.append(m)
    return maps


_PROG = {}


def kernel(**inp):
    p = {k_: np.asarray(v_, dtype=np.float32) for k_, v_ in inp.items()}
    S = 16384
    if 'nc' not in _PROG:
        _PROG['nc'] = build_fused(S)
    maps = fused_in_maps(p, S)
    res = run_bass_kernel_spmd(_PROG['nc'], maps, core_ids=list(range(8)))
    out = np.concatenate([r["out"] for r in res.results], axis=0)
    return out.reshape(1, S, 1024).astype(np.float32)
```
